# Optimizing a Trainium2 kernel written in Bass

```python
import math
import jax, jax.numpy as jnp
from jax import lax
import numpy as np


D_MODEL = 1024
BATCH = 2
SEQ = 8192
DEPTH = 2

N_META = 16
CHUNK = 128
PAD = CHUNK
ROPE_THETA = 10000.0
EPS = 1e-6
LB_FLOOR = 1e-30
RET_HEADS = 4
RET_DK = 128
RET_DV = 256
HG_HEADS = 8
HG_DK = 128
HG_DV = 128
DA_HEADS = 8
DA_DH = 64
DA_DV = 2 * DA_DH
N_BRANCH = 3
BRANCH_WIDTH = 1024
D_FF = 2816
CONV_W = 3
Q_BLOCK = 128
MASK_VALUE = -1e30
RET_QK_W = RET_HEADS * RET_DK
RET_V_W = RET_HEADS * RET_DV
HG_K_W = HG_HEADS * HG_DK
HG_V_W = HG_HEADS * HG_DV
DA_QK_W = DA_HEADS * 2 * DA_DH
DA_V_W = DA_HEADS * DA_DV
IN_SPLITS = (RET_QK_W, RET_QK_W, RET_V_W, RET_V_W, HG_K_W, HG_K_W, HG_V_W, HG_V_W, DA_QK_W, DA_QK_W, DA_V_W, N_BRANCH * D_MODEL)
IN_WIDTH = sum(IN_SPLITS)
F32 = jnp.float32

kernel_name = 'hybrid_retention_hgrn2_diffattn_block'


def _rms(x):
    xf = x.astype(F32)
    return xf * lax.rsqrt(jnp.mean(xf * xf, axis=-1, keepdims=True) + EPS)


def rms_norm(x, g):
    return (_rms(x) * g.astype(F32)).astype(x.dtype)


def rope(x, pos):
    d = x.shape[-1]
    inv = ROPE_THETA ** (-jnp.arange(0, d, 2, dtype=F32) / d)
    ang = pos.astype(F32)[:, None] * inv[None, :]
    cos = jnp.cos(ang)[None, :, None, :]
    sin = jnp.sin(ang)[None, :, None, :]
    x1, x2 = jnp.split(x, 2, axis=-1)
    return jnp.concatenate([x1 * cos - x2 * sin, x2 * cos + x1 * sin], axis=-1)


def to_chunks(x):
    b, l = x.shape[:2]
    return jnp.swapaxes(x.reshape((b, l // CHUNK, CHUNK) + x.shape[2:]), 0, 1)


def from_chunks(x):
    n, b = x.shape[:2]
    return jnp.swapaxes(x, 0, 1).reshape((b, n * CHUNK) + x.shape[3:])


def retention(q, k, v, gate, valid, pos):
    b = q.shape[0]
    q = rope(q, pos)
    k = jnp.where(valid[None, :, None, None], rope(k, pos) * RET_DK ** -0.5, 0.0)
    log_g = jnp.log1p(-jnp.exp2(-5.0 - jnp.arange(RET_HEADS, dtype=F32)))
    idx = jnp.arange(CHUNK, dtype=F32)
    gap = idx[:, None] - idx[None, :]
    intra = jnp.where(gap >= 0, jnp.exp(log_g[:, None, None] * jnp.maximum(gap, 0.0)), 0.0)
    q_dec = jnp.exp(log_g[:, None] * (idx[None, :] + 1.0))
    k_dec = jnp.exp(log_g[:, None] * (CHUNK - 1.0 - idx[None, :]))
    c_dec = jnp.exp(log_g * CHUNK)[None, :, None, None]

    def step(state, xs):
        qc, kc, vc = xs
        s = jnp.einsum('bqhd,bkhd->bhqk', qc, kc) * intra[None]
        o = jnp.einsum('bhqk,bkhe->bqhe', s, vc) + jnp.einsum('bqhd,hq,bhde->bqhe', qc, q_dec, state)
        state = c_dec * state + jnp.einsum('bkhd,hk,bkhe->bhde', kc, k_dec, vc)
        return state, o

    s0 = jnp.zeros((b, RET_HEADS, RET_DK, RET_DV), F32)
    _, o = lax.scan(step, s0, (to_chunks(q), to_chunks(k), to_chunks(v)))
    o = _rms(from_chunks(o)) * jax.nn.silu(gate)
    return o.reshape(b, -1, RET_V_W)


def hgrn2(q, f_logit, inp, gate, lb, valid):
    b = q.shape[0]
    lbh = lb.reshape(HG_HEADS, HG_DK)
    log_f = jnp.logaddexp(jnp.log(jnp.maximum(lbh, LB_FLOOR)), jnp.log1p(-lbh) + jax.nn.log_sigmoid(f_logit))
    k = (1.0 - lbh) * jax.nn.sigmoid(-f_logit)
    v = jnp.where(valid[None, :, None, None], inp, 0.0)
    causal = jnp.tril(jnp.ones((CHUNK, CHUNK), bool))[None, :, :, None, None]

    def step(state, xs):
        qc, kc, vc, lfc = xs
        cb = jnp.cumsum(lfc, axis=1)
        rel = cb[:, :, None] - cb[:, None, :]
        dec = jnp.where(causal, jnp.exp(jnp.where(causal, rel, 0.0)), 0.0)
        a = jnp.einsum('bqhd,bkhd,bqkhd->bhqk', qc, kc, dec)
        o = jnp.einsum('bhqk,bkhe->bqhe', a, vc) + jnp.einsum('bqhd,bhde->bqhe', qc * jnp.exp(cb), state)
        c_end = cb[:, -1]
        state = jnp.exp(c_end)[..., None] * state + jnp.einsum('bkhd,bkhe->bhde', kc * jnp.exp(c_end[:, None] - cb), vc)
        return state, o

    s0 = jnp.zeros((b, HG_HEADS, HG_DK, HG_DV), F32)
    _, o = lax.scan(step, s0, (to_chunks(q), to_chunks(k), to_chunks(v), to_chunks(log_f)))
    o = _rms(from_chunks(o)) * jax.nn.silu(gate)
    return o.reshape(b, -1, HG_V_W)


def diff_attention(q, k, v, lam_p, subln_g, lambda_init, valid, pos):
    b, l = q.shape[:2]
    lp = lam_p.astype(F32)
    lam = jnp.exp(jnp.sum(lp[0] * lp[1])) - jnp.exp(jnp.sum(lp[2] * lp[3])) + lambda_init
    q = rope(q.reshape(b, l, 2 * DA_HEADS, DA_DH), pos).reshape(b, l, DA_HEADS, 2, DA_DH) * DA_DH ** -0.5
    k = rope(k.reshape(b, l, 2 * DA_HEADS, DA_DH), pos).reshape(b, l, DA_HEADS, 2, DA_DH)
    n_blk = l // Q_BLOCK
    q_blocks = jnp.swapaxes(q.reshape(b, n_blk, Q_BLOCK, DA_HEADS, 2, DA_DH), 0, 1)
    key_pos = jnp.arange(l)

    def block(args):
        qb, bi = args
        s = jnp.einsum('bqhmd,bkhmd->bhmqk', qb, k)
        q_pos = bi * Q_BLOCK + jnp.arange(Q_BLOCK)
        allowed = (key_pos[None, :] <= q_pos[:, None]) & valid[None, :]
        p = jax.nn.softmax(jnp.where(allowed, s, MASK_VALUE), axis=-1)
        w = p[:, :, 0] - lam * p[:, :, 1]
        return jnp.einsum('bhqk,bkhe->bqhe', w, v)

    o = lax.map(block, (q_blocks, jnp.arange(n_blk)))
    o = jnp.swapaxes(o, 0, 1).reshape(b, l, DA_HEADS, DA_DV)
    o = _rms(o) * subln_g.astype(F32) * (1.0 - lambda_init)
    return o.reshape(b, l, DA_V_W)


def mixer(hn, w_in, w_branch, w_out, lb, lam_p, subln_g, lambda_init, valid, pos):
    b, l, _ = hn.shape
    proj = (hn @ w_in).astype(F32)
    offs = [int(o) for o in np.cumsum(IN_SPLITS)[:-1]]
    rq, rk, rv, rg, hq, hf, hi, hg, dq, dk, dv, mg = jnp.split(proj, offs, axis=-1)
    o_ret = retention(rq.reshape(b, l, RET_HEADS, RET_DK), rk.reshape(b, l, RET_HEADS, RET_DK),
                      rv.reshape(b, l, RET_HEADS, RET_DV), rg.reshape(b, l, RET_HEADS, RET_DV), valid, pos)
    o_hg = hgrn2(hq.reshape(b, l, HG_HEADS, HG_DK), hf.reshape(b, l, HG_HEADS, HG_DK),
                 hi.reshape(b, l, HG_HEADS, HG_DV), hg.reshape(b, l, HG_HEADS, HG_DV), lb, valid)
    o_da = diff_attention(dq.reshape(b, l, DA_HEADS, 2, DA_DH), dk.reshape(b, l, DA_HEADS, 2, DA_DH),
                          dv.reshape(b, l, DA_HEADS, DA_DV), lam_p, subln_g, lambda_init, valid, pos)
    branches = jnp.stack([o_ret, o_hg, o_da], axis=2)
    y_b = jnp.einsum('blnc,ncd->blnd', branches, w_branch.astype(F32))
    gates = jax.nn.sigmoid(mg.reshape(b, l, N_BRANCH, D_MODEL))
    y = jnp.sum(gates * y_b, axis=2)
    return y.astype(hn.dtype) @ w_out


def conv_ffn(hn, w_ffn_in, conv_w, conv_b, w_ffn_out, valid):
    l = hn.shape[1]
    u = jnp.where(valid[None, :, None], hn @ w_ffn_in, 0.0)
    up = jnp.pad(u, ((0, 0), (CONV_W - 1, 0), (0, 0)))
    c = conv_b + sum(conv_w[j] * up[:, j:j + l] for j in range(CONV_W))
    gate, val = jnp.split(c, 2, axis=-1)
    return (jax.nn.silu(gate) * val) @ w_ffn_out


def setup_inputs(seed: int = 0) -> dict:
    key = jax.random.key(seed)
    ks = jax.random.split(key, 16)

    def nrm(k, shape, scale):
        return jax.random.normal(k, shape, F32) * scale

    return {
        'x': nrm(ks[0], (BATCH, SEQ, D_MODEL), 1.0),
        'meta': nrm(ks[1], (N_META, D_MODEL), 1.0),
        'norm_mix_g': 1.0 + nrm(ks[2], (DEPTH, D_MODEL), 0.02),
        'w_in': nrm(ks[3], (DEPTH, D_MODEL, IN_WIDTH), D_MODEL ** -0.5),
        'w_branch': nrm(ks[4], (DEPTH, N_BRANCH, BRANCH_WIDTH, D_MODEL), BRANCH_WIDTH ** -0.5),
        'w_out': nrm(ks[5], (DEPTH, D_MODEL, D_MODEL), D_MODEL ** -0.5),
        'hg_lb': nrm(ks[6], (DEPTH, HG_K_W), 0.1),
        'da_lambda': nrm(ks[7], (DEPTH, 4, DA_DH), 0.1),
        'da_subln_g': 1.0 + nrm(ks[8], (DEPTH, DA_DV), 0.02),
        'norm_ffn_g': 1.0 + nrm(ks[9], (DEPTH, D_MODEL), 0.02),
        'w_ffn_in': nrm(ks[10], (DEPTH, D_MODEL, 2 * D_FF), D_MODEL ** -0.5),
        'ffn_conv_w': nrm(ks[11], (DEPTH, CONV_W, 2 * D_FF), CONV_W ** -0.5),
        'ffn_conv_b': nrm(ks[12], (DEPTH, 2 * D_FF), 0.01),
        'w_ffn_out': nrm(ks[13], (DEPTH, D_FF, D_MODEL), D_FF ** -0.5),
        'norm_final_g': 1.0 + nrm(ks[14], (D_MODEL,), 0.02),
    }


def reference(x, meta, norm_mix_g, w_in, w_branch, w_out, hg_lb, da_lambda, da_subln_g,
              norm_ffn_g, w_ffn_in, ffn_conv_w, ffn_conv_b, w_ffn_out, norm_final_g):
    b, s, d = x.shape
    l = PAD + s
    h = jnp.concatenate([jnp.zeros((b, PAD - N_META, d), x.dtype),
                         jnp.broadcast_to(meta[None].astype(x.dtype), (b, N_META, d)), x], axis=1)
    t = jnp.arange(l)
    valid = t >= PAD - N_META
    pos = t - (PAD - N_META)
    lb_soft = jax.nn.softmax(hg_lb.astype(F32), axis=0)
    lbs = jnp.cumsum(lb_soft, axis=0) - lb_soft[0]
    for li in range(DEPTH):
        lambda_init = 0.8 - 0.6 * math.exp(-0.3 * li)
        h = h + mixer(rms_norm(h, norm_mix_g[li]), w_in[li], w_branch[li], w_out[li], lbs[li],
                      da_lambda[li], da_subln_g[li], lambda_init, valid, pos)
        h = h + conv_ffn(rms_norm(h, norm_ffn_g[li]), w_ffn_in[li], ffn_conv_w[li], ffn_conv_b[li],
                         w_ffn_out[li], valid)
    h = rms_norm(h, norm_final_g)
    return h[:, PAD:]
```

```python
from contextlib import ExitStack
import math
import numpy as np
import ml_dtypes
import concourse.bass as bass
import concourse.mybir as mybir
from concourse.bass_utils import run_bass_kernel_spmd

F32 = mybir.dt.float32
BF16 = mybir.dt.bfloat16
ALU = mybir.AluOpType
AF = mybir.ActivationFunctionType

D = 1024
SEQ = 8192
NCH = 65
LTOT = NCH * 128
NLOC = 16
HALO = 4
TLOC = 128 + HALO + NLOC * 128
DFF = 2816
EPS = 1e-6
DEPTH = 2
NBW = 2560


class Res:
    __slots__ = ("name", "lw", "rd", "excl")

    def __init__(self, name="r", excl=False):
        self.name = name
        self.lw = None
        self.rd = {}
        self.excl = excl


import os
NO_POOL = os.environ.get("NO_POOL", "1") == "1"


class Prog:
    ENG = ("pe", "act", "dve", "pool", "sp")

    _uid = [0]

    def __init__(self, nc):
        Prog._uid[0] += 1
        self.pfx = "P%d_" % Prog._uid[0]
        self.nc = nc
        self.stack = ExitStack()
        self.ops = {e: [] for e in self.ENG}
        self.cnt = {e: 0 for e in self.ENG}
        self.seen = {e: {} for e in self.ENG}
        self.dma_cnt = {}
        self.n = 0

    def sb(self, shape, dt, name=None):
        self.n += 1
        name = self.pfx + (name or ("sb%d" % self.n))
        t = self.stack.enter_context(self.nc.sbuf_tensor(name, list(shape), dt))
        return t, Res(name)

    def ps(self, shape, dt, name=None):
        self.n += 1
        name = self.pfx + (name or ("ps%d" % self.n))
        t = self.stack.enter_context(self.nc.psum_tensor(name, list(shape), dt))
        return t, Res(name, excl=True)

    def op(self, eng, fn, reads=(), writes=(), dma_key=None):
        if eng == "pool" and dma_key is None and NO_POOL:
            eng = "dve"
        if eng == "gps":
            eng = "pool"
        deps = []
        for r in reads:
            if r.lw is not None:
                deps.append(r.lw)
            if r.excl:
                deps.extend((k, v) for k, v in r.rd.items() if k != eng)
        for w in writes:
            if w.lw is not None:
                deps.append(w.lw)
            deps.extend(w.rd.items())
        if dma_key is None:
            self.cnt[eng] += 1
            tok = (eng, self.cnt[eng])
        else:
            c = self.dma_cnt.get(dma_key, 0) + 16
            self.dma_cnt[dma_key] = c
            tok = (dma_key, c)
        waits = {}
        seen = self.seen[eng]
        for (k, v) in deps:
            if eng == "pe" and k == "pe":
                continue
            if seen.get(k, 0) >= v:
                continue
            if waits.get(k, 0) < v:
                waits[k] = v
        for k, v in waits.items():
            seen[k] = v
        self.ops[eng].append((fn, waits, tok, dma_key is not None))
        for r in reads:
            if r.rd.get(tok[0], 0) < tok[1]:
                r.rd[tok[0]] = tok[1]
        for w in writes:
            w.lw = tok
            w.rd = {}
        return tok

    def wait_only(self, eng, toks):
        waits = {}
        for (k, v) in toks:
            if self.seen[eng].get(k, 0) >= v:
                continue
            waits[k] = max(waits.get(k, 0), v)
        for k, v in waits.items():
            self.seen[eng][k] = v
        self.ops[eng].append((None, waits, None, False))

    def val(self, eng, name, fn, reads, store):
        waits = {}
        for r in reads:
            if r.lw is not None and self.seen[eng].get(r.lw[0], 0) < r.lw[1]:
                waits[r.lw[0]] = r.lw[1]
        for k, v in waits.items():
            self.seen[eng][k] = v

        def run(e):
            store[name] = fn(e)
            return None
        self.ops[eng].append((run, waits, None, False))

    def dma_dyn(self, eng, apfn, reads, writes, key, **kw):
        def run(e):
            o_, i_ = apfn()
            return e.dma_start(out=o_, in_=i_, **kw)
        return self.op(eng, run, reads, writes, dma_key=key)

    def finish(self):
        self.wait_only("sp", list(self.dma_cnt.items()))

    def dma(self, eng, out, in_, reads, writes, key, **kw):
        return self.op(eng, lambda e: e.dma_start(out=out, in_=in_, **kw), reads, writes, dma_key=key)

    def mm(self, out, lhsT, rhs, start, stop, reads, writes):
        return self.op("pe", lambda e: e.matmul(out, lhsT=lhsT, rhs=rhs, start=start, stop=stop), reads, writes)

    def tr(self, out, in_, ident, reads, writes):
        return self.op("pe", lambda e: e.transpose(out, in_, ident), reads, writes)

    def act(self, out, in_, func, reads, writes, **kw):
        return self.op("act", lambda e: e.activation(out=out, in_=in_, func=func, **kw), reads, writes)

    def tt(self, eng, out, in0, in1, op, reads, writes):
        return self.op(eng, lambda e: e.tensor_tensor(out=out, in0=in0, in1=in1, op=op), reads, writes)

    def ts(self, eng, out, in0, s1, s2, op0, op1, reads, writes):
        return self.op(eng, lambda e: e.tensor_scalar(out=out, in0=in0, scalar1=s1, scalar2=s2, op0=op0, op1=op1), reads, writes)

    def stt(self, eng, out, in0, scalar, in1, op0, op1, reads, writes):
        eng = "dve"
        return self.op(eng, lambda e: e.scalar_tensor_tensor(out=out, in0=in0, scalar=scalar, in1=in1, op0=op0, op1=op1), reads, writes)

    def cp(self, eng, out, in_, reads, writes):
        if eng == "act":
            return self.op("act", lambda e: e.copy(out=out, in_=in_), reads, writes)
        return self.op(eng, lambda e: e.tensor_copy(out=out, in_=in_), reads, writes)

    def memset(self, eng, ap, val, writes):
        return self.op(eng, lambda e: e.memset(ap, val), [], writes)

    def recip(self, out, in_, reads, writes):
        return self.op("dve", lambda e: e.reciprocal(out=out, in_=in_), reads, writes)

    def emit(self):
        nc = self.nc
        sems = {}
        for e in ("pe", "act", "dve", "pool"):
            sems[e] = nc.alloc_semaphore(name=self.pfx + "s_" + e)
        for k in self.dma_cnt:
            sems[k] = nc.alloc_semaphore(name=self.pfx + "d_" + str(k))
        ops = self.ops

        def mk(name):
            def body(eng):
                for fn, waits, tok, is_dma in ops[name]:
                    for k, v in waits.items():
                        eng.wait_ge(sems[k], v)
                    if fn is None:
                        continue
                    ins = fn(eng)
                    if ins is not None and tok is not None:
                        ins.then_inc(sems[tok[0]], 16 if is_dma else 1)
            return body

        with nc.Block() as block:
            block.tensor(mk("pe"))
            block.scalar(mk("act"))
            block.vector(mk("dve"))
            block.gpsimd(mk("pool"))
            block.sync(mk("sp"))
        self.stack.close()
        nc.all_engine_barrier()
        nc.clear_and_free_semaphores(list(sems.values()))
        nc.all_engine_barrier()


def rawap(t, extra):
    return bass.AP(t.tensor, t.offset, [list(t.ap[0])] + [list(x) for x in extra])


def phase_B(p, li, io, nchunks=NCH):
    nc = p.nc
    lam_init = 0.8 - 0.6 * math.exp(-0.3 * li)
    R = Res
    ident, r_id = p.sb([128, 128], BF16, "identB")
    p.dma("pool", ident[:], io["ident"], [], [r_id], "c_id")
    W, r_W = p.sb([128, 8, NBW], BF16, "Wg")
    wv = io["w"].rearrange("(k p) n -> p k n", p=128)
    for k in range(8):
        p.dma("pool", W[:, k, :], wv[:, k, :], [], [r_W], "c_w")
    tabs, r_tabs = p.sb([128, 6, 128], F32, "tabsB")
    p.dma("sp", tabs[:], io["tabs"][0:6].rearrange("k p n -> p k n"), [], [r_tabs], "c_tabs")
    cols, r_cols = p.sb([128, 8], F32, "colsB")
    p.dma("sp", cols[:], io["cols"], [], [r_cols], "c_cols")
    vcol, r_vcol = p.sb([128, NCH], F32, "vcolB")
    p.dma("sp", vcol[:], io["vcol"], [], [r_vcol], "c_vcol")
    lbr, r_lbr = p.sb([128, DEPTH, 256], F32, "lbr")
    for d_ in range(DEPTH):
        p.dma("sp", lbr[:, d_, :], io["hg_lb"][d_:d_ + 1, :].partition_broadcast(128), [], [r_lbr], "c_lb")
    p.act(lbr[:], lbr[:], AF.Exp, [r_lbr], [r_lbr])
    lsum, r_lsum = p.sb([128, 256], F32, "lsum")
    p.tt("dve", lsum[:], lbr[:, 0, :], lbr[:, 1, :], ALU.add, [r_lbr], [r_lsum])
    p.recip(lsum[:], lsum[:], [r_lsum], [r_lsum])
    lb, r_lb = p.sb([128, 256], F32, "lb")
    oml, r_oml = p.sb([128, 256], F32, "oml")
    p.memset("dve", lb[:], 0.0, [r_lb])
    for d_ in range(li + 1):
        p.stt("dve", lb[:], lbr[:, d_, :], 1.0, lb[:], ALU.mult, ALU.add, [r_lbr, r_lb], [r_lb])
    p.stt("dve", lb[:], lbr[:, 0, :], -1.0, lb[:], ALU.mult, ALU.add, [r_lbr, r_lb], [r_lb])
    p.tt("dve", lb[:], lb[:], lsum[:], ALU.mult, [r_lb, r_lsum], [r_lb])
    p.ts("dve", oml[:], lb[:], -1.0, 1.0, ALU.mult, ALU.add, [r_lb], [r_oml])
    lp, r_lp = p.sb([128, 4, 64], F32, "lp")
    p.dma("sp", lp[:].rearrange("p a d -> p (a d)"), io["da_lambda"].partition_broadcast(128), [], [r_lp], "c_lp")
    lpp, r_lpp = p.sb([128, 2, 64], F32, "lpp")
    p.tt("dve", lpp[:, 0, :], lp[:, 0, :], lp[:, 1, :], ALU.mult, [r_lp], [r_lpp])
    p.tt("dve", lpp[:, 1, :], lp[:, 2, :], lp[:, 3, :], ALU.mult, [r_lp], [r_lpp])
    lsm, r_lsm = p.sb([128, 2], F32, "lsm")
    p.op("dve", lambda e: e.reduce_sum(out=lsm[:], in_=lpp[:], axis=mybir.AxisListType.X), [r_lpp], [r_lsm])
    p.act(lsm[:], lsm[:], AF.Exp, [r_lsm], [r_lsm])
    nlam, r_nlam = p.sb([128, 1], F32, "nlam")
    p.tt("dve", nlam[:], lsm[:, 1:2], lsm[:, 0:1], ALU.subtract, [r_lsm], [r_nlam])
    p.ts("dve", nlam[:], nlam[:], -lam_init, None, ALU.add, ALU.bypass, [r_nlam], [r_nlam])
    gsub, r_gsub = p.sb([128, 128], F32, "gsub")
    p.dma("sp", gsub[:], io["subln"].partition_broadcast(128), [], [r_gsub], "c_gs")
    p.ts("dve", gsub[:], gsub[:], 1.0 - lam_init, None, ALU.mult, ALU.bypass, [r_gsub], [r_gsub])

    KT = []
    VA = []
    KT_r = [[Res("KT%d_%d" % (h, c)) for c in range(NCH)] for h in range(2)]
    VA_r = [[Res("VA%d_%d" % (h, c)) for c in range(NCH)] for h in range(2)]
    for h in range(2):
        KT.append(p.sb([128, LTOT], BF16, "KT%d" % h))
        VA.append(p.sb([128, NCH, 130], BF16, "VA%d" % h))
        p.memset("dve", VA[h][0][:], 0.0, VA_r[h])
    S_ret, r_Sret = p.sb([128, 256], F32, "S_ret")
    Sb_ret, r_Sbret = p.sb([128, 256], BF16, "Sb_ret")
    p.memset("pool", S_ret[:], 0.0, [r_Sret])
    p.memset("pool", Sb_ret[:], 0.0, [r_Sbret])
    Sb_retB, r_SbretB = p.sb([128, 256], BF16, "Sb_retB")
    p.memset("pool", Sb_retB[:], 0.0, [r_SbretB])
    Sb_ret2 = [(Sb_ret, r_Sbret), (Sb_retB, r_SbretB)]
    S_hg = []
    Sb_hg = []
    for h in range(2):
        s_, r_ = p.sb([128, 128], F32, "S_hg%d" % h)
        p.memset("pool", s_[:], 0.0, [r_])
        S_hg.append((s_, r_))
        lst = []
        for par in range(2):
            ring = []
            for j in range(4):
                sb_, rb_ = p.sb([128, 128], BF16, "Sb_hg%d_%d_%d" % (h, par, j))
                p.memset("pool", sb_[:], 0.0, [rb_])
                ring.append((sb_, rb_))
            lst.append(ring)
        Sb_hg.append(lst)
    QZ = []
    for h in range(2):
        q_, r_ = p.sb([128, 4, 128], BF16, "QZ%d" % h)
        p.memset("pool", q_[:], 0.0, [r_])
        QZ.append((q_, r_))

    hnb = [p.sb([128, 8, 512], BF16, "hnb%d" % i) for i in range(2)]
    rtb = [p.sb([128, 4, 288], F32, "rtb%d" % i) for i in range(2)]
    GP = [p.ps([128, 512], F32, "GP%d" % i) for i in range(4)]
    gp_i = [0]

    def gp():
        g = GP[gp_i[0] % 4]
        gp_i[0] += 1
        return g
    TRb, r_TRb = p.ps([128, 1024], BF16, "TRb")
    OArh, r_OArh = p.ps([128, 512], F32, "OArh")
    SUr, r_SUr = p.ps([128, 512], F32, "SUr")
    OAda, r_OAda = p.ps([128, 512], F32, "OAda")
    OAm = [(OAda, r_OAda), (SUr, r_SUr)]

    def dbl(shape, dt, name):
        return [p.sb(shape, dt, "%s_%d" % (name, i)) for i in range(2)]
    qk_t1 = dbl([128, 256], F32, "qk_t1")
    qk_t2 = dbl([128, 256], F32, "qk_t2")
    qk_r = dbl([128, 256], BF16, "qk_r")
    kd = dbl([128, 128], BF16, "kd")
    Vr = dbl([128, 256], BF16, "Vr")
    G = dbl([128, 512], F32, "G")
    sig = dbl([128, 256], F32, "sig")
    lf = dbl([128, 256], F32, "lf")
    kk = dbl([128, 256], F32, "kk")
    hq = dbl([128, 256], F32, "hq")
    hv = dbl([128, 256], BF16, "hv")
    _t1 = p.sb([128, 512], F32, "dq_t1")
    _t2 = p.sb([128, 512], F32, "dq_t2")
    dq_t1 = [_t1, _t1]
    dq_t2 = [_t2, _t2]
    dqk = dbl([128, 512], BF16, "dqk")
    rT = dbl([128, 3, 128], BF16, "rT")
    dqT = dbl([128, 2, 2, 128], BF16, "dqT")
    for i_ in range(2):
        p.memset("dve", dqT[i_][0][:], 0.0, [dqT[i_][1]])
    AT_r = dbl([128, 128], BF16, "AT_r")
    eq = dbl([128, 256], F32, "eq")
    qt = dbl([128, 256], BF16, "qt")
    kt = dbl([128, 256], BF16, "kt")
    kbar = dbl([128, 256], F32, "kbar")
    kbZ = dbl([128, 2, 4, 128], BF16, "kbZ")
    hT = dbl([128, 2, 128], BF16, "hT")
    eqT = dbl([128, 2, 128], BF16, "eqT")
    AT_h = dbl([128, 2, 128], BF16, "AT_h")
    dec = dbl([128, 2, 4], F32, "dec")
    on = dbl([128, 768], BF16, "on")
    oT_sb = dbl([128, 6, 128], BF16, "oT_sb")
    sm = dbl([128, 16], F32, "sm")
    dtmp = dbl([128, 2, 128], F32, "dtmp")
    wda = dbl([128, 128], F32, "wda")
    PT = [p.sb([128, 512], BF16, "PT%d" % i) for i in range(3)]
    pt_i = [0]
    r_oT = R("oT_dram")

    oidx = r_oidx = oT_v = None
    if "oT_sh" in io:
        oidx, r_oidx = p.sb([128, 6, NCH], mybir.dt.int32, "oidx")
        p.dma("sp", oidx[:], io["oidx"], [], [r_oidx], "c_oidx")
    else:
        oT_v = io["oT"].rearrange("(k p) t -> p k t", p=128)
    hn_full_v = hn_all_v = hn_meta_v = None
    if "hn_full" in io:
        hn_full_v = io["hn_full"].rearrange("(k p) t -> p k t", p=128)
    else:
        hn_all_v = io["hn_all"].rearrange("r (k p) t -> r p k t", p=128)
        hn_meta_v = io["hn_meta"].rearrange("(k p) t -> p k t", p=128)
    rope_v = io["rope"].rearrange("(c p) n -> p c n", p=128)

    def rope_ops(b, src_ps, r_src, t1, t2, dst, ngrp, half, cos_ap, sin_ap, nsin_ap, r_tab):
        (t1a, r_t1), (t2a, r_t2), (da_, r_d) = t1, t2, dst
        w = ngrp * 2 * half
        p.tt("dve", t1a[:, :w].rearrange("p (g h) -> p g h", h=half), src_ps.rearrange("p (g h) -> p g h", h=half),
             cos_ap.unsqueeze(1).to_broadcast([128, ngrp * 2, half]), ALU.mult, [r_src, r_tab], [r_t1])
        sv = src_ps.rearrange("p (g two h) -> p g two h", two=2, h=half)
        t2v = t2a[:, :w].rearrange("p (g two h) -> p g two h", two=2, h=half)
        p.tt("dve", t2v[:, :, 0, :], sv[:, :, 1, :], nsin_ap.unsqueeze(1).to_broadcast([128, ngrp, half]), ALU.mult,
             [r_src, r_tab], [r_t2])
        p.tt("dve", t2v[:, :, 1, :], sv[:, :, 0, :], sin_ap.unsqueeze(1).to_broadcast([128, ngrp, half]), ALU.mult,
             [r_src, r_tab], [r_t2])
        p.tt("pool", da_[:, :w], t1a[:, :w], t2a[:, :w], ALU.add, [r_t1, r_t2], [r_d])

    import os
    KSTOP = int(os.environ.get("KSTOP", "99"))

    def bail():
        if oT_v is not None:
            p.dma("sp", oT_v[:, 0:1, 0:128], ident[:].unsqueeze(1), [r_id], [r_oT], "st_o0")
        return r_oT
    if KSTOP == 0:
        return bail()
    ngroups = 1 + (nchunks - 1 + 3) // 4
    chunk_list = []
    for gi in range(ngroups):
        if gi == 0:
            clist = [0]
        else:
            c0 = 1 + (gi - 1) * 4
            clist = [c for c in range(c0, min(c0 + 4, nchunks))]
        for ji, c in enumerate(clist):
            chunk_list.append((gi, ji, c, clist))

    def group_load(gi, clist):
        hb, r_hb = hnb[gi % 2]
        rt, r_rt = rtb[gi % 2]
        if gi == 0:
            p.dma("sp", hb[:, :, 0:128], hn_full_v[:, :, 0:128] if hn_full_v is not None else hn_meta_v, [], [r_hb], "hn%d" % (gi % 2))
            p.dma("sp", rt[:, 0:1, :], rope_v[:, 0:1, :], [], [r_rt], "rt%d" % (gi % 2))
        else:
            c0 = clist[0]
            rk = (gi - 1) // 4
            lo = ((gi - 1) % 4) * 512
            nt = len(clist) * 128
            p.dma("sp", hb[:, :, 0:nt], hn_full_v[:, :, c0 * 128:c0 * 128 + nt] if hn_full_v is not None else hn_all_v[rk, :, :, lo:lo + nt],
                  [], [r_hb], "hn%d" % (gi % 2))
            p.dma("sp", rt[:, 0:len(clist), :], rope_v[:, c0:c0 + len(clist), :], [], [r_rt], "rt%d" % (gi % 2))

    def front_fns(gi, ji, c):
        b = c % 2
        hb, r_hb = hnb[gi % 2]
        rt, r_rt = rtb[gi % 2]
        hn_c = hb[:, :, ji * 128:(ji + 1) * 128]
        cosR, sinR, nsinR = rt[:, ji, 0:64], rt[:, ji, 64:128], rt[:, ji, 128:192]

        def proj(ti):
            g, r_g = gp()
            for k in range(8):
                p.mm(g[:, :], hn_c[:, k, :], W[:, k, ti * 512:(ti + 1) * 512], k == 0, k == 7, [r_hb, r_W], [r_g])
            return g, r_g

        def f1():
            p.memset("pool", sm[b][0][:], 0.0, [sm[b][1]])
            g0, r_g0 = proj(0)
            rope_ops(b, g0[:, 0:256], r_g0, qk_t1[b], qk_t2[b], qk_r[b], 2, 64, cosR, sinR, nsinR, r_rt)
            p.act(Vr[b][0][:], g0[:, 256:512], AF.Identity, [r_g0], [Vr[b][1]])
            p.ts("pool", kd[b][0][:], qk_r[b][0][:, 128:256], cols[:, 0:1], 0.0, ALU.mult, ALU.add, [qk_r[b][1], r_cols], [kd[b][1]])
            g1, r_g1 = proj(1)
            p.act(G[b][0][:], g1[:, :], AF.Silu, [r_g1], [G[b][1]])

        def f2():
            g2, r_g2 = proj(2)
            p.act(sig[b][0][:], g2[:, 256:512], AF.Sigmoid, [r_g2], [sig[b][1]])
            p.act(hq[b][0][:], g2[:, 0:256], AF.Identity, [r_g2], [hq[b][1]])
            p.tt("dve", sig[b][0][:], sig[b][0][:], oml[:], ALU.mult, [sig[b][1], r_oml], [sig[b][1]])
            p.tt("dve", sig[b][0][:], sig[b][0][:], lb[:], ALU.add, [sig[b][1], r_lb], [sig[b][1]])
            p.act(lf[b][0][:], sig[b][0][:], AF.Ln, [sig[b][1]], [lf[b][1]])
            p.ts("pool", kk[b][0][:], sig[b][0][:], -1.0, 1.0, ALU.mult, ALU.add, [sig[b][1]], [kk[b][1]])

        def f3():
            g3, r_g3 = proj(3)
            p.act(hv[b][0][:], g3[:, 0:256], AF.Identity, [r_g3], [hv[b][1]])
            for h in range(2):
                p.act(VA[h][0][:, c, 0:128], g3[:, 256 + h * 128:256 + (h + 1) * 128], AF.Identity, [r_g3], [VA_r[h][c]])
                p.cp("pool", VA[h][0][:, c, 128:129], vcol[:, c:c + 1], [r_vcol], [VA_r[h][c]])
            g4, r_g4 = proj(4)
            rope_ops(b, g4[:, 0:512], r_g4, dq_t1[b], dq_t2[b], dqk[b], 8, 32, rt[:, ji, 192:224], rt[:, ji, 224:256],
                     rt[:, ji, 256:288], r_rt)

        def f4():
            for i in range(2):
                p.tr(TRb[:, i * 128:(i + 1) * 128], qk_r[b][0][:, i * 128:(i + 1) * 128], ident[:], [qk_r[b][1], r_id], [r_TRb])
            for i in range(4):
                p.tr(TRb[:, (2 + i) * 128:(3 + i) * 128], dqk[b][0][:, i * 128:(i + 1) * 128], ident[:], [dqk[b][1], r_id], [r_TRb])
            p.cp("dve", rT[b][0][:, 0, :], TRb[:, 0:128], [r_TRb], [rT[b][1]])
            p.tt("dve", rT[b][0][:, 1, :], TRb[:, 0:128], tabs[:, 1, :], ALU.mult, [r_TRb, r_tabs], [rT[b][1]])
            p.cp("dve", rT[b][0][:, 2, :], TRb[:, 128:256], [r_TRb], [rT[b][1]])
            for h in range(2):
                p.act(dqT[b][0][0:64, h, 0, :], TRb[0:64, (2 + h) * 128:(3 + h) * 128], AF.Identity, [r_TRb], [dqT[b][1]])
                p.act(dqT[b][0][64:128, h, 1, :], TRb[64:128, (2 + h) * 128:(3 + h) * 128], AF.Identity, [r_TRb], [dqT[b][1]])
            for h in range(2):
                p.act(KT[h][0][:, c * 128:(c + 1) * 128], TRb[:, (4 + h) * 128:(5 + h) * 128], AF.Identity, [r_TRb], [KT_r[h][c]])
        return [f1, f2, f3, f4]

    group_load(0, chunk_list[0][3])
    for f_ in front_fns(*chunk_list[0][:3]):
        f_()
    for ci_, (gi, ji, c, clist) in enumerate(chunk_list):
            b = c % 2
            steps = []
            def ret_a(b=b, c=c):
                gs, r_gs = gp()
                p.mm(gs[:, 0:128], rT[b][0][:, 2, :], rT[b][0][:, 0, :], True, True, [rT[b][1]], [r_gs])
                p.tt("dve", AT_r[b][0][:], gs[:, 0:128], tabs[:, 0, :], ALU.mult, [r_gs, r_tabs], [AT_r[b][1]])
                gsu, r_gsu = gp()
                p.mm(gsu[:, 0:256], kd[b][0][:], Vr[b][0][:], True, True, [kd[b][1], Vr[b][1]], [r_gsu])
                p.stt("dve", S_ret[:], S_ret[:], cols[:, 1:2], gsu[:, 0:256], ALU.mult, ALU.add, [r_Sret, r_cols, r_gsu], [r_Sret])
                p.cp("pool", Sb_ret2[1 - b][0][:], S_ret[:], [r_Sret], [Sb_ret2[1 - b][1]])

            def ret_b(b=b, c=c):
                p.mm(OArh[:, 0:256], AT_r[b][0][:], Vr[b][0][:], True, False, [AT_r[b][1], Vr[b][1]], [r_OArh])
                p.mm(OArh[:, 0:256], rT[b][0][:, 1, :], Sb_ret2[b][0][:], False, True, [rT[b][1], Sb_ret2[b][1]], [r_OArh])
                p.act(dtmp[b][0][:].rearrange("p a n -> p (a n)"), OArh[:, 0:256], AF.Square, [r_OArh], [dtmp[b][1], sm[b][1]],
                      accum_out=sm[b][0][:, 0:1])
                p.act(sm[b][0][:, 0:1], sm[b][0][:, 0:1], AF.Sqrt, [sm[b][1]], [sm[b][1]], bias=EPS, scale=1.0 / 256)
                p.recip(sm[b][0][:, 0:1], sm[b][0][:, 0:1], [sm[b][1]], [sm[b][1]])
                p.stt("dve", on[b][0][:, 0:256], OArh[:, 0:256], sm[b][0][:, 0:1], G[b][0][:, 0:256], ALU.mult, ALU.mult,
                      [r_OArh, sm[b][1], G[b][1]], [on[b][1]])
            steps.append(ret_a)

            def hg_prep(b=b, c=c):
                gc, r_gc = gp()
                p.mm(gc[:, 0:256], tabs[:, 2, :], lf[b][0][:], True, True, [r_tabs, lf[b][1]], [r_gc])
                p.mm(gc[:, 256:512], tabs[:, 3, :], lf[b][0][:], True, True, [r_tabs, lf[b][1]], [r_gc])
                p.act(eq[b][0][:], gc[:, 0:256], AF.Exp, [r_gc], [eq[b][1]])
                p.tt("dve", qt[b][0][:], hq[b][0][:], eq[b][0][:], ALU.mult, [hq[b][1], eq[b][1]], [qt[b][1]])
                p.act(eq[b][0][:], gc[:, 0:256], AF.Exp, [r_gc, qt[b][1]], [eq[b][1]], scale=-1.0)
                p.tt("pool", kt[b][0][:], kk[b][0][:], eq[b][0][:], ALU.mult, [kk[b][1], eq[b][1]], [kt[b][1]])
                p.act(kbar[b][0][:], gc[:, 256:512], AF.Exp, [r_gc], [kbar[b][1]])
                p.tt("pool", kbar[b][0][:], kbar[b][0][:], kk[b][0][:], ALU.mult, [kbar[b][1], kk[b][1]], [kbar[b][1]])
                for h in range(2):
                    p.tt("pool", kbZ[b][0][:, h, :, :], kbar[b][0][:, h * 128:(h + 1) * 128].unsqueeze(1).to_broadcast([128, 4, 128]),
                         cols[:, 2:6].unsqueeze(2).to_broadcast([128, 4, 128]), ALU.mult, [kbar[b][1], r_cols], [kbZ[b][1]])
                gd, r_gd = gp()
                for h in range(2):
                    p.mm(gd[:, h * 4:(h + 1) * 4], lf[b][0][:, h * 128:(h + 1) * 128], cols[:, 2:6], True, True, [lf[b][1], r_cols], [r_gd])
                p.act(dec[b][0][:].rearrange("p h j -> p (h j)"), gd[:, 0:8], AF.Exp, [r_gd], [dec[b][1]])
            steps.append(hg_prep)

            def hg_prep2(b=b, c=c):
                for h in range(2):
                    p.tr(TRb[:, h * 128:(h + 1) * 128], qt[b][0][:, h * 128:(h + 1) * 128], ident[:], [qt[b][1], r_id], [r_TRb])
                    p.tr(TRb[:, (2 + h) * 128:(3 + h) * 128], kt[b][0][:, h * 128:(h + 1) * 128], ident[:], [kt[b][1], r_id], [r_TRb])
                p.cp("dve", hT[b][0][:].rearrange("p h t -> p (h t)"), TRb[:, 256:512], [r_TRb], [hT[b][1]])
                p.cp("dve", eqT[b][0][:].rearrange("p h t -> p (h t)"), TRb[:, 0:256], [r_TRb], [eqT[b][1]])
                for h in range(2):
                    qzf = QZ[h][0][:].rearrange("p j t -> p (j t)")
                    p.cp("pool", rawap(qzf, [[160, 4], [1, 32]]), eqT[b][0][:, h, :].rearrange("p (j i) -> p j i", i=32),
                         [eqT[b][1]], [QZ[h][1]])
            steps.append(hg_prep2)

            def hg_head(h, b=b, c=c):
                oc = slice(256 + h * 128, 256 + (h + 1) * 128)
                st = {}

                def fa():
                    gs, r_gs = gp()
                    p.mm(gs[:, 0:128], hT[b][0][:, h, :], eqT[b][0][:, h, :], True, True, [hT[b][1], eqT[b][1]], [r_gs])
                    p.tt("dve", AT_h[b][0][:, h, :], gs[:, 0:128], tabs[:, 4, :], ALU.mult, [r_gs, r_tabs], [AT_h[b][1]])
                    gu, r_gu = gp()
                    for j in range(4):
                        p.mm(gu[:, j * 128:(j + 1) * 128], kbZ[b][0][:, h, j, :], hv[b][0][:, h * 128:(h + 1) * 128], True, True,
                             [kbZ[b][1], hv[b][1]], [r_gu])
                    S_, r_S = S_hg[h]
                    for j in range(4):
                        p.stt("dve", S_[:], S_[:], dec[b][0][:, h, j:j + 1], gu[:, j * 128:(j + 1) * 128], ALU.mult, ALU.add,
                              [r_S, dec[b][1], r_gu], [r_S])
                        sbn, r_sbn = Sb_hg[h][b][j + 1] if j < 3 else Sb_hg[h][1 - b][0]
                        p.cp("pool", sbn[:], S_[:], [r_S], [r_sbn])

                def fb():
                    p.mm(OArh[:, oc], AT_h[b][0][:, h, :], hv[b][0][:, h * 128:(h + 1) * 128], True, False, [AT_h[b][1], hv[b][1]], [r_OArh])
                    for j in range(4):
                        sbj, r_sbj = Sb_hg[h][b][j]
                        p.mm(OArh[:, oc], QZ[h][0][:, j, :], sbj[:], False, j == 3, [QZ[h][1], r_sbj], [r_OArh])
                    k0 = 1 + h
                    p.act(dtmp[b][0][:, 0, :], OArh[:, oc], AF.Square, [r_OArh], [dtmp[b][1], sm[b][1]], accum_out=sm[b][0][:, k0:k0 + 1])
                    p.act(sm[b][0][:, k0:k0 + 1], sm[b][0][:, k0:k0 + 1], AF.Sqrt, [sm[b][1]], [sm[b][1]], bias=EPS, scale=1.0 / 128)
                    p.recip(sm[b][0][:, k0:k0 + 1], sm[b][0][:, k0:k0 + 1], [sm[b][1]], [sm[b][1]])
                    p.stt("dve", on[b][0][:, oc], OArh[:, oc], sm[b][0][:, k0:k0 + 1], G[b][0][:, oc], ALU.mult, ALU.mult,
                          [r_OArh, sm[b][1], G[b][1]], [on[b][1]])
                return fa, fb
            hh0, hh1 = hg_head(0), hg_head(1)
            steps += [hh0[0], hh1[0], ret_b, hh0[1], hh1[1]]

            def da_head(h, b=b, c=c):
                units = [(j, m) for j in range(c + 1) for m in range(2)]
                batches = [units[i:i + 4] for i in range(0, len(units), 4)]
                out = []

                def mk_batch(bi, bat):
                    st = {}

                    def fa():
                        gs, r_gs = gp()
                        for ui, (j, m) in enumerate(bat):
                            p.mm(gs[:, ui * 128:(ui + 1) * 128], KT[h][0][:, j * 128:(j + 1) * 128],
                                 dqT[b][0][:, h, m, :], True, True, [KT_r[h][j], dqT[b][1]], [r_gs])
                        pt, r_pt = PT[pt_i[0] % 3]
                        pt_i[0] += 1
                        st["pt"] = (pt, r_pt)
                        n = len(bat) * 128
                        p.act(pt[:, 0:n], gs[:, 0:n], AF.Exp, [r_gs], [r_pt], scale=0.125)
                        for ui, (j, m) in enumerate(bat):
                            if j == c:
                                p.tt("pool", pt[:, ui * 128:(ui + 1) * 128], pt[:, ui * 128:(ui + 1) * 128], tabs[:, 5, :], ALU.mult,
                                     [r_pt, r_tabs], [r_pt])

                    def fb():
                        pt, r_pt = st["pt"]
                        for ui, (j, m) in enumerate(bat):
                            p.mm(OAm[m][0][:, 0:130], pt[:, ui * 128:(ui + 1) * 128], VA[h][0][:, j, 0:130],
                                 j == 0, j == c, [r_pt, VA_r[h][j]], [OAm[m][1]])
                    return fa, fb
                pairs = [mk_batch(bi, bat) for bi, bat in enumerate(batches)]
                out.append(pairs[0][0])
                for bi in range(len(pairs)):
                    if bi + 1 < len(pairs):
                        out.append(pairs[bi + 1][0])
                    out.append(pairs[bi][1])

                def epi():
                    s0 = 4 + 4 * h
                    smt, r_sm = sm[b]
                    for m in range(2):
                        p.ts("dve", smt[:, s0 + m:s0 + m + 1], OAm[m][0][:, 128:129], 1e-30, None, ALU.max, ALU.bypass,
                             [OAm[m][1]], [r_sm])
                        p.recip(smt[:, s0 + m:s0 + m + 1], smt[:, s0 + m:s0 + m + 1], [r_sm], [r_sm])
                        p.act(dtmp[b][0][:, m, :], OAm[m][0][:, 0:128], AF.Identity, [OAm[m][1], r_sm], [dtmp[b][1]],
                              scale=smt[:, s0 + m:s0 + m + 1])
                    p.stt("dve", wda[b][0][:], dtmp[b][0][:, 1, :], nlam[:, 0:1], dtmp[b][0][:, 0, :], ALU.mult, ALU.add,
                          [dtmp[b][1], r_nlam], [wda[b][1]])
                    p.act(dtmp[b][0][:, 0, :], wda[b][0][:], AF.Square, [wda[b][1]], [dtmp[b][1], r_sm], accum_out=smt[:, s0 + 2:s0 + 3])
                    p.act(smt[:, s0 + 2:s0 + 3], smt[:, s0 + 2:s0 + 3], AF.Sqrt, [r_sm], [r_sm], bias=EPS, scale=1.0 / 128)
                    p.recip(smt[:, s0 + 2:s0 + 3], smt[:, s0 + 2:s0 + 3], [r_sm], [r_sm])
                    p.stt("dve", on[b][0][:, 512 + h * 128:512 + (h + 1) * 128], wda[b][0][:], smt[:, s0 + 2:s0 + 3], gsub[:],
                          ALU.mult, ALU.mult, [wda[b][1], r_sm, r_gsub], [on[b][1]])
                out.append(epi)
                return out

            def finish(b=b, c=c):
                def f():
                    for i in range(6):
                        p.tr(TRb[:, i * 128:(i + 1) * 128], on[b][0][:, i * 128:(i + 1) * 128], ident[:], [on[b][1], r_id], [r_TRb])
                    p.act(oT_sb[b][0][:].rearrange("p k t -> p (k t)"), TRb[:, 0:768], AF.Identity, [r_TRb], [oT_sb[b][1]])
                    if oidx is None:
                        p.dma("sp", oT_v[:, :, c * 128:(c + 1) * 128], oT_sb[b][0][:], [oT_sb[b][1]], [r_oT], "st_o%d" % b)
                    else:
                        for k6 in range(6):
                            p.op("pool", lambda e, k6=k6: e.indirect_dma_start(
                                out=io["oT_sh"],
                                out_offset=bass.IndirectOffsetOnAxis(ap=oidx[:, k6, c:c + 1], axis=0),
                                in_=oT_sb[b][0][:, k6, :], in_offset=None, bounds_check=4 * 768 * NCH - 1, oob_is_err=False),
                                [oT_sb[b][1], r_oidx], [r_oT], dma_key="st_o%d" % b)
                return f

            da_list = da_head(0) + da_head(1) + [finish()]
            fr = []
            if ci_ + 1 < len(chunk_list):
                ngi, nji, ncc, nclist = chunk_list[ci_ + 1]
                if nji == 0:
                    fr.append(lambda ngi=ngi, nclist=nclist: group_load(ngi, nclist))
                fr += front_fns(ngi, nji, ncc)
            allsteps = []
            for i_ in range(max(len(steps), len(fr))):
                if i_ < len(steps):
                    allsteps.append(steps[i_])
                if i_ < len(fr):
                    allsteps.append(fr[i_])
            ns, nd = len(allsteps), len(da_list)
            si = 0
            for di, dfn in enumerate(da_list[:-1]):
                while si < ns and si * (nd - 1) <= di * ns:
                    allsteps[si]()
                    si += 1
                dfn()
            while si < ns:
                allsteps[si]()
                si += 1
            da_list[-1]()
    return r_oT


def rms_tile(p, h_t, r_h, n, gcol, r_g, out_t, r_out, ones, r_ones, sq, r_sq, rstd, r_rstd, ps, r_ps):
    p.act(sq[:, :, :n], h_t[:, :, :n], AF.Square, [r_h], [r_sq])
    for k in range(8):
        p.mm(ps[:, :n], ones[:], sq[:, k, :n], k == 0, k == 7, [r_ones, r_sq], [r_ps])
    p.act(rstd[:, :n], ps[:, :n], AF.Sqrt, [r_ps], [r_rstd], bias=EPS, scale=1.0 / D)
    p.recip(rstd[:, :n], rstd[:, :n], [r_rstd], [r_rstd])
    for k in range(8):
        p.stt("dve" if k % 2 == 0 else "pool", out_t[:, k, :n], h_t[:, k, :n], gcol[:, k:k + 1], rstd[:, :n], ALU.mult, ALU.mult,
              [r_h, r_g, r_rstd], [r_out])


TILES = [(0, 128)] + [(128 + i * 342, 342) for i in range(6)]
NT = 342


def phase_A(p, io, tiles=None):
    tiles = tiles or TILES
    ones, r_ones = p.sb([128, 128], BF16, "onesA")
    p.memset("pool", ones[:], 1.0, [r_ones])
    gcol, r_g = p.sb([128, 8], F32, "gcolA")
    p.dma("sp", gcol[:], io["g"], [], [r_g], "c_g")
    xv = io["xT"].rearrange("(k p) t -> p k t", p=128)
    ov = io["hnT"].rearrange("(k p) t -> p k t", p=128)
    r_o = Res("hnT_d")
    bufs = []
    for i in range(2):
        bufs.append((p.sb([128, 8, NT], F32, "hA%d" % i), p.sb([128, 8, NT], BF16, "oA%d" % i), p.sb([128, 8, NT], BF16, "sqA%d" % i),
                     p.sb([128, NT], F32, "rsA%d" % i), p.ps([128, 512], F32, "psA%d" % i)))
    for ti, (t0, n) in enumerate(tiles):
        (h_t, r_h), (o_t, r_ot), (sq, r_sq), (rs, r_rs), (ps, r_ps) = bufs[ti % 2]
        p.dma("sp", h_t[:, :, :n], xv[:, :, t0:t0 + n], [], [r_h], "ldA%d" % (ti % 2))
        rms_tile(p, h_t, r_h, n, gcol, r_g, o_t, r_ot, ones, r_ones, sq, r_sq, rs, r_rs, ps, r_ps)
        p.dma("sp", ov[:, :, t0:t0 + n], o_t[:, :, :n], [r_ot], [r_o], "stA%d" % (ti % 2))
    return [r_o]


def phase_C(p, io, last, tiles=None, ybase=132, hist_reset=(0, 1)):
    nc = p.nc
    tiles = tiles or TILES
    WB, r_WB = p.sb([128, 67584], BF16, "WBUF")
    ones, r_ones = p.sb([128, 128], BF16, "onesC")
    p.memset("pool", ones[:], 1.0, [r_ones])
    gc2, r_gc2 = p.sb([128, 8], F32, "gffn")
    gc3, r_gc3 = p.sb([128, 8], F32, "gnext")
    p.dma("sp", gc2[:], io["g_ffn"], [], [r_gc2], "c_g2")
    p.dma("sp", gc3[:], io["g_next"], [], [r_gc3], "c_g3")
    cw, r_cw = p.sb([128, 44, 4], F32, "convw")
    p.dma("sp", cw[:], io["convp"], [], [r_cw], "c_cw")
    wg = WB[:, 0:24576].rearrange("p (k n) -> p k n", k=8)
    wb = WB[:, 24576:49152].rearrange("p (b k n) -> p b k n", b=3, k=8)
    wo = WB[:, 49152:57344].rearrange("p (k n) -> p k n", k=8)
    wgv = io["wg"].rearrange("(k p) n -> p k n", p=128)
    wbv = io["wb"].rearrange("b (k p) n -> p b k n", p=128)
    wov = io["wo"].rearrange("(k p) n -> p k n", p=128)
    for k in range(8):
        p.dma("pool", wg[:, k, :], wgv[:, k, :], [], [r_WB], "c_W")
    for b_ in range(3):
        for k in range(8):
            p.dma("pool", wb[:, b_, k, :], wbv[:, b_, k, :], [], [r_WB], "c_W")
    for k in range(8):
        p.dma("pool", wo[:, k, :], wov[:, k, :], [], [r_WB], "c_W")

    h_t, r_h = p.sb([128, 8, NT], F32, "hC")
    hn_t, r_hn = p.sb([128, 8, NT], BF16, "hnC")
    big, r_big = p.sb([128, 24 * NT], BF16, "bigC")
    y_t, r_y = p.sb([128, 8, NT], BF16, "yC")
    gt = [p.sb([128, NT], F32, "gt%d" % i) for i in range(2)]
    yacc, r_yacc = p.sb([128, NT], F32, "yacc")
    tmp = [p.sb([128, NT], F32, "tmpC%d" % i) for i in range(2)]
    sq, r_sq = p.sb([128, 8, NT], BF16, "sqC")
    rstd, r_rstd = p.sb([128, NT], F32, "rstdC")
    GPc = [p.ps([128, 512], F32, "GPc%d" % i) for i in range(7)]
    gi_ = [0]

    def gp():
        g = GPc[gi_[0] % 7]
        gi_[0] += 1
        return g
    hv_in = io["hT_in"].rearrange("(k p) t -> p k t", p=128)
    hnv_in = io["hnT_in"].rearrange("(k p) t -> p k t", p=128)
    brv = io["brT"].rearrange("g (k p) t -> p g k t", p=128)
    hmid_v = io["hmidT"].rearrange("(k p) t -> p k t", p=128)
    hn2_v = io["hn2T"].rearrange("(k p) t -> p k t", p=128)
    r_hmid, r_hn2d = Res("hmid_d"), Res("hn2_d")

    for ti, (t0, n) in enumerate(tiles):
        if int(os.environ.get("KC_STOP", "99")) == 0 and ti == 1:
            return [r_hmid, r_hn2d]
        br_t = big[:, 0:24 * n].rearrange("p (g k t) -> p g k t", g=4, k=6)
        p.dma("sp", h_t[:, :, :n], hv_in[:, :, t0:t0 + n], [], [r_h], "ldh")
        p.dma("sp", hn_t[:, :, :n], hnv_in[:, :, t0:t0 + n], [], [r_hn], "ldhn")
        for g in range(4):
            p.dma("sp", br_t[:, g, :, :], brv[:, g, :, t0:t0 + n], [], [r_big], "ldbr")
        for m in range(8):
            ms = slice(m * 128, (m + 1) * 128)
            for nb in range(3):
                gps, r_gps = gp()
                for k in range(8):
                    p.mm(gps[:, :n], wg[:, k, nb * 1024 + m * 128:nb * 1024 + (m + 1) * 128], hn_t[:, k, :n], k == 0, k == 7,
                         [r_WB, r_hn], [r_gps])
                g_, r_g_ = gt[nb % 2]
                p.act(g_[:, :n], gps[:, :n], AF.Sigmoid, [r_gps], [r_g_])
                bps, r_bps = gp()
                for k in range(8):
                    p.mm(bps[:, :n], wb[:, nb, k, ms], br_t[:, k // 2, nb * 2 + k % 2, :], k == 0, k == 7, [r_WB, r_big], [r_bps])
                if nb == 0:
                    p.tt("dve", yacc[:, :n], g_[:, :n], bps[:, :n], ALU.mult, [r_g_, r_bps], [r_yacc])
                else:
                    t_, r_t = tmp[nb % 2]
                    p.tt("dve", t_[:, :n], g_[:, :n], bps[:, :n], ALU.mult, [r_g_, r_bps], [r_t])
                    if nb == 1:
                        p.tt("gps", yacc[:, :n], yacc[:, :n], t_[:, :n], ALU.add, [r_yacc, r_t], [r_yacc])
                    else:
                        p.tt("gps", y_t[:, m, :n], yacc[:, :n], t_[:, :n], ALU.add, [r_yacc, r_t], [r_y])
        for m in range(8):
            ops_, r_ops = gp()
            for k in range(8):
                p.mm(ops_[:, :n], wo[:, k, m * 128:(m + 1) * 128], y_t[:, k, :n], k == 0, k == 7, [r_WB, r_y], [r_ops])
            p.tt("dve", h_t[:, m, :n], h_t[:, m, :n], ops_[:, :n], ALU.add, [r_h, r_ops], [r_h])
        if ti == 0:
            p.memset("pool", h_t[:, :, 0:112], 0.0, [r_h])
        p.dma("sp", hmid_v[:, :, t0:t0 + n], h_t[:, :, :n], [r_h], [r_hmid], "sth")
        ps, r_ps = gp()
        rms_tile(p, h_t, r_h, n, gc2, r_gc2, hn_t, r_hn, ones, r_ones, sq, r_sq, rstd, r_rstd, ps, r_ps)
        p.dma("sp", hn2_v[:, :, t0:t0 + n], hn_t[:, :, :n], [r_hn], [r_hn2d], "sthn")

    KC = int(os.environ.get("KC_STOP", "99"))
    if KC == 1:
        return [r_hmid, r_hn2d]
    wfi = WB[:, 0:45056].rearrange("p (k n) -> p k n", k=8)
    wfo = WB[:, 45056:67584].rearrange("p (k n) -> p k n", k=22)
    wfiv = io["wfi"].rearrange("(k p) n -> p k n", p=128)
    wfov = io["wfo"].rearrange("(k p) n -> p k n", p=128)
    for k in range(8):
        p.dma("pool", wfi[:, k, :], wfiv[:, k, :], [], [r_WB], "c_W")
    for k in range(22):
        p.dma("pool", wfo[:, k, :], wfov[:, k, :], [], [r_WB], "c_W")
    U = [p.sb([128, NT + 2], F32, "U%d" % i) for i in range(2)]
    cb = [p.sb([128, NT], F32, "cb%d" % i) for i in range(2)]
    Hh, r_Hh = p.sb([128, 44, 2], F32, "Hh")
    sg, r_sg = p.sb([128, NT], F32, "sgC")
    r_hout, r_out2 = Res("hout_d"), Res("out2_d")
    hout_v = io["hT_out"].rearrange("(k p) t -> p k t", p=128)
    if last:
        yv = io["yT"].rearrange("(k p) t -> p k t", p=128)
        yo, r_yo = p.sb([128, 8, NT], F32, "yo")
    else:
        hnout_v = io["hnT_out"].rearrange("(k p) t -> p k t", p=128)
    for ti, (t0, n) in enumerate(tiles):
        a_t = big[:, 0:22 * n].rearrange("p (k t) -> p k t", k=22)
        if ti in hist_reset:
            p.memset("pool", Hh[:], 0.0, [r_Hh])
        p.dma("sp", h_t[:, :, :n], hmid_v[:, :, t0:t0 + n], [r_hmid], [r_h], "ldh")
        p.dma("sp", hn_t[:, :, :n], hn2_v[:, :, t0:t0 + n], [r_hn2d], [r_hn], "ldhn")
        for i in range(22):
            for wi, ci in enumerate((i, 22 + i)):
                ups, r_ups = gp()
                for k in range(8):
                    p.mm(ups[:, :n], wfi[:, k, ci * 128:(ci + 1) * 128], hn_t[:, k, :n], k == 0, k == 7, [r_WB, r_hn], [r_ups])
                u_, r_u = U[wi]
                c_, r_c = cb[wi]
                p.act(u_[:, 2:2 + n], ups[:, :n], AF.Identity, [r_ups], [r_u])
                p.act(c_[:, :n], ups[:, :n], AF.Identity, [r_ups, r_cw], [r_c], scale=cw[:, ci, 2:3], bias=cw[:, ci, 3:4])
                p.cp("gps", u_[:, 0:2], Hh[:, ci, :], [r_Hh], [r_u])
                p.stt("dve", c_[:, :n], u_[:, 1:1 + n], cw[:, ci, 1:2], c_[:, :n], ALU.mult, ALU.add, [r_u, r_cw, r_c], [r_c])
                p.stt("dve", c_[:, :n], u_[:, 0:n], cw[:, ci, 0:1], c_[:, :n], ALU.mult, ALU.add, [r_u, r_cw, r_c], [r_c])
                p.cp("gps", Hh[:, ci, :], u_[:, n:n + 2], [r_u], [r_Hh])
            p.act(sg[:, :n], cb[0][0][:, :n], AF.Silu, [cb[0][1]], [r_sg])
            p.tt("gps", a_t[:, i, :], sg[:, :n], cb[1][0][:, :n], ALU.mult, [r_sg, cb[1][1]], [r_big])
        for m in range(8):
            fps, r_fps = gp()
            for k in range(22):
                p.mm(fps[:, :n], wfo[:, k, m * 128:(m + 1) * 128], a_t[:, k, :], k == 0, k == 21, [r_WB, r_big], [r_fps])
            p.tt("dve", h_t[:, m, :n], h_t[:, m, :n], fps[:, :n], ALU.add, [r_h, r_fps], [r_h])
        if ti == 0:
            p.memset("pool", h_t[:, :, 0:112], 0.0, [r_h])
        p.dma("sp", hout_v[:, :, t0:t0 + n], h_t[:, :, :n], [r_h], [r_hout], "sth")
        ps, r_ps = gp()
        if last:
            lo = max(0, ybase - t0)
            if lo < n:
                rms_tile(p, h_t, r_h, n, gc3, r_gc3, yo, r_yo, ones, r_ones, sq, r_sq, rstd, r_rstd, ps, r_ps)
                g0 = t0 + lo - ybase
                p.dma("sp", yv[:, :, g0:g0 + n - lo], yo[:, :, lo:n], [r_yo], [r_out2], "sty")
        else:
            rms_tile(p, h_t, r_h, n, gc3, r_gc3, hn_t, r_hn, ones, r_ones, sq, r_sq, rstd, r_rstd, ps, r_ps)
            p.dma("sp", hnout_v[:, :, t0:t0 + n], hn_t[:, :, :n], [r_hn], [r_out2], "sthn")
    return [r_hout, r_out2, r_hmid, r_hn2d]


def _dram(nc, name, shape, dt, kind):
    return nc.dram_tensor(name, list(shape), dt, kind=kind).ap()


def build_A():
    nc = bass.Bass("TRN2", target_bir_lowering=False)
    io = {"xT": _dram(nc, "xT", [D, TLOC], F32, "ExternalInput"), "g": _dram(nc, "g", [128, 8], F32, "ExternalInput"),
          "hnT": _dram(nc, "hnT", [D, TLOC], BF16, "ExternalOutput")}
    p = Prog(nc)
    outs = phase_A(p, io)
    p.wait_only("sp", [r.lw for r in outs])
    p.emit()
    return nc


def build_B(li, nchunks=NCH):
    nc = bass.Bass("TRN2", target_bir_lowering=False)
    I = "ExternalInput"
    io = {"hn_meta": _dram(nc, "hn_meta", [D, 128], BF16, I), "hn_all": _dram(nc, "hn_all", [4, D, 2048], BF16, I),
          "w": _dram(nc, "w", [D, NBW], F32, I), "rope": _dram(nc, "rope", [LTOT, 288], F32, I),
          "tabs": _dram(nc, "tabs", [8, 128, 128], F32, I), "cols": _dram(nc, "cols", [128, 8], F32, I),
          "vcol": _dram(nc, "vcol", [128, NCH], F32, I), "hg_lb": _dram(nc, "hg_lb", [DEPTH, 256], F32, I),
          "da_lambda": _dram(nc, "da_lambda", [1, 256], F32, I), "subln": _dram(nc, "subln", [1, 128], F32, I),
          "ident": _dram(nc, "ident", [128, 128], F32, I),
          "oT": _dram(nc, "oT", [768, LTOT], BF16, "ExternalOutput")}
    p = Prog(nc)
    r_o = phase_B(p, li, io, nchunks)
    p.wait_only("sp", [r_o.lw])
    p.emit()
    return nc


def build_C(last):
    nc = bass.Bass("TRN2", target_bir_lowering=False)
    I = "ExternalInput"
    O = "ExternalOutput"
    io = {"hT_in": _dram(nc, "hT_in", [D, TLOC], F32, I), "hnT_in": _dram(nc, "hnT_in", [D, TLOC], BF16, I),
          "brT": _dram(nc, "brT", [4, 768, TLOC], BF16, I), "wg": _dram(nc, "wg", [D, 3072], F32, I),
          "wb": _dram(nc, "wb", [3, D, D], F32, I), "wo": _dram(nc, "wo", [D, D], F32, I),
          "g_ffn": _dram(nc, "g_ffn", [128, 8], F32, I), "g_next": _dram(nc, "g_next", [128, 8], F32, I),
          "convp": _dram(nc, "convp", [128, 44, 4], F32, I), "wfi": _dram(nc, "wfi", [D, 2 * DFF], F32, I),
          "wfo": _dram(nc, "wfo", [DFF, D], F32, I),
          "hmidT": _dram(nc, "hmidT", [D, TLOC], F32, "Internal"), "hn2T": _dram(nc, "hn2T", [D, TLOC], BF16, "Internal"),
          "hT_out": _dram(nc, "hT_out", [D, TLOC], F32, O)}
    if last:
        io["yT"] = _dram(nc, "yT", [D, NLOC * 128], F32, O)
    else:
        io["hnT_out"] = _dram(nc, "hnT_out", [D, TLOC], BF16, O)
    p = Prog(nc)
    outs = phase_C(p, io, last)
    p.wait_only("sp", [r.lw for r in outs])
    p.emit()
    return nc


def const_tables():
    f32 = np.float32
    pos = (np.arange(LTOT) - 112).astype(f32)
    rope = np.zeros((LTOT, 288), f32)
    inv = (10000.0 ** (-np.arange(0, 128, 2, dtype=f32) / 128)).astype(f32)
    ang = pos[:, None] * inv[None, :]
    rope[:, 0:64] = np.cos(ang)
    rope[:, 64:128] = np.sin(ang)
    rope[:, 128:192] = -np.sin(ang)
    inv = (10000.0 ** (-np.arange(0, 64, 2, dtype=f32) / 64)).astype(f32)
    ang = pos[:, None] * inv[None, :]
    rope[:, 192:224] = np.cos(ang)
    rope[:, 224:256] = np.sin(ang)
    rope[:, 256:288] = -np.sin(ang)
    idx = np.arange(128)
    tabs_h, cols_h = [], []
    same = (idx[:, None] // 32) == (idx[None, :] // 32)
    for hd in range(4):
        log_g = np.log1p(-np.exp2(-5.0 - hd))
        tabs = np.zeros((8, 128, 128), f32)
        gap = idx[None, :] - idx[:, None]
        tabs[0] = np.where(gap >= 0, np.exp(log_g * np.maximum(gap, 0)), 0.0) * 128 ** -0.5
        tabs[1] = np.exp(log_g * (idx[None, :] + 1.0)) * np.ones((128, 1))
        tabs[2] = (same & (idx[:, None] <= idx[None, :])).astype(f32)
        tabs[3] = (same & (idx[:, None] > idx[None, :])).astype(f32)
        tabs[4] = (same & (idx[None, :] >= idx[:, None])).astype(f32)
        tabs[5] = (idx[None, :] >= idx[:, None]).astype(f32)
        cols = np.zeros((128, 8), f32)
        cols[:, 0] = np.exp(log_g * (127.0 - idx)) * 128 ** -0.5
        cols[:, 1] = np.exp(log_g * 128.0)
        for j in range(4):
            cols[:, 2 + j] = (idx // 32 == j)
        tabs_h.append(tabs)
        cols_h.append(cols)
    vcol = np.ones((128, NCH), f32)
    vcol[:112, 0] = 0.0
    return rope, tabs_h, cols_h, vcol


def gcols(g):
    return np.ascontiguousarray(np.asarray(g, np.float32).reshape(8, 128).T)


def w_group(w_in_l, g):
    s = lambda off, width: w_in_l[:, off + g * width: off + (g + 1) * width]
    parts = [s(0, 128), s(512, 128), s(1024, 256), s(2048, 256), s(6144, 256), s(3072, 256), s(4096, 256), s(5120, 256),
             s(9216, 256), s(7168, 256), s(8192, 256)]
    return np.ascontiguousarray(np.concatenate(parts, axis=1))


_NC_CACHE = {}
_DBG = None


def _get(name, fn):
    if name not in _NC_CACHE:
        _NC_CACHE[name] = fn()
    return _NC_CACHE[name]


def local_tokens(q):
    base = 128 + q * 2048
    return np.concatenate([np.arange(128), np.arange(base - HALO, base), np.arange(base, base + 2048)])


def kernel(x, meta, norm_mix_g, w_in, w_branch, w_out, hg_lb, da_lambda, da_subln_g, norm_ffn_g, w_ffn_in,
           ffn_conv_w, ffn_conv_b, w_ffn_out, norm_final_g):
    f32 = np.float32
    bf = ml_dtypes.bfloat16
    A = lambda a: np.asarray(a, f32)
    x, meta, w_in, w_branch, w_out = A(x), A(meta), A(w_in), A(w_branch), A(w_out)
    w_ffn_in, w_ffn_out, ffn_conv_w, ffn_conv_b = A(w_ffn_in), A(w_ffn_out), A(ffn_conv_w), A(ffn_conv_b)
    hg_lb, da_lambda, da_subln_g = A(hg_lb), A(da_lambda), A(da_subln_g)
    rope, tabs_h, cols_h, vcol = const_tables()
    ident = np.eye(128, dtype=f32)
    cores = list(range(8))
    hfull = np.zeros((2, LTOT, D), f32)
    hfull[:, 112:128] = meta[None]
    hfull[:, 128:] = x
    loc = [local_tokens(q) for q in range(4)]
    in_maps = [{"xT": np.ascontiguousarray(hfull[c // 4][loc[c % 4]].T), "g": gcols(norm_mix_g[0])} for c in cores]
    hT = [m["xT"] for m in in_maps]
    res = run_bass_kernel_spmd(_get("A", build_A), in_maps, core_ids=cores)
    hnT = [np.asarray(r["hnT"]) for r in res.results]
    if _DBG is not None:
        _DBG["hnT_A"] = hnT
    out = np.zeros((2, SEQ, D), f32)
    for li in range(DEPTH):
        last = li == DEPTH - 1
        in_maps = []
        for c in cores:
            b, g = c // 4, c % 4
            in_maps.append({
                "hn_meta": np.ascontiguousarray(hnT[b * 4][:, 0:128]),
                "hn_all": np.ascontiguousarray(np.stack([hnT[b * 4 + q][:, 132:] for q in range(4)])),
                "w": w_group(w_in[li], g), "rope": rope, "tabs": tabs_h[g], "cols": cols_h[g], "vcol": vcol,
                "hg_lb": np.ascontiguousarray(hg_lb[:, g * 256:(g + 1) * 256]),
                "da_lambda": np.ascontiguousarray(da_lambda[li].reshape(1, 256)),
                "subln": np.ascontiguousarray(da_subln_g[li].reshape(1, 128)), "ident": ident})
        res = run_bass_kernel_spmd(_get("B%d" % li, lambda: build_B(li)), in_maps, core_ids=cores)
        oT = [np.asarray(r["oT"]) for r in res.results]
        if _DBG is not None:
            _DBG["oT%d" % li] = oT
        convp = np.concatenate([ffn_conv_w[li].T, ffn_conv_b[li][:, None]], axis=1)
        convp = np.ascontiguousarray(convp.reshape(44, 128, 4).transpose(1, 0, 2))
        g_next = norm_final_g if last else norm_mix_g[li + 1]
        in_maps = []
        for c in cores:
            b, q = c // 4, c % 4
            in_maps.append({
                "hT_in": hT[c], "hnT_in": hnT[c],
                "brT": np.ascontiguousarray(np.stack([oT[b * 4 + g][:, loc[q]] for g in range(4)])),
                "wg": np.ascontiguousarray(w_in[li][:, 10240:13312]), "wb": w_branch[li], "wo": w_out[li],
                "g_ffn": gcols(norm_ffn_g[li]), "g_next": gcols(g_next), "convp": convp,
                "wfi": w_ffn_in[li], "wfo": w_ffn_out[li]})
        res = run_bass_kernel_spmd(_get("C%d" % int(last), lambda: build_C(last)), in_maps, core_ids=cores)
        hT = [np.asarray(r["hT_out"]) for r in res.results]
        if _DBG is not None:
            _DBG["hT%d" % li] = hT
        if last:
            for c in cores:
                out[c // 4, (c % 4) * 2048:(c % 4 + 1) * 2048] = np.asarray(res.results[c]["yT"]).T
        else:
            hnT = [np.asarray(r["hnT_out"]) for r in res.results]
    return out


def phase_X(p, priv, sh2, oidx):
    bufs = [p.sb([128, LTOT], BF16, "xb%d" % i) for i in range(2)]
    r_sh = Res("sh")
    n = 0
    for j in range(NGL):
        ix, r_ix = p.sb([128, 6], mybir.dt.int32, "xi%d" % j)
        p.dma("sp", ix[:], oidx[j], [], [r_ix], "ldxi%d" % j)
        pv = priv[j].rearrange("(k p) t -> p k t", p=128)
        for k6 in range(6):
            buf, r_buf = bufs[n % 2]
            p.dma("sp", buf[:], pv[:, k6, :], [], [r_buf], "ldx%d" % (n % 2))
            p.op("pool", lambda e, buf=buf, ix=ix, k6=k6: e.indirect_dma_start(
                out=sh2, out_offset=bass.IndirectOffsetOnAxis(ap=ix[:, k6:k6 + 1], axis=0), in_=buf[:], in_offset=None,
                bounds_check=4 * 768 - 1, oob_is_err=False), [r_buf, r_ix], [r_sh], dma_key="scx%d" % (n % 2))
            n += 1


NTF = 320
TILES_F = [(i * NTF, NTF) for i in range(LTOT // NTF)]


PAIR = os.environ.get("K_PAIR", "1") == "1"
NGL = 2 if PAIR else 4


def build_fused():
    nc = bass.Bass("TRN2", target_bir_lowering=False, num_devices=4) if PAIR else bass.Bass("TRN2", target_bir_lowering=False)
    I = "ExternalInput"
    ext = {}

    def inp(name, shape, dt=F32):
        ext[name] = _dram(nc, name, shape, dt, I)
        return ext[name]
    xT = inp("xT", [D, LTOT])
    rope = inp("rope", [LTOT, 288])
    vcol = inp("vcol", [128, NCH])
    ident = inp("ident", [128, 128])
    tabs = [inp("tabs%d" % g, [8, 128, 128]) for g in range(NGL)]
    cols = [inp("cols%d" % g, [128, 8]) for g in range(NGL)]
    hglb = [inp("hg_lb%d" % g, [DEPTH, 256]) for g in range(NGL)]
    oidx = [inp("oidx%d" % g, [128, 6], mybir.dt.int32) for g in range(NGL)] if PAIR else None
    gmix = [inp("g_mix%d" % l, [128, 8]) for l in range(DEPTH)]
    gfin = inp("g_fin", [128, 8])
    L = []
    for l in range(DEPTH):
        L.append({"w": [inp("w%d_%d" % (l, g), [D, NBW]) for g in range(NGL)],
                  "da_lambda": inp("da_lambda%d" % l, [1, 256]), "subln": inp("subln%d" % l, [1, 128]),
                  "wg": inp("wg%d" % l, [D, 3072]), "wb": inp("wb%d" % l, [3, D, D]), "wo": inp("wo%d" % l, [D, D]),
                  "g_ffn": inp("g_ffn%d" % l, [128, 8]), "convp": inp("convp%d" % l, [128, 44, 4]),
                  "wfi": inp("wfi%d" % l, [D, 2 * DFF]), "wfo": inp("wfo%d" % l, [DFF, D])})
    yT = _dram(nc, "yT", [D, SEQ], F32, "ExternalOutput")
    hnT = _dram(nc, "hnT_i", [D, LTOT], BF16, "Internal")
    if PAIR:
        brT = nc.dram_tensor("brT_sh", [4, 768, LTOT], BF16, addr_space="Shared").ap()
        brT2 = brT.rearrange("g f t -> (g f) t")
        brP = _dram(nc, "brP_i", [NGL, 768, LTOT], BF16, "Internal")
    else:
        brT = _dram(nc, "brT_i", [4, 768, LTOT], BF16, "Internal")
    hT = _dram(nc, "hT_i", [D, LTOT], F32, "Internal")
    hmidT = _dram(nc, "hmidT_i", [D, LTOT], F32, "Internal")
    hn2T = _dram(nc, "hn2T_i", [D, LTOT], BF16, "Internal")
    counts = []

    fstop = int(os.environ.get("K_FSTOP", "99"))

    def close(p):
        if len(counts) >= fstop:
            p.stack.close()
            counts.append(None)
            return
        p.finish()
        counts.append({e: len(v) for e, v in p.ops.items()})
        p.emit()
        nc.all_engine_barrier()
    p = Prog(nc)
    phase_A(p, {"xT": xT, "g": gmix[0], "hnT": hnT}, TILES_F)
    close(p)
    for l in range(DEPTH):
        last = l == DEPTH - 1
        for g in range(NGL):
            p = Prog(nc)
            iob = {"hn_full": hnT, "w": L[l]["w"][g], "rope": rope, "tabs": tabs[g], "cols": cols[g], "vcol": vcol,
                   "hg_lb": hglb[g], "da_lambda": L[l]["da_lambda"], "subln": L[l]["subln"], "ident": ident}
            iob["oT"] = brP[g] if PAIR else brT[g]
            phase_B(p, l, iob)
            close(p)
        if PAIR:
            p = Prog(nc)
            phase_X(p, brP, brT2, oidx)
            close(p)
            nc.all_core_barrier()
        p = Prog(nc)
        io = {"hT_in": xT if l == 0 else hT, "hnT_in": hnT, "brT": brT, "wg": L[l]["wg"], "wb": L[l]["wb"], "wo": L[l]["wo"],
              "g_ffn": L[l]["g_ffn"], "g_next": gfin if last else gmix[l + 1], "convp": L[l]["convp"], "wfi": L[l]["wfi"],
              "wfo": L[l]["wfo"], "hmidT": hmidT, "hn2T": hn2T, "hT_out": hT}
        if last:
            io["yT"] = yT
        else:
            io["hnT_out"] = hnT
        phase_C(p, io, last, TILES_F, ybase=128, hist_reset=(0,))
        close(p)
        if PAIR and not last:
            nc.all_core_barrier()
    print("fused program op counts per phase:", counts)
    return nc


def kernel_fused(x, meta, norm_mix_g, w_in, w_branch, w_out, hg_lb, da_lambda, da_subln_g, norm_ffn_g, w_ffn_in,
                 ffn_conv_w, ffn_conv_b, w_ffn_out, norm_final_g):
    f32 = np.float32
    A = lambda a: np.asarray(a, f32)
    x, meta, w_in, w_branch, w_out = A(x), A(meta), A(w_in), A(w_branch), A(w_out)
    w_ffn_in, w_ffn_out, ffn_conv_w, ffn_conv_b = A(w_ffn_in), A(w_ffn_out), A(ffn_conv_w), A(ffn_conv_b)
    hg_lb, da_lambda, da_subln_g = A(hg_lb), A(da_lambda), A(da_subln_g)
    rope, tabs_h, cols_h, vcol = const_tables()
    shared = {"rope": rope, "vcol": vcol, "ident": np.eye(128, dtype=f32), "g_fin": gcols(norm_final_g)}
    percore = [dict() for _ in range(2)]
    for e in range(2 if PAIR else 1):
        for j in range(NGL):
            g = NGL * e + j
            percore[e]["tabs%d" % j] = tabs_h[g]
            percore[e]["cols%d" % j] = cols_h[g]
            percore[e]["hg_lb%d" % j] = np.ascontiguousarray(hg_lb[:, g * 256:(g + 1) * 256])
            if PAIR:
                percore[e]["oidx%d" % j] = np.ascontiguousarray(
                    (g * 768 + np.arange(6)[None, :] * 128 + np.arange(128)[:, None]).astype(np.int32))
            for l in range(DEPTH):
                percore[e]["w%d_%d" % (l, j)] = w_group(w_in[l], g)
    for l in range(DEPTH):
        shared["g_mix%d" % l] = gcols(norm_mix_g[l])
        shared["da_lambda%d" % l] = np.ascontiguousarray(da_lambda[l].reshape(1, 256))
        shared["subln%d" % l] = np.ascontiguousarray(da_subln_g[l].reshape(1, 128))
        shared["wg%d" % l] = np.ascontiguousarray(w_in[l][:, 10240:13312])
        shared["wb%d" % l] = w_branch[l]
        shared["wo%d" % l] = w_out[l]
        shared["g_ffn%d" % l] = gcols(norm_ffn_g[l])
        convp = np.concatenate([ffn_conv_w[l].T, ffn_conv_b[l][:, None]], axis=1)
        shared["convp%d" % l] = np.ascontiguousarray(convp.reshape(44, 128, 4).transpose(1, 0, 2))
        shared["wfi%d" % l] = w_ffn_in[l]
        shared["wfo%d" % l] = w_ffn_out[l]
    in_maps = []
    npc = 2 if PAIR else 1
    for b in range(2):
        hfull = np.zeros((LTOT, D), f32)
        hfull[112:128] = meta
        hfull[128:] = x[b]
        xT = np.ascontiguousarray(hfull.T)
        for e in range(npc):
            m = dict(shared)
            m.update(percore[e])
            m["xT"] = xT
            in_maps.append(m)
    res = run_bass_kernel_spmd(_get("F", build_fused), in_maps, core_ids=list(range(2 * npc)))
    return np.stack([np.ascontiguousarray(np.asarray(res.results[b * npc]["yT"]).T) for b in range(2)])


kernel_unfused = kernel
if os.environ.get("K_FUSED", "1") == "1":
    kernel = kernel_fused
```

```python
from contextlib import ExitStack
import math
import numpy as np
import ml_dtypes
import concourse.bass as bass
import concourse.mybir as mybir
from concourse.bass_utils import run_bass_kernel_spmd

F32 = mybir.dt.float32
BF16 = mybir.dt.bfloat16
ALU = mybir.AluOpType
AF = mybir.ActivationFunctionType

D = 1024
SEQ = 8192
NCH = 65
LTOT = NCH * 128
NLOC = 16
HALO = 4
TLOC = 128 + HALO + NLOC * 128
DFF = 2816
EPS = 1e-6
DEPTH = 2
NBW = 2560


class Res:
    __slots__ = ("name", "lw", "rd", "excl")

    def __init__(self, name="r", excl=False):
        self.name = name
        self.lw = None
        self.rd = {}
        self.excl = excl


import os
NO_POOL = os.environ.get("NO_POOL", "1") == "1"


class Prog:
    ENG = ("pe", "act", "dve", "pool", "sp")

    _uid = [0]

    def __init__(self, nc):
        Prog._uid[0] += 1
        self.pfx = "P%d_" % Prog._uid[0]
        self.nc = nc
        self.stack = ExitStack()
        self.ops = {e: [] for e in self.ENG}
        self.cnt = {e: 0 for e in self.ENG}
        self.seen = {e: {} for e in self.ENG}
        self.dma_cnt = {}
        self.n = 0

    def sb(self, shape, dt, name=None):
        self.n += 1
        name = self.pfx + (name or ("sb%d" % self.n))
        t = self.stack.enter_context(self.nc.sbuf_tensor(name, list(shape), dt))
        return t, Res(name)

    def ps(self, shape, dt, name=None):
        self.n += 1
        name = self.pfx + (name or ("ps%d" % self.n))
        t = self.stack.enter_context(self.nc.psum_tensor(name, list(shape), dt))
        return t, Res(name, excl=True)

    def op(self, eng, fn, reads=(), writes=(), dma_key=None):
        if eng == "pool" and dma_key is None and NO_POOL:
            eng = "dve"
        if eng == "gps":
            eng = "pool"
        deps = []
        for r in reads:
            if r.lw is not None:
                deps.append(r.lw)
            if r.excl:
                deps.extend((k, v) for k, v in r.rd.items() if k != eng)
        for w in writes:
            if w.lw is not None:
                deps.append(w.lw)
            deps.extend(w.rd.items())
        if dma_key is None:
            self.cnt[eng] += 1
            tok = (eng, self.cnt[eng])
        else:
            c = self.dma_cnt.get(dma_key, 0) + 16
            self.dma_cnt[dma_key] = c
            tok = (dma_key, c)
        waits = {}
        seen = self.seen[eng]
        for (k, v) in deps:
            if eng == "pe" and k == "pe":
                continue
            if seen.get(k, 0) >= v:
                continue
            if waits.get(k, 0) < v:
                waits[k] = v
        for k, v in waits.items():
            seen[k] = v
        self.ops[eng].append((fn, waits, tok, dma_key is not None))
        for r in reads:
            if r.rd.get(tok[0], 0) < tok[1]:
                r.rd[tok[0]] = tok[1]
        for w in writes:
            w.lw = tok
            w.rd = {}
        return tok

    def wait_only(self, eng, toks):
        waits = {}
        for (k, v) in toks:
            if self.seen[eng].get(k, 0) >= v:
                continue
            waits[k] = max(waits.get(k, 0), v)
        for k, v in waits.items():
            self.seen[eng][k] = v
        self.ops[eng].append((None, waits, None, False))

    def val(self, eng, name, fn, reads, store):
        waits = {}
        for r in reads:
            if r.lw is not None and self.seen[eng].get(r.lw[0], 0) < r.lw[1]:
                waits[r.lw[0]] = r.lw[1]
        for k, v in waits.items():
            self.seen[eng][k] = v

        def run(e):
            store[name] = fn(e)
            return None
        self.ops[eng].append((run, waits, None, False))

    def dma_dyn(self, eng, apfn, reads, writes, key, **kw):
        def run(e):
            o_, i_ = apfn()
            return e.dma_start(out=o_, in_=i_, **kw)
        return self.op(eng, run, reads, writes, dma_key=key)

    def finish(self):
        self.wait_only("sp", list(self.dma_cnt.items()))

    def dma(self, eng, out, in_, reads, writes, key, **kw):
        return self.op(eng, lambda e: e.dma_start(out=out, in_=in_, **kw), reads, writes, dma_key=key)

    def mm(self, out, lhsT, rhs, start, stop, reads, writes):
        return self.op("pe", lambda e: e.matmul(out, lhsT=lhsT, rhs=rhs, start=start, stop=stop), reads, writes)

    def tr(self, out, in_, ident, reads, writes):
        return self.op("pe", lambda e: e.transpose(out, in_, ident), reads, writes)

    def act(self, out, in_, func, reads, writes, **kw):
        return self.op("act", lambda e: e.activation(out=out, in_=in_, func=func, **kw), reads, writes)

    def tt(self, eng, out, in0, in1, op, reads, writes):
        return self.op(eng, lambda e: e.tensor_tensor(out=out, in0=in0, in1=in1, op=op), reads, writes)

    def ts(self, eng, out, in0, s1, s2, op0, op1, reads, writes):
        return self.op(eng, lambda e: e.tensor_scalar(out=out, in0=in0, scalar1=s1, scalar2=s2, op0=op0, op1=op1), reads, writes)

    def stt(self, eng, out, in0, scalar, in1, op0, op1, reads, writes):
        eng = "dve"
        return self.op(eng, lambda e: e.scalar_tensor_tensor(out=out, in0=in0, scalar=scalar, in1=in1, op0=op0, op1=op1), reads, writes)

    def cp(self, eng, out, in_, reads, writes):
        if eng == "act":
            return self.op("act", lambda e: e.copy(out=out, in_=in_), reads, writes)
        return self.op(eng, lambda e: e.tensor_copy(out=out, in_=in_), reads, writes)

    def memset(self, eng, ap, val, writes):
        return self.op(eng, lambda e: e.memset(ap, val), [], writes)

    def recip(self, out, in_, reads, writes):
        return self.op("dve", lambda e: e.reciprocal(out=out, in_=in_), reads, writes)

    def emit(self):
        nc = self.nc
        sems = {}
        for e in ("pe", "act", "dve", "pool"):
            sems[e] = nc.alloc_semaphore(name=self.pfx + "s_" + e)
        for k in self.dma_cnt:
            sems[k] = nc.alloc_semaphore(name=self.pfx + "d_" + str(k))
        ops = self.ops

        def mk(name):
            def body(eng):
                for fn, waits, tok, is_dma in ops[name]:
                    for k, v in waits.items():
                        eng.wait_ge(sems[k], v)
                    if fn is None:
                        continue
                    ins = fn(eng)
                    if ins is not None and tok is not None:
                        ins.then_inc(sems[tok[0]], 16 if is_dma else 1)
            return body

        with nc.Block() as block:
            block.tensor(mk("pe"))
            block.scalar(mk("act"))
            block.vector(mk("dve"))
            block.gpsimd(mk("pool"))
            block.sync(mk("sp"))
        self.stack.close()
        nc.all_engine_barrier()
        nc.clear_and_free_semaphores(list(sems.values()))
        nc.all_engine_barrier()


def rawap(t, extra):
    return bass.AP(t.tensor, t.offset, [list(t.ap[0])] + [list(x) for x in extra])


def phase_B(p, li, io, nchunks=NCH):
    nc = p.nc
    lam_init = 0.8 - 0.6 * math.exp(-0.3 * li)
    R = Res
    ident, r_id = p.sb([128, 128], BF16, "identB")
    p.dma("pool", ident[:], io["ident"], [], [r_id], "c_id")
    W, r_W = p.sb([128, 8, NBW], BF16, "Wg")
    wv = io["w"].rearrange("(k p) n -> p k n", p=128)
    for k in range(8):
        p.dma("pool", W[:, k, :], wv[:, k, :], [], [r_W], "c_w")
    tabs, r_tabs = p.sb([128, 6, 128], F32, "tabsB")
    p.dma("sp", tabs[:], io["tabs"][0:6].rearrange("k p n -> p k n"), [], [r_tabs], "c_tabs")
    cols, r_cols = p.sb([128, 8], F32, "colsB")
    p.dma("sp", cols[:], io["cols"], [], [r_cols], "c_cols")
    vcol, r_vcol = p.sb([128, NCH], F32, "vcolB")
    p.dma("sp", vcol[:], io["vcol"], [], [r_vcol], "c_vcol")
    lbr, r_lbr = p.sb([128, DEPTH, 256], F32, "lbr")
    for d_ in range(DEPTH):
        p.dma("sp", lbr[:, d_, :], io["hg_lb"][d_:d_ + 1, :].partition_broadcast(128), [], [r_lbr], "c_lb")
    p.act(lbr[:], lbr[:], AF.Exp, [r_lbr], [r_lbr])
    lsum, r_lsum = p.sb([128, 256], F32, "lsum")
    p.tt("dve", lsum[:], lbr[:, 0, :], lbr[:, 1, :], ALU.add, [r_lbr], [r_lsum])
    p.recip(lsum[:], lsum[:], [r_lsum], [r_lsum])
    lb, r_lb = p.sb([128, 256], F32, "lb")
    oml, r_oml = p.sb([128, 256], F32, "oml")
    p.memset("dve", lb[:], 0.0, [r_lb])
    for d_ in range(li + 1):
        p.stt("dve", lb[:], lbr[:, d_, :], 1.0, lb[:], ALU.mult, ALU.add, [r_lbr, r_lb], [r_lb])
    p.stt("dve", lb[:], lbr[:, 0, :], -1.0, lb[:], ALU.mult, ALU.add, [r_lbr, r_lb], [r_lb])
    p.tt("dve", lb[:], lb[:], lsum[:], ALU.mult, [r_lb, r_lsum], [r_lb])
    p.ts("dve", oml[:], lb[:], -1.0, 1.0, ALU.mult, ALU.add, [r_lb], [r_oml])
    lp, r_lp = p.sb([128, 4, 64], F32, "lp")
    p.dma("sp", lp[:].rearrange("p a d -> p (a d)"), io["da_lambda"].partition_broadcast(128), [], [r_lp], "c_lp")
    lpp, r_lpp = p.sb([128, 2, 64], F32, "lpp")
    p.tt("dve", lpp[:, 0, :], lp[:, 0, :], lp[:, 1, :], ALU.mult, [r_lp], [r_lpp])
    p.tt("dve", lpp[:, 1, :], lp[:, 2, :], lp[:, 3, :], ALU.mult, [r_lp], [r_lpp])
    lsm, r_lsm = p.sb([128, 2], F32, "lsm")
    p.op("dve", lambda e: e.reduce_sum(out=lsm[:], in_=lpp[:], axis=mybir.AxisListType.X), [r_lpp], [r_lsm])
    p.act(lsm[:], lsm[:], AF.Exp, [r_lsm], [r_lsm])
    nlam, r_nlam = p.sb([128, 1], F32, "nlam")
    p.tt("dve", nlam[:], lsm[:, 1:2], lsm[:, 0:1], ALU.subtract, [r_lsm], [r_nlam])
    p.ts("dve", nlam[:], nlam[:], -lam_init, None, ALU.add, ALU.bypass, [r_nlam], [r_nlam])
    gsub, r_gsub = p.sb([128, 128], F32, "gsub")
    p.dma("sp", gsub[:], io["subln"].partition_broadcast(128), [], [r_gsub], "c_gs")
    p.ts("dve", gsub[:], gsub[:], 1.0 - lam_init, None, ALU.mult, ALU.bypass, [r_gsub], [r_gsub])

    KT = []
    VA = []
    KT_r = [[Res("KT%d_%d" % (h, c)) for c in range(NCH)] for h in range(2)]
    VA_r = [[Res("VA%d_%d" % (h, c)) for c in range(NCH)] for h in range(2)]
    for h in range(2):
        KT.append(p.sb([128, LTOT], BF16, "KT%d" % h))
        VA.append(p.sb([128, NCH, 130], BF16, "VA%d" % h))
        p.memset("dve", VA[h][0][:], 0.0, VA_r[h])
    S_ret, r_Sret = p.sb([128, 256], F32, "S_ret")
    Sb_ret, r_Sbret = p.sb([128, 256], BF16, "Sb_ret")
    p.memset("pool", S_ret[:], 0.0, [r_Sret])
    p.memset("pool", Sb_ret[:], 0.0, [r_Sbret])
    S_hg = []
    Sb_hg = []
    for h in range(2):
        s_, r_ = p.sb([128, 128], F32, "S_hg%d" % h)
        p.memset("pool", s_[:], 0.0, [r_])
        S_hg.append((s_, r_))
        lst = []
        for par in range(2):
            ring = []
            for j in range(4):
                sb_, rb_ = p.sb([128, 128], BF16, "Sb_hg%d_%d_%d" % (h, par, j))
                p.memset("pool", sb_[:], 0.0, [rb_])
                ring.append((sb_, rb_))
            lst.append(ring)
        Sb_hg.append(lst)
    QZ = []
    for h in range(2):
        q_, r_ = p.sb([128, 4, 128], BF16, "QZ%d" % h)
        p.memset("pool", q_[:], 0.0, [r_])
        QZ.append((q_, r_))

    hnb = [p.sb([128, 8, 512], BF16, "hnb%d" % i) for i in range(2)]
    rtb = [p.sb([128, 4, 288], F32, "rtb%d" % i) for i in range(2)]
    GP = [p.ps([128, 512], F32, "GP%d" % i) for i in range(4)]
    gp_i = [0]

    def gp():
        g = GP[gp_i[0] % 4]
        gp_i[0] += 1
        return g
    TRb, r_TRb = p.ps([128, 1024], BF16, "TRb")
    OArh, r_OArh = p.ps([128, 512], F32, "OArh")
    SUr, r_SUr = p.ps([128, 512], F32, "SUr")
    OAda, r_OAda = p.ps([128, 512], F32, "OAda")
    OAm = [(OAda, r_OAda), (SUr, r_SUr)]

    def dbl(shape, dt, name):
        return [p.sb(shape, dt, "%s_%d" % (name, i)) for i in range(2)]
    qk_t1 = dbl([128, 256], F32, "qk_t1")
    qk_t2 = dbl([128, 256], F32, "qk_t2")
    qk_r = dbl([128, 256], BF16, "qk_r")
    kd = dbl([128, 128], BF16, "kd")
    Vr = dbl([128, 256], BF16, "Vr")
    G = dbl([128, 512], F32, "G")
    sig = dbl([128, 256], F32, "sig")
    lf = dbl([128, 256], F32, "lf")
    kk = dbl([128, 256], F32, "kk")
    hq = dbl([128, 256], F32, "hq")
    hv = dbl([128, 256], BF16, "hv")
    _t1 = p.sb([128, 512], F32, "dq_t1")
    _t2 = p.sb([128, 512], F32, "dq_t2")
    dq_t1 = [_t1, _t1]
    dq_t2 = [_t2, _t2]
    dqk = dbl([128, 512], BF16, "dqk")
    rT = dbl([128, 3, 128], BF16, "rT")
    dqT = dbl([128, 2, 2, 128], BF16, "dqT")
    for i_ in range(2):
        p.memset("dve", dqT[i_][0][:], 0.0, [dqT[i_][1]])
    AT_r = dbl([128, 128], BF16, "AT_r")
    eq = dbl([128, 256], F32, "eq")
    qt = dbl([128, 256], BF16, "qt")
    kt = dbl([128, 256], BF16, "kt")
    kbar = dbl([128, 256], F32, "kbar")
    kbZ = dbl([128, 2, 4, 128], BF16, "kbZ")
    hT = dbl([128, 2, 128], BF16, "hT")
    eqT = dbl([128, 2, 128], BF16, "eqT")
    AT_h = dbl([128, 2, 128], BF16, "AT_h")
    dec = dbl([128, 2, 4], F32, "dec")
    on = dbl([128, 768], BF16, "on")
    oT_sb = dbl([128, 6, 128], BF16, "oT_sb")
    sm = dbl([128, 16], F32, "sm")
    dtmp = dbl([128, 2, 128], F32, "dtmp")
    wda = dbl([128, 128], F32, "wda")
    PT = [p.sb([128, 512], BF16, "PT%d" % i) for i in range(3)]
    pt_i = [0]
    r_oT = R("oT_dram")

    oidx = r_oidx = oT_v = None
    if "oT_sh" in io:
        oidx, r_oidx = p.sb([128, 6, NCH], mybir.dt.int32, "oidx")
        p.dma("sp", oidx[:], io["oidx"], [], [r_oidx], "c_oidx")
    else:
        oT_v = io["oT"].rearrange("(k p) t -> p k t", p=128)
    hn_full_v = hn_all_v = hn_meta_v = None
    if "hn_full" in io:
        hn_full_v = io["hn_full"].rearrange("(k p) t -> p k t", p=128)
    else:
        hn_all_v = io["hn_all"].rearrange("r (k p) t -> r p k t", p=128)
        hn_meta_v = io["hn_meta"].rearrange("(k p) t -> p k t", p=128)
    rope_v = io["rope"].rearrange("(c p) n -> p c n", p=128)

    def rope_ops(b, src_ps, r_src, t1, t2, dst, ngrp, half, cos_ap, sin_ap, nsin_ap, r_tab):
        (t1a, r_t1), (t2a, r_t2), (da_, r_d) = t1, t2, dst
        w = ngrp * 2 * half
        p.tt("dve", t1a[:, :w].rearrange("p (g h) -> p g h", h=half), src_ps.rearrange("p (g h) -> p g h", h=half),
             cos_ap.unsqueeze(1).to_broadcast([128, ngrp * 2, half]), ALU.mult, [r_src, r_tab], [r_t1])
        sv = src_ps.rearrange("p (g two h) -> p g two h", two=2, h=half)
        t2v = t2a[:, :w].rearrange("p (g two h) -> p g two h", two=2, h=half)
        p.tt("dve", t2v[:, :, 0, :], sv[:, :, 1, :], nsin_ap.unsqueeze(1).to_broadcast([128, ngrp, half]), ALU.mult,
             [r_src, r_tab], [r_t2])
        p.tt("dve", t2v[:, :, 1, :], sv[:, :, 0, :], sin_ap.unsqueeze(1).to_broadcast([128, ngrp, half]), ALU.mult,
             [r_src, r_tab], [r_t2])
        p.tt("pool", da_[:, :w], t1a[:, :w], t2a[:, :w], ALU.add, [r_t1, r_t2], [r_d])

    import os
    KSTOP = int(os.environ.get("KSTOP", "99"))

    def bail():
        if oT_v is not None:
            p.dma("sp", oT_v[:, 0:1, 0:128], ident[:].unsqueeze(1), [r_id], [r_oT], "st_o0")
        return r_oT
    if KSTOP == 0:
        return bail()
    ngroups = 1 + (nchunks - 1 + 3) // 4
    chunk_list = []
    for gi in range(ngroups):
        if gi == 0:
            clist = [0]
        else:
            c0 = 1 + (gi - 1) * 4
            clist = [c for c in range(c0, min(c0 + 4, nchunks))]
        for ji, c in enumerate(clist):
            chunk_list.append((gi, ji, c, clist))

    def group_load(gi, clist):
        hb, r_hb = hnb[gi % 2]
        rt, r_rt = rtb[gi % 2]
        if gi == 0:
            p.dma("sp", hb[:, :, 0:128], hn_full_v[:, :, 0:128] if hn_full_v is not None else hn_meta_v, [], [r_hb], "hn%d" % (gi % 2))
            p.dma("sp", rt[:, 0:1, :], rope_v[:, 0:1, :], [], [r_rt], "rt%d" % (gi % 2))
        else:
            c0 = clist[0]
            rk = (gi - 1) // 4
            lo = ((gi - 1) % 4) * 512
            nt = len(clist) * 128
            p.dma("sp", hb[:, :, 0:nt], hn_full_v[:, :, c0 * 128:c0 * 128 + nt] if hn_full_v is not None else hn_all_v[rk, :, :, lo:lo + nt],
                  [], [r_hb], "hn%d" % (gi % 2))
            p.dma("sp", rt[:, 0:len(clist), :], rope_v[:, c0:c0 + len(clist), :], [], [r_rt], "rt%d" % (gi % 2))

    def front_fns(gi, ji, c):
        b = c % 2
        hb, r_hb = hnb[gi % 2]
        rt, r_rt = rtb[gi % 2]
        hn_c = hb[:, :, ji * 128:(ji + 1) * 128]
        cosR, sinR, nsinR = rt[:, ji, 0:64], rt[:, ji, 64:128], rt[:, ji, 128:192]

        def proj(ti):
            g, r_g = gp()
            for k in range(8):
                p.mm(g[:, :], hn_c[:, k, :], W[:, k, ti * 512:(ti + 1) * 512], k == 0, k == 7, [r_hb, r_W], [r_g])
            return g, r_g

        def f1():
            p.memset("pool", sm[b][0][:], 0.0, [sm[b][1]])
            g0, r_g0 = proj(0)
            rope_ops(b, g0[:, 0:256], r_g0, qk_t1[b], qk_t2[b], qk_r[b], 2, 64, cosR, sinR, nsinR, r_rt)
            p.act(Vr[b][0][:], g0[:, 256:512], AF.Identity, [r_g0], [Vr[b][1]])
            p.ts("pool", kd[b][0][:], qk_r[b][0][:, 128:256], cols[:, 0:1], 0.0, ALU.mult, ALU.add, [qk_r[b][1], r_cols], [kd[b][1]])
            g1, r_g1 = proj(1)
            p.act(G[b][0][:], g1[:, :], AF.Silu, [r_g1], [G[b][1]])

        def f2():
            g2, r_g2 = proj(2)
            p.act(sig[b][0][:], g2[:, 256:512], AF.Sigmoid, [r_g2], [sig[b][1]])
            p.act(hq[b][0][:], g2[:, 0:256], AF.Identity, [r_g2], [hq[b][1]])
            p.tt("dve", sig[b][0][:], sig[b][0][:], oml[:], ALU.mult, [sig[b][1], r_oml], [sig[b][1]])
            p.tt("dve", sig[b][0][:], sig[b][0][:], lb[:], ALU.add, [sig[b][1], r_lb], [sig[b][1]])
            p.act(lf[b][0][:], sig[b][0][:], AF.Ln, [sig[b][1]], [lf[b][1]])
            p.ts("pool", kk[b][0][:], sig[b][0][:], -1.0, 1.0, ALU.mult, ALU.add, [sig[b][1]], [kk[b][1]])

        def f3():
            g3, r_g3 = proj(3)
            p.act(hv[b][0][:], g3[:, 0:256], AF.Identity, [r_g3], [hv[b][1]])
            for h in range(2):
                p.act(VA[h][0][:, c, 0:128], g3[:, 256 + h * 128:256 + (h + 1) * 128], AF.Identity, [r_g3], [VA_r[h][c]])
                p.cp("pool", VA[h][0][:, c, 128:129], vcol[:, c:c + 1], [r_vcol], [VA_r[h][c]])
            g4, r_g4 = proj(4)
            rope_ops(b, g4[:, 0:512], r_g4, dq_t1[b], dq_t2[b], dqk[b], 8, 32, rt[:, ji, 192:224], rt[:, ji, 224:256],
                     rt[:, ji, 256:288], r_rt)

        def f4():
            for i in range(2):
                p.tr(TRb[:, i * 128:(i + 1) * 128], qk_r[b][0][:, i * 128:(i + 1) * 128], ident[:], [qk_r[b][1], r_id], [r_TRb])
            for i in range(4):
                p.tr(TRb[:, (2 + i) * 128:(3 + i) * 128], dqk[b][0][:, i * 128:(i + 1) * 128], ident[:], [dqk[b][1], r_id], [r_TRb])
            p.cp("dve", rT[b][0][:, 0, :], TRb[:, 0:128], [r_TRb], [rT[b][1]])
            p.tt("dve", rT[b][0][:, 1, :], TRb[:, 0:128], tabs[:, 1, :], ALU.mult, [r_TRb, r_tabs], [rT[b][1]])
            p.cp("dve", rT[b][0][:, 2, :], TRb[:, 128:256], [r_TRb], [rT[b][1]])
            for h in range(2):
                p.act(dqT[b][0][0:64, h, 0, :], TRb[0:64, (2 + h) * 128:(3 + h) * 128], AF.Identity, [r_TRb], [dqT[b][1]])
                p.act(dqT[b][0][64:128, h, 1, :], TRb[64:128, (2 + h) * 128:(3 + h) * 128], AF.Identity, [r_TRb], [dqT[b][1]])
            for h in range(2):
                p.act(KT[h][0][:, c * 128:(c + 1) * 128], TRb[:, (4 + h) * 128:(5 + h) * 128], AF.Identity, [r_TRb], [KT_r[h][c]])
        return [f1, f2, f3, f4]

    group_load(0, chunk_list[0][3])
    for f_ in front_fns(*chunk_list[0][:3]):
        f_()
    for ci_, (gi, ji, c, clist) in enumerate(chunk_list):
            b = c % 2
            steps = []
            def ret_step(b=b, c=c):
                gs, r_gs = gp()
                p.mm(gs[:, 0:128], rT[b][0][:, 2, :], rT[b][0][:, 0, :], True, True, [rT[b][1]], [r_gs])
                p.tt("dve", AT_r[b][0][:], gs[:, 0:128], tabs[:, 0, :], ALU.mult, [r_gs, r_tabs], [AT_r[b][1]])
                p.mm(OArh[:, 0:256], AT_r[b][0][:], Vr[b][0][:], True, False, [AT_r[b][1], Vr[b][1]], [r_OArh])
                p.mm(OArh[:, 0:256], rT[b][0][:, 1, :], Sb_ret[:], False, True, [rT[b][1], r_Sbret], [r_OArh])
                gsu, r_gsu = gp()
                p.mm(gsu[:, 0:256], kd[b][0][:], Vr[b][0][:], True, True, [kd[b][1], Vr[b][1]], [r_gsu])
                p.stt("dve", S_ret[:], S_ret[:], cols[:, 1:2], gsu[:, 0:256], ALU.mult, ALU.add, [r_Sret, r_cols, r_gsu], [r_Sret])
                p.cp("pool", Sb_ret[:], S_ret[:], [r_Sret], [r_Sbret])
                p.act(dtmp[b][0][:].rearrange("p a n -> p (a n)"), OArh[:, 0:256], AF.Square, [r_OArh], [dtmp[b][1], sm[b][1]],
                      accum_out=sm[b][0][:, 0:1])
                p.act(sm[b][0][:, 0:1], sm[b][0][:, 0:1], AF.Sqrt, [sm[b][1]], [sm[b][1]], bias=EPS, scale=1.0 / 256)
                p.recip(sm[b][0][:, 0:1], sm[b][0][:, 0:1], [sm[b][1]], [sm[b][1]])
                p.stt("dve", on[b][0][:, 0:256], OArh[:, 0:256], sm[b][0][:, 0:1], G[b][0][:, 0:256], ALU.mult, ALU.mult,
                      [r_OArh, sm[b][1], G[b][1]], [on[b][1]])
            steps.append(ret_step)

            def hg_prep(b=b, c=c):
                gc, r_gc = gp()
                p.mm(gc[:, 0:256], tabs[:, 2, :], lf[b][0][:], True, True, [r_tabs, lf[b][1]], [r_gc])
                p.mm(gc[:, 256:512], tabs[:, 3, :], lf[b][0][:], True, True, [r_tabs, lf[b][1]], [r_gc])
                p.act(eq[b][0][:], gc[:, 0:256], AF.Exp, [r_gc], [eq[b][1]])
                p.tt("dve", qt[b][0][:], hq[b][0][:], eq[b][0][:], ALU.mult, [hq[b][1], eq[b][1]], [qt[b][1]])
                p.act(eq[b][0][:], gc[:, 0:256], AF.Exp, [r_gc, qt[b][1]], [eq[b][1]], scale=-1.0)
                p.tt("pool", kt[b][0][:], kk[b][0][:], eq[b][0][:], ALU.mult, [kk[b][1], eq[b][1]], [kt[b][1]])
                p.act(kbar[b][0][:], gc[:, 256:512], AF.Exp, [r_gc], [kbar[b][1]])
                p.tt("pool", kbar[b][0][:], kbar[b][0][:], kk[b][0][:], ALU.mult, [kbar[b][1], kk[b][1]], [kbar[b][1]])
                for h in range(2):
                    p.tt("pool", kbZ[b][0][:, h, :, :], kbar[b][0][:, h * 128:(h + 1) * 128].unsqueeze(1).to_broadcast([128, 4, 128]),
                         cols[:, 2:6].unsqueeze(2).to_broadcast([128, 4, 128]), ALU.mult, [kbar[b][1], r_cols], [kbZ[b][1]])
                gd, r_gd = gp()
                for h in range(2):
                    p.mm(gd[:, h * 4:(h + 1) * 4], lf[b][0][:, h * 128:(h + 1) * 128], cols[:, 2:6], True, True, [lf[b][1], r_cols], [r_gd])
                p.act(dec[b][0][:].rearrange("p h j -> p (h j)"), gd[:, 0:8], AF.Exp, [r_gd], [dec[b][1]])
                for h in range(2):
                    p.tr(TRb[:, h * 128:(h + 1) * 128], qt[b][0][:, h * 128:(h + 1) * 128], ident[:], [qt[b][1], r_id], [r_TRb])
                    p.tr(TRb[:, (2 + h) * 128:(3 + h) * 128], kt[b][0][:, h * 128:(h + 1) * 128], ident[:], [kt[b][1], r_id], [r_TRb])
                p.cp("dve", hT[b][0][:].rearrange("p h t -> p (h t)"), TRb[:, 256:512], [r_TRb], [hT[b][1]])
                p.cp("dve", eqT[b][0][:].rearrange("p h t -> p (h t)"), TRb[:, 0:256], [r_TRb], [eqT[b][1]])
                for h in range(2):
                    qzf = QZ[h][0][:].rearrange("p j t -> p (j t)")
                    p.cp("pool", rawap(qzf, [[160, 4], [1, 32]]), eqT[b][0][:, h, :].rearrange("p (j i) -> p j i", i=32),
                         [eqT[b][1]], [QZ[h][1]])
            steps.append(hg_prep)

            def hg_head(h, b=b, c=c):
                oc = slice(256 + h * 128, 256 + (h + 1) * 128)
                st = {}

                def fa():
                    gs, r_gs = gp()
                    p.mm(gs[:, 0:128], hT[b][0][:, h, :], eqT[b][0][:, h, :], True, True, [hT[b][1], eqT[b][1]], [r_gs])
                    p.tt("dve", AT_h[b][0][:, h, :], gs[:, 0:128], tabs[:, 4, :], ALU.mult, [r_gs, r_tabs], [AT_h[b][1]])
                    gu, r_gu = gp()
                    for j in range(4):
                        p.mm(gu[:, j * 128:(j + 1) * 128], kbZ[b][0][:, h, j, :], hv[b][0][:, h * 128:(h + 1) * 128], True, True,
                             [kbZ[b][1], hv[b][1]], [r_gu])
                    S_, r_S = S_hg[h]
                    for j in range(4):
                        p.stt("dve", S_[:], S_[:], dec[b][0][:, h, j:j + 1], gu[:, j * 128:(j + 1) * 128], ALU.mult, ALU.add,
                              [r_S, dec[b][1], r_gu], [r_S])
                        sbn, r_sbn = Sb_hg[h][b][j + 1] if j < 3 else Sb_hg[h][1 - b][0]
                        p.cp("pool", sbn[:], S_[:], [r_S], [r_sbn])

                def fb():
                    p.mm(OArh[:, oc], AT_h[b][0][:, h, :], hv[b][0][:, h * 128:(h + 1) * 128], True, False, [AT_h[b][1], hv[b][1]], [r_OArh])
                    for j in range(4):
                        sbj, r_sbj = Sb_hg[h][b][j]
                        p.mm(OArh[:, oc], QZ[h][0][:, j, :], sbj[:], False, j == 3, [QZ[h][1], r_sbj], [r_OArh])
                    k0 = 1 + h
                    p.act(dtmp[b][0][:, 0, :], OArh[:, oc], AF.Square, [r_OArh], [dtmp[b][1], sm[b][1]], accum_out=sm[b][0][:, k0:k0 + 1])
                    p.act(sm[b][0][:, k0:k0 + 1], sm[b][0][:, k0:k0 + 1], AF.Sqrt, [sm[b][1]], [sm[b][1]], bias=EPS, scale=1.0 / 128)
                    p.recip(sm[b][0][:, k0:k0 + 1], sm[b][0][:, k0:k0 + 1], [sm[b][1]], [sm[b][1]])
                    p.stt("dve", on[b][0][:, oc], OArh[:, oc], sm[b][0][:, k0:k0 + 1], G[b][0][:, oc], ALU.mult, ALU.mult,
                          [r_OArh, sm[b][1], G[b][1]], [on[b][1]])
                return fa, fb
            hh0, hh1 = hg_head(0), hg_head(1)
            steps += [hh0[0], hh1[0], hh0[1], hh1[1]]

            def da_head(h, b=b, c=c):
                units = [(j, m) for j in range(c + 1) for m in range(2)]
                batches = [units[i:i + 4] for i in range(0, len(units), 4)]
                out = []

                def mk_batch(bi, bat):
                    st = {}

                    def fa():
                        gs, r_gs = gp()
                        for ui, (j, m) in enumerate(bat):
                            p.mm(gs[:, ui * 128:(ui + 1) * 128], KT[h][0][:, j * 128:(j + 1) * 128],
                                 dqT[b][0][:, h, m, :], True, True, [KT_r[h][j], dqT[b][1]], [r_gs])
                        pt, r_pt = PT[pt_i[0] % 3]
                        pt_i[0] += 1
                        st["pt"] = (pt, r_pt)
                        n = len(bat) * 128
                        p.act(pt[:, 0:n], gs[:, 0:n], AF.Exp, [r_gs], [r_pt], scale=0.125)
                        for ui, (j, m) in enumerate(bat):
                            if j == c:
                                p.tt("pool", pt[:, ui * 128:(ui + 1) * 128], pt[:, ui * 128:(ui + 1) * 128], tabs[:, 5, :], ALU.mult,
                                     [r_pt, r_tabs], [r_pt])

                    def fb():
                        pt, r_pt = st["pt"]
                        for ui, (j, m) in enumerate(bat):
                            p.mm(OAm[m][0][:, 0:130], pt[:, ui * 128:(ui + 1) * 128], VA[h][0][:, j, 0:130],
                                 j == 0, j == c, [r_pt, VA_r[h][j]], [OAm[m][1]])
                    return fa, fb
                pairs = [mk_batch(bi, bat) for bi, bat in enumerate(batches)]
                out.append(pairs[0][0])
                if len(pairs) > 1:
                    out.append(pairs[1][0])
                for bi in range(len(pairs)):
                    if bi + 2 < len(pairs):
                        out.append(pairs[bi + 2][0])
                    out.append(pairs[bi][1])

                def epi():
                    s0 = 4 + 4 * h
                    smt, r_sm = sm[b]
                    for m in range(2):
                        p.ts("dve", smt[:, s0 + m:s0 + m + 1], OAm[m][0][:, 128:129], 1e-30, None, ALU.max, ALU.bypass,
                             [OAm[m][1]], [r_sm])
                        p.recip(smt[:, s0 + m:s0 + m + 1], smt[:, s0 + m:s0 + m + 1], [r_sm], [r_sm])
                        p.act(dtmp[b][0][:, m, :], OAm[m][0][:, 0:128], AF.Identity, [OAm[m][1], r_sm], [dtmp[b][1]],
                              scale=smt[:, s0 + m:s0 + m + 1])
                    p.stt("dve", wda[b][0][:], dtmp[b][0][:, 1, :], nlam[:, 0:1], dtmp[b][0][:, 0, :], ALU.mult, ALU.add,
                          [dtmp[b][1], r_nlam], [wda[b][1]])
                    p.act(dtmp[b][0][:, 0, :], wda[b][0][:], AF.Square, [wda[b][1]], [dtmp[b][1], r_sm], accum_out=smt[:, s0 + 2:s0 + 3])
                    p.act(smt[:, s0 + 2:s0 + 3], smt[:, s0 + 2:s0 + 3], AF.Sqrt, [r_sm], [r_sm], bias=EPS, scale=1.0 / 128)
                    p.recip(smt[:, s0 + 2:s0 + 3], smt[:, s0 + 2:s0 + 3], [r_sm], [r_sm])
                    p.stt("dve", on[b][0][:, 512 + h * 128:512 + (h + 1) * 128], wda[b][0][:], smt[:, s0 + 2:s0 + 3], gsub[:],
                          ALU.mult, ALU.mult, [wda[b][1], r_sm, r_gsub], [on[b][1]])
                out.append(epi)
                return out

            def finish(b=b, c=c):
                def f():
                    for i in range(6):
                        p.tr(TRb[:, i * 128:(i + 1) * 128], on[b][0][:, i * 128:(i + 1) * 128], ident[:], [on[b][1], r_id], [r_TRb])
                    p.act(oT_sb[b][0][:].rearrange("p k t -> p (k t)"), TRb[:, 0:768], AF.Identity, [r_TRb], [oT_sb[b][1]])
                    if oidx is None:
                        p.dma("sp", oT_v[:, :, c * 128:(c + 1) * 128], oT_sb[b][0][:], [oT_sb[b][1]], [r_oT], "st_o%d" % b)
                    else:
                        for k6 in range(6):
                            p.op("pool", lambda e, k6=k6: e.indirect_dma_start(
                                out=io["oT_sh"],
                                out_offset=bass.IndirectOffsetOnAxis(ap=oidx[:, k6, c:c + 1], axis=0),
                                in_=oT_sb[b][0][:, k6, :], in_offset=None, bounds_check=4 * 768 * NCH - 1, oob_is_err=False),
                                [oT_sb[b][1], r_oidx], [r_oT], dma_key="st_o%d" % b)
                return f

            da_list = da_head(0) + da_head(1) + [finish()]
            fr = []
            if ci_ + 1 < len(chunk_list):
                ngi, nji, ncc, nclist = chunk_list[ci_ + 1]
                if nji == 0:
                    fr.append(lambda ngi=ngi, nclist=nclist: group_load(ngi, nclist))
                fr += front_fns(ngi, nji, ncc)
            allsteps = []
            for i_ in range(max(len(steps), len(fr))):
                if i_ < len(steps):
                    allsteps.append(steps[i_])
                if i_ < len(fr):
                    allsteps.append(fr[i_])
            ns, nd = len(allsteps), len(da_list)
            si = 0
            for di, dfn in enumerate(da_list[:-1]):
                while si < ns and si * (nd - 1) <= di * ns:
                    allsteps[si]()
                    si += 1
                dfn()
            while si < ns:
                allsteps[si]()
                si += 1
            da_list[-1]()
    return r_oT


def rms_tile(p, h_t, r_h, n, gcol, r_g, out_t, r_out, ones, r_ones, sq, r_sq, rstd, r_rstd, ps, r_ps):
    p.act(sq[:, :, :n], h_t[:, :, :n], AF.Square, [r_h], [r_sq])
    for k in range(8):
        p.mm(ps[:, :n], ones[:], sq[:, k, :n], k == 0, k == 7, [r_ones, r_sq], [r_ps])
    p.act(rstd[:, :n], ps[:, :n], AF.Sqrt, [r_ps], [r_rstd], bias=EPS, scale=1.0 / D)
    p.recip(rstd[:, :n], rstd[:, :n], [r_rstd], [r_rstd])
    for k in range(8):
        p.stt("dve" if k % 2 == 0 else "pool", out_t[:, k, :n], h_t[:, k, :n], gcol[:, k:k + 1], rstd[:, :n], ALU.mult, ALU.mult,
              [r_h, r_g, r_rstd], [r_out])


TILES = [(0, 128)] + [(128 + i * 342, 342) for i in range(6)]
NT = 342


def phase_A(p, io, tiles=None):
    tiles = tiles or TILES
    ones, r_ones = p.sb([128, 128], BF16, "onesA")
    p.memset("pool", ones[:], 1.0, [r_ones])
    gcol, r_g = p.sb([128, 8], F32, "gcolA")
    p.dma("sp", gcol[:], io["g"], [], [r_g], "c_g")
    xv = io["xT"].rearrange("(k p) t -> p k t", p=128)
    ov = io["hnT"].rearrange("(k p) t -> p k t", p=128)
    r_o = Res("hnT_d")
    bufs = []
    for i in range(2):
        bufs.append((p.sb([128, 8, NT], F32, "hA%d" % i), p.sb([128, 8, NT], BF16, "oA%d" % i), p.sb([128, 8, NT], BF16, "sqA%d" % i),
                     p.sb([128, NT], F32, "rsA%d" % i), p.ps([128, 512], F32, "psA%d" % i)))
    for ti, (t0, n) in enumerate(tiles):
        (h_t, r_h), (o_t, r_ot), (sq, r_sq), (rs, r_rs), (ps, r_ps) = bufs[ti % 2]
        p.dma("sp", h_t[:, :, :n], xv[:, :, t0:t0 + n], [], [r_h], "ldA%d" % (ti % 2))
        rms_tile(p, h_t, r_h, n, gcol, r_g, o_t, r_ot, ones, r_ones, sq, r_sq, rs, r_rs, ps, r_ps)
        p.dma("sp", ov[:, :, t0:t0 + n], o_t[:, :, :n], [r_ot], [r_o], "stA%d" % (ti % 2))
    return [r_o]


def phase_C(p, io, last, tiles=None, ybase=132, hist_reset=(0, 1)):
    nc = p.nc
    tiles = tiles or TILES
    WB, r_WB = p.sb([128, 67584], BF16, "WBUF")
    ones, r_ones = p.sb([128, 128], BF16, "onesC")
    p.memset("pool", ones[:], 1.0, [r_ones])
    gc2, r_gc2 = p.sb([128, 8], F32, "gffn")
    gc3, r_gc3 = p.sb([128, 8], F32, "gnext")
    p.dma("sp", gc2[:], io["g_ffn"], [], [r_gc2], "c_g2")
    p.dma("sp", gc3[:], io["g_next"], [], [r_gc3], "c_g3")
    cw, r_cw = p.sb([128, 44, 4], F32, "convw")
    p.dma("sp", cw[:], io["convp"], [], [r_cw], "c_cw")
    wg = WB[:, 0:24576].rearrange("p (k n) -> p k n", k=8)
    wb = WB[:, 24576:49152].rearrange("p (b k n) -> p b k n", b=3, k=8)
    wo = WB[:, 49152:57344].rearrange("p (k n) -> p k n", k=8)
    wgv = io["wg"].rearrange("(k p) n -> p k n", p=128)
    wbv = io["wb"].rearrange("b (k p) n -> p b k n", p=128)
    wov = io["wo"].rearrange("(k p) n -> p k n", p=128)
    for k in range(8):
        p.dma("pool", wg[:, k, :], wgv[:, k, :], [], [r_WB], "c_W")
    for b_ in range(3):
        for k in range(8):
            p.dma("pool", wb[:, b_, k, :], wbv[:, b_, k, :], [], [r_WB], "c_W")
    for k in range(8):
        p.dma("pool", wo[:, k, :], wov[:, k, :], [], [r_WB], "c_W")

    h_t, r_h = p.sb([128, 8, NT], F32, "hC")
    hn_t, r_hn = p.sb([128, 8, NT], BF16, "hnC")
    big, r_big = p.sb([128, 24 * NT], BF16, "bigC")
    y_t, r_y = p.sb([128, 8, NT], BF16, "yC")
    gt = [p.sb([128, NT], F32, "gt%d" % i) for i in range(2)]
    yacc, r_yacc = p.sb([128, NT], F32, "yacc")
    tmp = [p.sb([128, NT], F32, "tmpC%d" % i) for i in range(2)]
    sq, r_sq = p.sb([128, 8, NT], BF16, "sqC")
    rstd, r_rstd = p.sb([128, NT], F32, "rstdC")
    GPc = [p.ps([128, 512], F32, "GPc%d" % i) for i in range(7)]
    gi_ = [0]

    def gp():
        g = GPc[gi_[0] % 7]
        gi_[0] += 1
        return g
    hv_in = io["hT_in"].rearrange("(k p) t -> p k t", p=128)
    hnv_in = io["hnT_in"].rearrange("(k p) t -> p k t", p=128)
    brv = io["brT"].rearrange("g (k p) t -> p g k t", p=128)
    hmid_v = io["hmidT"].rearrange("(k p) t -> p k t", p=128)
    hn2_v = io["hn2T"].rearrange("(k p) t -> p k t", p=128)
    r_hmid, r_hn2d = Res("hmid_d"), Res("hn2_d")

    for ti, (t0, n) in enumerate(tiles):
        if int(os.environ.get("KC_STOP", "99")) == 0 and ti == 1:
            return [r_hmid, r_hn2d]
        br_t = big[:, 0:24 * n].rearrange("p (g k t) -> p g k t", g=4, k=6)
        p.dma("sp", h_t[:, :, :n], hv_in[:, :, t0:t0 + n], [], [r_h], "ldh")
        p.dma("sp", hn_t[:, :, :n], hnv_in[:, :, t0:t0 + n], [], [r_hn], "ldhn")
        for g in range(4):
            p.dma("sp", br_t[:, g, :, :], brv[:, g, :, t0:t0 + n], [], [r_big], "ldbr")
        for m in range(8):
            ms = slice(m * 128, (m + 1) * 128)
            for nb in range(3):
                gps, r_gps = gp()
                for k in range(8):
                    p.mm(gps[:, :n], wg[:, k, nb * 1024 + m * 128:nb * 1024 + (m + 1) * 128], hn_t[:, k, :n], k == 0, k == 7,
                         [r_WB, r_hn], [r_gps])
                g_, r_g_ = gt[nb % 2]
                p.act(g_[:, :n], gps[:, :n], AF.Sigmoid, [r_gps], [r_g_])
                bps, r_bps = gp()
                for k in range(8):
                    p.mm(bps[:, :n], wb[:, nb, k, ms], br_t[:, k // 2, nb * 2 + k % 2, :], k == 0, k == 7, [r_WB, r_big], [r_bps])
                if nb == 0:
                    p.tt("dve", yacc[:, :n], g_[:, :n], bps[:, :n], ALU.mult, [r_g_, r_bps], [r_yacc])
                else:
                    t_, r_t = tmp[nb % 2]
                    p.tt("dve", t_[:, :n], g_[:, :n], bps[:, :n], ALU.mult, [r_g_, r_bps], [r_t])
                    if nb == 1:
                        p.tt("gps", yacc[:, :n], yacc[:, :n], t_[:, :n], ALU.add, [r_yacc, r_t], [r_yacc])
                    else:
                        p.tt("gps", y_t[:, m, :n], yacc[:, :n], t_[:, :n], ALU.add, [r_yacc, r_t], [r_y])
        for m in range(8):
            ops_, r_ops = gp()
            for k in range(8):
                p.mm(ops_[:, :n], wo[:, k, m * 128:(m + 1) * 128], y_t[:, k, :n], k == 0, k == 7, [r_WB, r_y], [r_ops])
            p.tt("dve", h_t[:, m, :n], h_t[:, m, :n], ops_[:, :n], ALU.add, [r_h, r_ops], [r_h])
        if ti == 0:
            p.memset("pool", h_t[:, :, 0:112], 0.0, [r_h])
        p.dma("sp", hmid_v[:, :, t0:t0 + n], h_t[:, :, :n], [r_h], [r_hmid], "sth")
        ps, r_ps = gp()
        rms_tile(p, h_t, r_h, n, gc2, r_gc2, hn_t, r_hn, ones, r_ones, sq, r_sq, rstd, r_rstd, ps, r_ps)
        p.dma("sp", hn2_v[:, :, t0:t0 + n], hn_t[:, :, :n], [r_hn], [r_hn2d], "sthn")

    KC = int(os.environ.get("KC_STOP", "99"))
    if KC == 1:
        return [r_hmid, r_hn2d]
    wfi = WB[:, 0:45056].rearrange("p (k n) -> p k n", k=8)
    wfo = WB[:, 45056:67584].rearrange("p (k n) -> p k n", k=22)
    wfiv = io["wfi"].rearrange("(k p) n -> p k n", p=128)
    wfov = io["wfo"].rearrange("(k p) n -> p k n", p=128)
    for k in range(8):
        p.dma("pool", wfi[:, k, :], wfiv[:, k, :], [], [r_WB], "c_W")
    for k in range(22):
        p.dma("pool", wfo[:, k, :], wfov[:, k, :], [], [r_WB], "c_W")
    U = [p.sb([128, NT + 2], F32, "U%d" % i) for i in range(2)]
    cb = [p.sb([128, NT], F32, "cb%d" % i) for i in range(2)]
    Hh, r_Hh = p.sb([128, 44, 2], F32, "Hh")
    sg, r_sg = p.sb([128, NT], F32, "sgC")
    r_hout, r_out2 = Res("hout_d"), Res("out2_d")
    hout_v = io["hT_out"].rearrange("(k p) t -> p k t", p=128)
    if last:
        yv = io["yT"].rearrange("(k p) t -> p k t", p=128)
        yo, r_yo = p.sb([128, 8, NT], F32, "yo")
    else:
        hnout_v = io["hnT_out"].rearrange("(k p) t -> p k t", p=128)
    for ti, (t0, n) in enumerate(tiles):
        a_t = big[:, 0:22 * n].rearrange("p (k t) -> p k t", k=22)
        if ti in hist_reset:
            p.memset("pool", Hh[:], 0.0, [r_Hh])
        p.dma("sp", h_t[:, :, :n], hmid_v[:, :, t0:t0 + n], [r_hmid], [r_h], "ldh")
        p.dma("sp", hn_t[:, :, :n], hn2_v[:, :, t0:t0 + n], [r_hn2d], [r_hn], "ldhn")
        for i in range(22):
            for wi, ci in enumerate((i, 22 + i)):
                ups, r_ups = gp()
                for k in range(8):
                    p.mm(ups[:, :n], wfi[:, k, ci * 128:(ci + 1) * 128], hn_t[:, k, :n], k == 0, k == 7, [r_WB, r_hn], [r_ups])
                u_, r_u = U[wi]
                c_, r_c = cb[wi]
                p.act(u_[:, 2:2 + n], ups[:, :n], AF.Identity, [r_ups], [r_u])
                p.act(c_[:, :n], ups[:, :n], AF.Identity, [r_ups, r_cw], [r_c], scale=cw[:, ci, 2:3], bias=cw[:, ci, 3:4])
                p.cp("gps", u_[:, 0:2], Hh[:, ci, :], [r_Hh], [r_u])
                p.stt("dve", c_[:, :n], u_[:, 1:1 + n], cw[:, ci, 1:2], c_[:, :n], ALU.mult, ALU.add, [r_u, r_cw, r_c], [r_c])
                p.stt("dve", c_[:, :n], u_[:, 0:n], cw[:, ci, 0:1], c_[:, :n], ALU.mult, ALU.add, [r_u, r_cw, r_c], [r_c])
                p.cp("gps", Hh[:, ci, :], u_[:, n:n + 2], [r_u], [r_Hh])
            p.act(sg[:, :n], cb[0][0][:, :n], AF.Silu, [cb[0][1]], [r_sg])
            p.tt("gps", a_t[:, i, :], sg[:, :n], cb[1][0][:, :n], ALU.mult, [r_sg, cb[1][1]], [r_big])
        for m in range(8):
            fps, r_fps = gp()
            for k in range(22):
                p.mm(fps[:, :n], wfo[:, k, m * 128:(m + 1) * 128], a_t[:, k, :], k == 0, k == 21, [r_WB, r_big], [r_fps])
            p.tt("dve", h_t[:, m, :n], h_t[:, m, :n], fps[:, :n], ALU.add, [r_h, r_fps], [r_h])
        if ti == 0:
            p.memset("pool", h_t[:, :, 0:112], 0.0, [r_h])
        p.dma("sp", hout_v[:, :, t0:t0 + n], h_t[:, :, :n], [r_h], [r_hout], "sth")
        ps, r_ps = gp()
        if last:
            lo = max(0, ybase - t0)
            if lo < n:
                rms_tile(p, h_t, r_h, n, gc3, r_gc3, yo, r_yo, ones, r_ones, sq, r_sq, rstd, r_rstd, ps, r_ps)
                g0 = t0 + lo - ybase
                p.dma("sp", yv[:, :, g0:g0 + n - lo], yo[:, :, lo:n], [r_yo], [r_out2], "sty")
        else:
            rms_tile(p, h_t, r_h, n, gc3, r_gc3, hn_t, r_hn, ones, r_ones, sq, r_sq, rstd, r_rstd, ps, r_ps)
            p.dma("sp", hnout_v[:, :, t0:t0 + n], hn_t[:, :, :n], [r_hn], [r_out2], "sthn")
    return [r_hout, r_out2, r_hmid, r_hn2d]


def _dram(nc, name, shape, dt, kind):
    return nc.dram_tensor(name, list(shape), dt, kind=kind).ap()


def build_A():
    nc = bass.Bass("TRN2", target_bir_lowering=False)
    io = {"xT": _dram(nc, "xT", [D, TLOC], F32, "ExternalInput"), "g": _dram(nc, "g", [128, 8], F32, "ExternalInput"),
          "hnT": _dram(nc, "hnT", [D, TLOC], BF16, "ExternalOutput")}
    p = Prog(nc)
    outs = phase_A(p, io)
    p.wait_only("sp", [r.lw for r in outs])
    p.emit()
    return nc


def build_B(li, nchunks=NCH):
    nc = bass.Bass("TRN2", target_bir_lowering=False)
    I = "ExternalInput"
    io = {"hn_meta": _dram(nc, "hn_meta", [D, 128], BF16, I), "hn_all": _dram(nc, "hn_all", [4, D, 2048], BF16, I),
          "w": _dram(nc, "w", [D, NBW], F32, I), "rope": _dram(nc, "rope", [LTOT, 288], F32, I),
          "tabs": _dram(nc, "tabs", [8, 128, 128], F32, I), "cols": _dram(nc, "cols", [128, 8], F32, I),
          "vcol": _dram(nc, "vcol", [128, NCH], F32, I), "hg_lb": _dram(nc, "hg_lb", [DEPTH, 256], F32, I),
          "da_lambda": _dram(nc, "da_lambda", [1, 256], F32, I), "subln": _dram(nc, "subln", [1, 128], F32, I),
          "ident": _dram(nc, "ident", [128, 128], F32, I),
          "oT": _dram(nc, "oT", [768, LTOT], BF16, "ExternalOutput")}
    p = Prog(nc)
    r_o = phase_B(p, li, io, nchunks)
    p.wait_only("sp", [r_o.lw])
    p.emit()
    return nc


def build_C(last):
    nc = bass.Bass("TRN2", target_bir_lowering=False)
    I = "ExternalInput"
    O = "ExternalOutput"
    io = {"hT_in": _dram(nc, "hT_in", [D, TLOC], F32, I), "hnT_in": _dram(nc, "hnT_in", [D, TLOC], BF16, I),
          "brT": _dram(nc, "brT", [4, 768, TLOC], BF16, I), "wg": _dram(nc, "wg", [D, 3072], F32, I),
          "wb": _dram(nc, "wb", [3, D, D], F32, I), "wo": _dram(nc, "wo", [D, D], F32, I),
          "g_ffn": _dram(nc, "g_ffn", [128, 8], F32, I), "g_next": _dram(nc, "g_next", [128, 8], F32, I),
          "convp": _dram(nc, "convp", [128, 44, 4], F32, I), "wfi": _dram(nc, "wfi", [D, 2 * DFF], F32, I),
          "wfo": _dram(nc, "wfo", [DFF, D], F32, I),
          "hmidT": _dram(nc, "hmidT", [D, TLOC], F32, "Internal"), "hn2T": _dram(nc, "hn2T", [D, TLOC], BF16, "Internal"),
          "hT_out": _dram(nc, "hT_out", [D, TLOC], F32, O)}
    if last:
        io["yT"] = _dram(nc, "yT", [D, NLOC * 128], F32, O)
    else:
        io["hnT_out"] = _dram(nc, "hnT_out", [D, TLOC], BF16, O)
    p = Prog(nc)
    outs = phase_C(p, io, last)
    p.wait_only("sp", [r.lw for r in outs])
    p.emit()
    return nc


def const_tables():
    f32 = np.float32
    pos = (np.arange(LTOT) - 112).astype(f32)
    rope = np.zeros((LTOT, 288), f32)
    inv = (10000.0 ** (-np.arange(0, 128, 2, dtype=f32) / 128)).astype(f32)
    ang = pos[:, None] * inv[None, :]
    rope[:, 0:64] = np.cos(ang)
    rope[:, 64:128] = np.sin(ang)
    rope[:, 128:192] = -np.sin(ang)
    inv = (10000.0 ** (-np.arange(0, 64, 2, dtype=f32) / 64)).astype(f32)
    ang = pos[:, None] * inv[None, :]
    rope[:, 192:224] = np.cos(ang)
    rope[:, 224:256] = np.sin(ang)
    rope[:, 256:288] = -np.sin(ang)
    idx = np.arange(128)
    tabs_h, cols_h = [], []
    same = (idx[:, None] // 32) == (idx[None, :] // 32)
    for hd in range(4):
        log_g = np.log1p(-np.exp2(-5.0 - hd))
        tabs = np.zeros((8, 128, 128), f32)
        gap = idx[None, :] - idx[:, None]
        tabs[0] = np.where(gap >= 0, np.exp(log_g * np.maximum(gap, 0)), 0.0) * 128 ** -0.5
        tabs[1] = np.exp(log_g * (idx[None, :] + 1.0)) * np.ones((128, 1))
        tabs[2] = (same & (idx[:, None] <= idx[None, :])).astype(f32)
        tabs[3] = (same & (idx[:, None] > idx[None, :])).astype(f32)
        tabs[4] = (same & (idx[None, :] >= idx[:, None])).astype(f32)
        tabs[5] = (idx[None, :] >= idx[:, None]).astype(f32)
        cols = np.zeros((128, 8), f32)
        cols[:, 0] = np.exp(log_g * (127.0 - idx)) * 128 ** -0.5
        cols[:, 1] = np.exp(log_g * 128.0)
        for j in range(4):
            cols[:, 2 + j] = (idx // 32 == j)
        tabs_h.append(tabs)
        cols_h.append(cols)
    vcol = np.ones((128, NCH), f32)
    vcol[:112, 0] = 0.0
    return rope, tabs_h, cols_h, vcol


def gcols(g):
    return np.ascontiguousarray(np.asarray(g, np.float32).reshape(8, 128).T)


def w_group(w_in_l, g):
    s = lambda off, width: w_in_l[:, off + g * width: off + (g + 1) * width]
    parts = [s(0, 128), s(512, 128), s(1024, 256), s(2048, 256), s(6144, 256), s(3072, 256), s(4096, 256), s(5120, 256),
             s(9216, 256), s(7168, 256), s(8192, 256)]
    return np.ascontiguousarray(np.concatenate(parts, axis=1))


_NC_CACHE = {}
_DBG = None


def _get(name, fn):
    if name not in _NC_CACHE:
        _NC_CACHE[name] = fn()
    return _NC_CACHE[name]


def local_tokens(q):
    base = 128 + q * 2048
    return np.concatenate([np.arange(128), np.arange(base - HALO, base), np.arange(base, base + 2048)])


def kernel(x, meta, norm_mix_g, w_in, w_branch, w_out, hg_lb, da_lambda, da_subln_g, norm_ffn_g, w_ffn_in,
           ffn_conv_w, ffn_conv_b, w_ffn_out, norm_final_g):
    f32 = np.float32
    bf = ml_dtypes.bfloat16
    A = lambda a: np.asarray(a, f32)
    x, meta, w_in, w_branch, w_out = A(x), A(meta), A(w_in), A(w_branch), A(w_out)
    w_ffn_in, w_ffn_out, ffn_conv_w, ffn_conv_b = A(w_ffn_in), A(w_ffn_out), A(ffn_conv_w), A(ffn_conv_b)
    hg_lb, da_lambda, da_subln_g = A(hg_lb), A(da_lambda), A(da_subln_g)
    rope, tabs_h, cols_h, vcol = const_tables()
    ident = np.eye(128, dtype=f32)
    cores = list(range(8))
    hfull = np.zeros((2, LTOT, D), f32)
    hfull[:, 112:128] = meta[None]
    hfull[:, 128:] = x
    loc = [local_tokens(q) for q in range(4)]
    in_maps = [{"xT": np.ascontiguousarray(hfull[c // 4][loc[c % 4]].T), "g": gcols(norm_mix_g[0])} for c in cores]
    hT = [m["xT"] for m in in_maps]
    res = run_bass_kernel_spmd(_get("A", build_A), in_maps, core_ids=cores)
    hnT = [np.asarray(r["hnT"]) for r in res.results]
    if _DBG is not None:
        _DBG["hnT_A"] = hnT
    out = np.zeros((2, SEQ, D), f32)
    for li in range(DEPTH):
        last = li == DEPTH - 1
        in_maps = []
        for c in cores:
            b, g = c // 4, c % 4
            in_maps.append({
                "hn_meta": np.ascontiguousarray(hnT[b * 4][:, 0:128]),
                "hn_all": np.ascontiguousarray(np.stack([hnT[b * 4 + q][:, 132:] for q in range(4)])),
                "w": w_group(w_in[li], g), "rope": rope, "tabs": tabs_h[g], "cols": cols_h[g], "vcol": vcol,
                "hg_lb": np.ascontiguousarray(hg_lb[:, g * 256:(g + 1) * 256]),
                "da_lambda": np.ascontiguousarray(da_lambda[li].reshape(1, 256)),
                "subln": np.ascontiguousarray(da_subln_g[li].reshape(1, 128)), "ident": ident})
        res = run_bass_kernel_spmd(_get("B%d" % li, lambda: build_B(li)), in_maps, core_ids=cores)
        oT = [np.asarray(r["oT"]) for r in res.results]
        if _DBG is not None:
            _DBG["oT%d" % li] = oT
        convp = np.concatenate([ffn_conv_w[li].T, ffn_conv_b[li][:, None]], axis=1)
        convp = np.ascontiguousarray(convp.reshape(44, 128, 4).transpose(1, 0, 2))
        g_next = norm_final_g if last else norm_mix_g[li + 1]
        in_maps = []
        for c in cores:
            b, q = c // 4, c % 4
            in_maps.append({
                "hT_in": hT[c], "hnT_in": hnT[c],
                "brT": np.ascontiguousarray(np.stack([oT[b * 4 + g][:, loc[q]] for g in range(4)])),
                "wg": np.ascontiguousarray(w_in[li][:, 10240:13312]), "wb": w_branch[li], "wo": w_out[li],
                "g_ffn": gcols(norm_ffn_g[li]), "g_next": gcols(g_next), "convp": convp,
                "wfi": w_ffn_in[li], "wfo": w_ffn_out[li]})
        res = run_bass_kernel_spmd(_get("C%d" % int(last), lambda: build_C(last)), in_maps, core_ids=cores)
        hT = [np.asarray(r["hT_out"]) for r in res.results]
        if _DBG is not None:
            _DBG["hT%d" % li] = hT
        if last:
            for c in cores:
                out[c // 4, (c % 4) * 2048:(c % 4 + 1) * 2048] = np.asarray(res.results[c]["yT"]).T
        else:
            hnT = [np.asarray(r["hnT_out"]) for r in res.results]
    return out


def phase_X(p, priv, sh2, oidx):
    bufs = [p.sb([128, LTOT], BF16, "xb%d" % i) for i in range(2)]
    r_sh = Res("sh")
    n = 0
    for j in range(NGL):
        ix, r_ix = p.sb([128, 6], mybir.dt.int32, "xi%d" % j)
        p.dma("sp", ix[:], oidx[j], [], [r_ix], "ldxi%d" % j)
        pv = priv[j].rearrange("(k p) t -> p k t", p=128)
        for k6 in range(6):
            buf, r_buf = bufs[n % 2]
            p.dma("sp", buf[:], pv[:, k6, :], [], [r_buf], "ldx%d" % (n % 2))
            p.op("pool", lambda e, buf=buf, ix=ix, k6=k6: e.indirect_dma_start(
                out=sh2, out_offset=bass.IndirectOffsetOnAxis(ap=ix[:, k6:k6 + 1], axis=0), in_=buf[:], in_offset=None,
                bounds_check=4 * 768 - 1, oob_is_err=False), [r_buf, r_ix], [r_sh], dma_key="scx%d" % (n % 2))
            n += 1


NTF = 320
TILES_F = [(i * NTF, NTF) for i in range(LTOT // NTF)]


PAIR = os.environ.get("K_PAIR", "1") == "1"
NGL = 2 if PAIR else 4


def build_fused():
    nc = bass.Bass("TRN2", target_bir_lowering=False, num_devices=4) if PAIR else bass.Bass("TRN2", target_bir_lowering=False)
    I = "ExternalInput"
    ext = {}

    def inp(name, shape, dt=F32):
        ext[name] = _dram(nc, name, shape, dt, I)
        return ext[name]
    xT = inp("xT", [D, LTOT])
    rope = inp("rope", [LTOT, 288])
    vcol = inp("vcol", [128, NCH])
    ident = inp("ident", [128, 128])
    tabs = [inp("tabs%d" % g, [8, 128, 128]) for g in range(NGL)]
    cols = [inp("cols%d" % g, [128, 8]) for g in range(NGL)]
    hglb = [inp("hg_lb%d" % g, [DEPTH, 256]) for g in range(NGL)]
    oidx = [inp("oidx%d" % g, [128, 6], mybir.dt.int32) for g in range(NGL)] if PAIR else None
    gmix = [inp("g_mix%d" % l, [128, 8]) for l in range(DEPTH)]
    gfin = inp("g_fin", [128, 8])
    L = []
    for l in range(DEPTH):
        L.append({"w": [inp("w%d_%d" % (l, g), [D, NBW]) for g in range(NGL)],
                  "da_lambda": inp("da_lambda%d" % l, [1, 256]), "subln": inp("subln%d" % l, [1, 128]),
                  "wg": inp("wg%d" % l, [D, 3072]), "wb": inp("wb%d" % l, [3, D, D]), "wo": inp("wo%d" % l, [D, D]),
                  "g_ffn": inp("g_ffn%d" % l, [128, 8]), "convp": inp("convp%d" % l, [128, 44, 4]),
                  "wfi": inp("wfi%d" % l, [D, 2 * DFF]), "wfo": inp("wfo%d" % l, [DFF, D])})
    yT = _dram(nc, "yT", [D, SEQ], F32, "ExternalOutput")
    hnT = _dram(nc, "hnT_i", [D, LTOT], BF16, "Internal")
    if PAIR:
        brT = nc.dram_tensor("brT_sh", [4, 768, LTOT], BF16, addr_space="Shared").ap()
        brT2 = brT.rearrange("g f t -> (g f) t")
        brP = _dram(nc, "brP_i", [NGL, 768, LTOT], BF16, "Internal")
    else:
        brT = _dram(nc, "brT_i", [4, 768, LTOT], BF16, "Internal")
    hT = _dram(nc, "hT_i", [D, LTOT], F32, "Internal")
    hmidT = _dram(nc, "hmidT_i", [D, LTOT], F32, "Internal")
    hn2T = _dram(nc, "hn2T_i", [D, LTOT], BF16, "Internal")
    counts = []

    fstop = int(os.environ.get("K_FSTOP", "99"))

    def close(p):
        if len(counts) >= fstop:
            p.stack.close()
            counts.append(None)
            return
        p.finish()
        counts.append({e: len(v) for e, v in p.ops.items()})
        p.emit()
        nc.all_engine_barrier()
    p = Prog(nc)
    phase_A(p, {"xT": xT, "g": gmix[0], "hnT": hnT}, TILES_F)
    close(p)
    for l in range(DEPTH):
        last = l == DEPTH - 1
        for g in range(NGL):
            p = Prog(nc)
            iob = {"hn_full": hnT, "w": L[l]["w"][g], "rope": rope, "tabs": tabs[g], "cols": cols[g], "vcol": vcol,
                   "hg_lb": hglb[g], "da_lambda": L[l]["da_lambda"], "subln": L[l]["subln"], "ident": ident}
            iob["oT"] = brP[g] if PAIR else brT[g]
            phase_B(p, l, iob)
            close(p)
        if PAIR:
            p = Prog(nc)
            phase_X(p, brP, brT2, oidx)
            close(p)
            nc.all_core_barrier()
        p = Prog(nc)
        io = {"hT_in": xT if l == 0 else hT, "hnT_in": hnT, "brT": brT, "wg": L[l]["wg"], "wb": L[l]["wb"], "wo": L[l]["wo"],
              "g_ffn": L[l]["g_ffn"], "g_next": gfin if last else gmix[l + 1], "convp": L[l]["convp"], "wfi": L[l]["wfi"],
              "wfo": L[l]["wfo"], "hmidT": hmidT, "hn2T": hn2T, "hT_out": hT}
        if last:
            io["yT"] = yT
        else:
            io["hnT_out"] = hnT
        phase_C(p, io, last, TILES_F, ybase=128, hist_reset=(0,))
        close(p)
        if PAIR and not last:
            nc.all_core_barrier()
    print("fused program op counts per phase:", counts)
    return nc


def kernel_fused(x, meta, norm_mix_g, w_in, w_branch, w_out, hg_lb, da_lambda, da_subln_g, norm_ffn_g, w_ffn_in,
                 ffn_conv_w, ffn_conv_b, w_ffn_out, norm_final_g):
    f32 = np.float32
    A = lambda a: np.asarray(a, f32)
    x, meta, w_in, w_branch, w_out = A(x), A(meta), A(w_in), A(w_branch), A(w_out)
    w_ffn_in, w_ffn_out, ffn_conv_w, ffn_conv_b = A(w_ffn_in), A(w_ffn_out), A(ffn_conv_w), A(ffn_conv_b)
    hg_lb, da_lambda, da_subln_g = A(hg_lb), A(da_lambda), A(da_subln_g)
    rope, tabs_h, cols_h, vcol = const_tables()
    shared = {"rope": rope, "vcol": vcol, "ident": np.eye(128, dtype=f32), "g_fin": gcols(norm_final_g)}
    percore = [dict() for _ in range(2)]
    for e in range(2 if PAIR else 1):
        for j in range(NGL):
            g = NGL * e + j
            percore[e]["tabs%d" % j] = tabs_h[g]
            percore[e]["cols%d" % j] = cols_h[g]
            percore[e]["hg_lb%d" % j] = np.ascontiguousarray(hg_lb[:, g * 256:(g + 1) * 256])
            if PAIR:
                percore[e]["oidx%d" % j] = np.ascontiguousarray(
                    (g * 768 + np.arange(6)[None, :] * 128 + np.arange(128)[:, None]).astype(np.int32))
            for l in range(DEPTH):
                percore[e]["w%d_%d" % (l, j)] = w_group(w_in[l], g)
    for l in range(DEPTH):
        shared["g_mix%d" % l] = gcols(norm_mix_g[l])
        shared["da_lambda%d" % l] = np.ascontiguousarray(da_lambda[l].reshape(1, 256))
        shared["subln%d" % l] = np.ascontiguousarray(da_subln_g[l].reshape(1, 128))
        shared["wg%d" % l] = np.ascontiguousarray(w_in[l][:, 10240:13312])
        shared["wb%d" % l] = w_branch[l]
        shared["wo%d" % l] = w_out[l]
        shared["g_ffn%d" % l] = gcols(norm_ffn_g[l])
        convp = np.concatenate([ffn_conv_w[l].T, ffn_conv_b[l][:, None]], axis=1)
        shared["convp%d" % l] = np.ascontiguousarray(convp.reshape(44, 128, 4).transpose(1, 0, 2))
        shared["wfi%d" % l] = w_ffn_in[l]
        shared["wfo%d" % l] = w_ffn_out[l]
    in_maps = []
    npc = 2 if PAIR else 1
    for b in range(2):
        hfull = np.zeros((LTOT, D), f32)
        hfull[112:128] = meta
        hfull[128:] = x[b]
        xT = np.ascontiguousarray(hfull.T)
        for e in range(npc):
            m = dict(shared)
            m.update(percore[e])
            m["xT"] = xT
            in_maps.append(m)
    res = run_bass_kernel_spmd(_get("F", build_fused), in_maps, core_ids=list(range(2 * npc)))
    return np.stack([np.ascontiguousarray(np.asarray(res.results[b * npc]["yT"]).T) for b in range(2)])


kernel_unfused = kernel
if os.environ.get("K_FUSED", "1") == "1":
    kernel = kernel_fused
```

```python
from contextlib import ExitStack
import math
import numpy as np
import ml_dtypes
import concourse.bass as bass
import concourse.mybir as mybir
from concourse.bass_utils import run_bass_kernel_spmd

F32 = mybir.dt.float32
BF16 = mybir.dt.bfloat16
ALU = mybir.AluOpType
AF = mybir.ActivationFunctionType

D = 1024
SEQ = 8192
NCH = 65
LTOT = NCH * 128
NLOC = 16
HALO = 4
TLOC = 128 + HALO + NLOC * 128
DFF = 2816
EPS = 1e-6
DEPTH = 2
NBW = 2560


class Res:
    __slots__ = ("name", "lw", "rd", "excl")

    def __init__(self, name="r", excl=False):
        self.name = name
        self.lw = None
        self.rd = {}
        self.excl = excl


import os
NO_POOL = os.environ.get("NO_POOL", "1") == "1"


class Prog:
    ENG = ("pe", "act", "dve", "pool", "sp")

    _uid = [0]

    def __init__(self, nc):
        Prog._uid[0] += 1
        self.pfx = "P%d_" % Prog._uid[0]
        self.nc = nc
        self.stack = ExitStack()
        self.ops = {e: [] for e in self.ENG}
        self.cnt = {e: 0 for e in self.ENG}
        self.seen = {e: {} for e in self.ENG}
        self.dma_cnt = {}
        self.n = 0

    def sb(self, shape, dt, name=None):
        self.n += 1
        name = self.pfx + (name or ("sb%d" % self.n))
        t = self.stack.enter_context(self.nc.sbuf_tensor(name, list(shape), dt))
        return t, Res(name)

    def ps(self, shape, dt, name=None):
        self.n += 1
        name = self.pfx + (name or ("ps%d" % self.n))
        t = self.stack.enter_context(self.nc.psum_tensor(name, list(shape), dt))
        return t, Res(name, excl=True)

    def op(self, eng, fn, reads=(), writes=(), dma_key=None):
        if eng == "pool" and dma_key is None and NO_POOL:
            eng = "dve"
        if eng == "gps":
            eng = "pool"
        deps = []
        for r in reads:
            if r.lw is not None:
                deps.append(r.lw)
            if r.excl:
                deps.extend((k, v) for k, v in r.rd.items() if k != eng)
        for w in writes:
            if w.lw is not None:
                deps.append(w.lw)
            deps.extend(w.rd.items())
        if dma_key is None:
            self.cnt[eng] += 1
            tok = (eng, self.cnt[eng])
        else:
            c = self.dma_cnt.get(dma_key, 0) + 16
            self.dma_cnt[dma_key] = c
            tok = (dma_key, c)
        waits = {}
        seen = self.seen[eng]
        for (k, v) in deps:
            if eng == "pe" and k == "pe":
                continue
            if seen.get(k, 0) >= v:
                continue
            if waits.get(k, 0) < v:
                waits[k] = v
        for k, v in waits.items():
            seen[k] = v
        self.ops[eng].append((fn, waits, tok, dma_key is not None))
        for r in reads:
            if r.rd.get(tok[0], 0) < tok[1]:
                r.rd[tok[0]] = tok[1]
        for w in writes:
            w.lw = tok
            w.rd = {}
        return tok

    def wait_only(self, eng, toks):
        waits = {}
        for (k, v) in toks:
            if self.seen[eng].get(k, 0) >= v:
                continue
            waits[k] = max(waits.get(k, 0), v)
        for k, v in waits.items():
            self.seen[eng][k] = v
        self.ops[eng].append((None, waits, None, False))

    def val(self, eng, name, fn, reads, store):
        waits = {}
        for r in reads:
            if r.lw is not None and self.seen[eng].get(r.lw[0], 0) < r.lw[1]:
                waits[r.lw[0]] = r.lw[1]
        for k, v in waits.items():
            self.seen[eng][k] = v

        def run(e):
            store[name] = fn(e)
            return None
        self.ops[eng].append((run, waits, None, False))

    def dma_dyn(self, eng, apfn, reads, writes, key, **kw):
        def run(e):
            o_, i_ = apfn()
            return e.dma_start(out=o_, in_=i_, **kw)
        return self.op(eng, run, reads, writes, dma_key=key)

    def finish(self):
        self.wait_only("sp", list(self.dma_cnt.items()))

    def dma(self, eng, out, in_, reads, writes, key, **kw):
        return self.op(eng, lambda e: e.dma_start(out=out, in_=in_, **kw), reads, writes, dma_key=key)

    def mm(self, out, lhsT, rhs, start, stop, reads, writes):
        return self.op("pe", lambda e: e.matmul(out, lhsT=lhsT, rhs=rhs, start=start, stop=stop), reads, writes)

    def tr(self, out, in_, ident, reads, writes):
        return self.op("pe", lambda e: e.transpose(out, in_, ident), reads, writes)

    def act(self, out, in_, func, reads, writes, **kw):
        return self.op("act", lambda e: e.activation(out=out, in_=in_, func=func, **kw), reads, writes)

    def tt(self, eng, out, in0, in1, op, reads, writes):
        return self.op(eng, lambda e: e.tensor_tensor(out=out, in0=in0, in1=in1, op=op), reads, writes)

    def ts(self, eng, out, in0, s1, s2, op0, op1, reads, writes):
        return self.op(eng, lambda e: e.tensor_scalar(out=out, in0=in0, scalar1=s1, scalar2=s2, op0=op0, op1=op1), reads, writes)

    def stt(self, eng, out, in0, scalar, in1, op0, op1, reads, writes):
        eng = "dve"
        return self.op(eng, lambda e: e.scalar_tensor_tensor(out=out, in0=in0, scalar=scalar, in1=in1, op0=op0, op1=op1), reads, writes)

    def cp(self, eng, out, in_, reads, writes):
        if eng == "act":
            return self.op("act", lambda e: e.copy(out=out, in_=in_), reads, writes)
        return self.op(eng, lambda e: e.tensor_copy(out=out, in_=in_), reads, writes)

    def memset(self, eng, ap, val, writes):
        return self.op(eng, lambda e: e.memset(ap, val), [], writes)

    def recip(self, out, in_, reads, writes):
        return self.op("dve", lambda e: e.reciprocal(out=out, in_=in_), reads, writes)

    def emit(self):
        nc = self.nc
        sems = {}
        for e in ("pe", "act", "dve", "pool"):
            sems[e] = nc.alloc_semaphore(name=self.pfx + "s_" + e)
        for k in self.dma_cnt:
            sems[k] = nc.alloc_semaphore(name=self.pfx + "d_" + str(k))
        ops = self.ops

        def mk(name):
            def body(eng):
                for fn, waits, tok, is_dma in ops[name]:
                    for k, v in waits.items():
                        eng.wait_ge(sems[k], v)
                    if fn is None:
                        continue
                    ins = fn(eng)
                    if ins is not None and tok is not None:
                        ins.then_inc(sems[tok[0]], 16 if is_dma else 1)
            return body

        with nc.Block() as block:
            block.tensor(mk("pe"))
            block.scalar(mk("act"))
            block.vector(mk("dve"))
            block.gpsimd(mk("pool"))
            block.sync(mk("sp"))
        self.stack.close()
        nc.all_engine_barrier()
        nc.clear_and_free_semaphores(list(sems.values()))
        nc.all_engine_barrier()


def rawap(t, extra):
    return bass.AP(t.tensor, t.offset, [list(t.ap[0])] + [list(x) for x in extra])


def phase_B(p, li, io, nchunks=NCH):
    nc = p.nc
    lam_init = 0.8 - 0.6 * math.exp(-0.3 * li)
    R = Res
    ident, r_id = p.sb([128, 128], BF16, "identB")
    p.dma("pool", ident[:], io["ident"], [], [r_id], "c_id")
    W, r_W = p.sb([128, 8, NBW], BF16, "Wg")
    wv = io["w"].rearrange("(k p) n -> p k n", p=128)
    for k in range(8):
        p.dma("pool", W[:, k, :], wv[:, k, :], [], [r_W], "c_w")
    tabs, r_tabs = p.sb([128, 6, 128], F32, "tabsB")
    p.dma("sp", tabs[:], io["tabs"][0:6].rearrange("k p n -> p k n"), [], [r_tabs], "c_tabs")
    cols, r_cols = p.sb([128, 8], F32, "colsB")
    p.dma("sp", cols[:], io["cols"], [], [r_cols], "c_cols")
    vcol, r_vcol = p.sb([128, NCH], F32, "vcolB")
    p.dma("sp", vcol[:], io["vcol"], [], [r_vcol], "c_vcol")
    lbr, r_lbr = p.sb([128, DEPTH, 256], F32, "lbr")
    for d_ in range(DEPTH):
        p.dma("sp", lbr[:, d_, :], io["hg_lb"][d_:d_ + 1, :].partition_broadcast(128), [], [r_lbr], "c_lb")
    p.act(lbr[:], lbr[:], AF.Exp, [r_lbr], [r_lbr])
    lsum, r_lsum = p.sb([128, 256], F32, "lsum")
    p.tt("dve", lsum[:], lbr[:, 0, :], lbr[:, 1, :], ALU.add, [r_lbr], [r_lsum])
    p.recip(lsum[:], lsum[:], [r_lsum], [r_lsum])
    lb, r_lb = p.sb([128, 256], F32, "lb")
    oml, r_oml = p.sb([128, 256], F32, "oml")
    p.memset("dve", lb[:], 0.0, [r_lb])
    for d_ in range(li + 1):
        p.stt("dve", lb[:], lbr[:, d_, :], 1.0, lb[:], ALU.mult, ALU.add, [r_lbr, r_lb], [r_lb])
    p.stt("dve", lb[:], lbr[:, 0, :], -1.0, lb[:], ALU.mult, ALU.add, [r_lbr, r_lb], [r_lb])
    p.tt("dve", lb[:], lb[:], lsum[:], ALU.mult, [r_lb, r_lsum], [r_lb])
    p.ts("dve", oml[:], lb[:], -1.0, 1.0, ALU.mult, ALU.add, [r_lb], [r_oml])
    lp, r_lp = p.sb([128, 4, 64], F32, "lp")
    p.dma("sp", lp[:].rearrange("p a d -> p (a d)"), io["da_lambda"].partition_broadcast(128), [], [r_lp], "c_lp")
    lpp, r_lpp = p.sb([128, 2, 64], F32, "lpp")
    p.tt("dve", lpp[:, 0, :], lp[:, 0, :], lp[:, 1, :], ALU.mult, [r_lp], [r_lpp])
    p.tt("dve", lpp[:, 1, :], lp[:, 2, :], lp[:, 3, :], ALU.mult, [r_lp], [r_lpp])
    lsm, r_lsm = p.sb([128, 2], F32, "lsm")
    p.op("dve", lambda e: e.reduce_sum(out=lsm[:], in_=lpp[:], axis=mybir.AxisListType.X), [r_lpp], [r_lsm])
    p.act(lsm[:], lsm[:], AF.Exp, [r_lsm], [r_lsm])
    nlam, r_nlam = p.sb([128, 1], F32, "nlam")
    p.tt("dve", nlam[:], lsm[:, 1:2], lsm[:, 0:1], ALU.subtract, [r_lsm], [r_nlam])
    p.ts("dve", nlam[:], nlam[:], -lam_init, None, ALU.add, ALU.bypass, [r_nlam], [r_nlam])
    gsub, r_gsub = p.sb([128, 128], F32, "gsub")
    p.dma("sp", gsub[:], io["subln"].partition_broadcast(128), [], [r_gsub], "c_gs")
    p.ts("dve", gsub[:], gsub[:], 1.0 - lam_init, None, ALU.mult, ALU.bypass, [r_gsub], [r_gsub])

    KT = []
    VA = []
    KT_r = [[Res("KT%d_%d" % (h, c)) for c in range(NCH)] for h in range(2)]
    VA_r = [[Res("VA%d_%d" % (h, c)) for c in range(NCH)] for h in range(2)]
    for h in range(2):
        KT.append(p.sb([128, LTOT], BF16, "KT%d" % h))
        VA.append(p.sb([128, NCH, 130], BF16, "VA%d" % h))
        p.memset("dve", VA[h][0][:], 0.0, VA_r[h])
    S_ret, r_Sret = p.sb([128, 256], F32, "S_ret")
    Sb_ret, r_Sbret = p.sb([128, 256], BF16, "Sb_ret")
    p.memset("pool", S_ret[:], 0.0, [r_Sret])
    p.memset("pool", Sb_ret[:], 0.0, [r_Sbret])
    S_hg = []
    Sb_hg = []
    for h in range(2):
        s_, r_ = p.sb([128, 128], F32, "S_hg%d" % h)
        p.memset("pool", s_[:], 0.0, [r_])
        S_hg.append((s_, r_))
        lst = []
        for par in range(2):
            ring = []
            for j in range(4):
                sb_, rb_ = p.sb([128, 128], BF16, "Sb_hg%d_%d_%d" % (h, par, j))
                p.memset("pool", sb_[:], 0.0, [rb_])
                ring.append((sb_, rb_))
            lst.append(ring)
        Sb_hg.append(lst)
    QZ = []
    for h in range(2):
        q_, r_ = p.sb([128, 4, 128], BF16, "QZ%d" % h)
        p.memset("pool", q_[:], 0.0, [r_])
        QZ.append((q_, r_))

    hnb = [p.sb([128, 8, 512], BF16, "hnb%d" % i) for i in range(2)]
    rtb = [p.sb([128, 4, 288], F32, "rtb%d" % i) for i in range(2)]
    GP = [p.ps([128, 512], F32, "GP%d" % i) for i in range(4)]
    gp_i = [0]

    def gp():
        g = GP[gp_i[0] % 4]
        gp_i[0] += 1
        return g
    TRb, r_TRb = p.ps([128, 1024], BF16, "TRb")
    OArh, r_OArh = p.ps([128, 512], F32, "OArh")
    SUr, r_SUr = p.ps([128, 512], F32, "SUr")
    OAda, r_OAda = p.ps([128, 512], F32, "OAda")
    OAm = [(OAda, r_OAda), (SUr, r_SUr)]

    def dbl(shape, dt, name):
        return [p.sb(shape, dt, "%s_%d" % (name, i)) for i in range(2)]
    qk_t1 = dbl([128, 256], F32, "qk_t1")
    qk_t2 = dbl([128, 256], F32, "qk_t2")
    qk_r = dbl([128, 256], BF16, "qk_r")
    kd = dbl([128, 128], BF16, "kd")
    Vr = dbl([128, 256], BF16, "Vr")
    G = dbl([128, 512], F32, "G")
    sig = dbl([128, 256], F32, "sig")
    lf = dbl([128, 256], F32, "lf")
    kk = dbl([128, 256], F32, "kk")
    hq = dbl([128, 256], F32, "hq")
    hv = dbl([128, 256], BF16, "hv")
    _t1 = p.sb([128, 512], F32, "dq_t1")
    _t2 = p.sb([128, 512], F32, "dq_t2")
    dq_t1 = [_t1, _t1]
    dq_t2 = [_t2, _t2]
    dqk = dbl([128, 512], BF16, "dqk")
    rT = dbl([128, 3, 128], BF16, "rT")
    dqT = dbl([128, 2, 2, 128], BF16, "dqT")
    for i_ in range(2):
        p.memset("dve", dqT[i_][0][:], 0.0, [dqT[i_][1]])
    AT_r = dbl([128, 128], BF16, "AT_r")
    eq = dbl([128, 256], F32, "eq")
    qt = dbl([128, 256], BF16, "qt")
    kt = dbl([128, 256], BF16, "kt")
    kbar = dbl([128, 256], F32, "kbar")
    kbZ = dbl([128, 2, 4, 128], BF16, "kbZ")
    hT = dbl([128, 2, 128], BF16, "hT")
    eqT = dbl([128, 2, 128], BF16, "eqT")
    AT_h = dbl([128, 2, 128], BF16, "AT_h")
    dec = dbl([128, 2, 4], F32, "dec")
    on = dbl([128, 768], BF16, "on")
    oT_sb = dbl([128, 6, 128], BF16, "oT_sb")
    sm = dbl([128, 16], F32, "sm")
    dtmp = dbl([128, 2, 128], F32, "dtmp")
    wda = dbl([128, 128], F32, "wda")
    PT = [p.sb([128, 512], BF16, "PT%d" % i) for i in range(3)]
    pt_i = [0]
    r_oT = R("oT_dram")

    oidx = r_oidx = oT_v = None
    if "oT_sh" in io:
        oidx, r_oidx = p.sb([128, 6, NCH], mybir.dt.int32, "oidx")
        p.dma("sp", oidx[:], io["oidx"], [], [r_oidx], "c_oidx")
    else:
        oT_v = io["oT"].rearrange("(k p) t -> p k t", p=128)
    hn_full_v = hn_all_v = hn_meta_v = None
    if "hn_full" in io:
        hn_full_v = io["hn_full"].rearrange("(k p) t -> p k t", p=128)
    else:
        hn_all_v = io["hn_all"].rearrange("r (k p) t -> r p k t", p=128)
        hn_meta_v = io["hn_meta"].rearrange("(k p) t -> p k t", p=128)
    rope_v = io["rope"].rearrange("(c p) n -> p c n", p=128)

    def rope_ops(b, src_ps, r_src, t1, t2, dst, ngrp, half, cos_ap, sin_ap, nsin_ap, r_tab):
        (t1a, r_t1), (t2a, r_t2), (da_, r_d) = t1, t2, dst
        w = ngrp * 2 * half
        p.tt("dve", t1a[:, :w].rearrange("p (g h) -> p g h", h=half), src_ps.rearrange("p (g h) -> p g h", h=half),
             cos_ap.unsqueeze(1).to_broadcast([128, ngrp * 2, half]), ALU.mult, [r_src, r_tab], [r_t1])
        sv = src_ps.rearrange("p (g two h) -> p g two h", two=2, h=half)
        t2v = t2a[:, :w].rearrange("p (g two h) -> p g two h", two=2, h=half)
        p.tt("dve", t2v[:, :, 0, :], sv[:, :, 1, :], nsin_ap.unsqueeze(1).to_broadcast([128, ngrp, half]), ALU.mult,
             [r_src, r_tab], [r_t2])
        p.tt("dve", t2v[:, :, 1, :], sv[:, :, 0, :], sin_ap.unsqueeze(1).to_broadcast([128, ngrp, half]), ALU.mult,
             [r_src, r_tab], [r_t2])
        p.tt("pool", da_[:, :w], t1a[:, :w], t2a[:, :w], ALU.add, [r_t1, r_t2], [r_d])

    import os
    KSTOP = int(os.environ.get("KSTOP", "99"))

    def bail():
        if oT_v is not None:
            p.dma("sp", oT_v[:, 0:1, 0:128], ident[:].unsqueeze(1), [r_id], [r_oT], "st_o0")
        return r_oT
    if KSTOP == 0:
        return bail()
    ngroups = 1 + (nchunks - 1 + 3) // 4
    chunk_list = []
    for gi in range(ngroups):
        if gi == 0:
            clist = [0]
        else:
            c0 = 1 + (gi - 1) * 4
            clist = [c for c in range(c0, min(c0 + 4, nchunks))]
        for ji, c in enumerate(clist):
            chunk_list.append((gi, ji, c, clist))

    def group_load(gi, clist):
        hb, r_hb = hnb[gi % 2]
        rt, r_rt = rtb[gi % 2]
        if gi == 0:
            p.dma("sp", hb[:, :, 0:128], hn_full_v[:, :, 0:128] if hn_full_v is not None else hn_meta_v, [], [r_hb], "hn%d" % (gi % 2))
            p.dma("sp", rt[:, 0:1, :], rope_v[:, 0:1, :], [], [r_rt], "rt%d" % (gi % 2))
        else:
            c0 = clist[0]
            rk = (gi - 1) // 4
            lo = ((gi - 1) % 4) * 512
            nt = len(clist) * 128
            p.dma("sp", hb[:, :, 0:nt], hn_full_v[:, :, c0 * 128:c0 * 128 + nt] if hn_full_v is not None else hn_all_v[rk, :, :, lo:lo + nt],
                  [], [r_hb], "hn%d" % (gi % 2))
            p.dma("sp", rt[:, 0:len(clist), :], rope_v[:, c0:c0 + len(clist), :], [], [r_rt], "rt%d" % (gi % 2))

    def front_fns(gi, ji, c):
        b = c % 2
        hb, r_hb = hnb[gi % 2]
        rt, r_rt = rtb[gi % 2]
        hn_c = hb[:, :, ji * 128:(ji + 1) * 128]
        cosR, sinR, nsinR = rt[:, ji, 0:64], rt[:, ji, 64:128], rt[:, ji, 128:192]

        def proj(ti):
            g, r_g = gp()
            for k in range(8):
                p.mm(g[:, :], hn_c[:, k, :], W[:, k, ti * 512:(ti + 1) * 512], k == 0, k == 7, [r_hb, r_W], [r_g])
            return g, r_g

        def f1():
            p.memset("pool", sm[b][0][:], 0.0, [sm[b][1]])
            g0, r_g0 = proj(0)
            rope_ops(b, g0[:, 0:256], r_g0, qk_t1[b], qk_t2[b], qk_r[b], 2, 64, cosR, sinR, nsinR, r_rt)
            p.act(Vr[b][0][:], g0[:, 256:512], AF.Identity, [r_g0], [Vr[b][1]])
            p.ts("pool", kd[b][0][:], qk_r[b][0][:, 128:256], cols[:, 0:1], 0.0, ALU.mult, ALU.add, [qk_r[b][1], r_cols], [kd[b][1]])
            g1, r_g1 = proj(1)
            p.act(G[b][0][:], g1[:, :], AF.Silu, [r_g1], [G[b][1]])

        def f2():
            g2, r_g2 = proj(2)
            p.act(sig[b][0][:], g2[:, 256:512], AF.Sigmoid, [r_g2], [sig[b][1]])
            p.act(hq[b][0][:], g2[:, 0:256], AF.Identity, [r_g2], [hq[b][1]])
            p.tt("dve", sig[b][0][:], sig[b][0][:], oml[:], ALU.mult, [sig[b][1], r_oml], [sig[b][1]])
            p.tt("dve", sig[b][0][:], sig[b][0][:], lb[:], ALU.add, [sig[b][1], r_lb], [sig[b][1]])
            p.act(lf[b][0][:], sig[b][0][:], AF.Ln, [sig[b][1]], [lf[b][1]])
            p.ts("pool", kk[b][0][:], sig[b][0][:], -1.0, 1.0, ALU.mult, ALU.add, [sig[b][1]], [kk[b][1]])

        def f3():
            g3, r_g3 = proj(3)
            p.act(hv[b][0][:], g3[:, 0:256], AF.Identity, [r_g3], [hv[b][1]])
            for h in range(2):
                p.act(VA[h][0][:, c, 0:128], g3[:, 256 + h * 128:256 + (h + 1) * 128], AF.Identity, [r_g3], [VA_r[h][c]])
                p.cp("pool", VA[h][0][:, c, 128:129], vcol[:, c:c + 1], [r_vcol], [VA_r[h][c]])
            g4, r_g4 = proj(4)
            rope_ops(b, g4[:, 0:512], r_g4, dq_t1[b], dq_t2[b], dqk[b], 8, 32, rt[:, ji, 192:224], rt[:, ji, 224:256],
                     rt[:, ji, 256:288], r_rt)

        def f4():
            for i in range(2):
                p.tr(TRb[:, i * 128:(i + 1) * 128], qk_r[b][0][:, i * 128:(i + 1) * 128], ident[:], [qk_r[b][1], r_id], [r_TRb])
            for i in range(4):
                p.tr(TRb[:, (2 + i) * 128:(3 + i) * 128], dqk[b][0][:, i * 128:(i + 1) * 128], ident[:], [dqk[b][1], r_id], [r_TRb])
            p.cp("dve", rT[b][0][:, 0, :], TRb[:, 0:128], [r_TRb], [rT[b][1]])
            p.tt("dve", rT[b][0][:, 1, :], TRb[:, 0:128], tabs[:, 1, :], ALU.mult, [r_TRb, r_tabs], [rT[b][1]])
            p.cp("dve", rT[b][0][:, 2, :], TRb[:, 128:256], [r_TRb], [rT[b][1]])
            for h in range(2):
                p.act(dqT[b][0][0:64, h, 0, :], TRb[0:64, (2 + h) * 128:(3 + h) * 128], AF.Identity, [r_TRb], [dqT[b][1]])
                p.act(dqT[b][0][64:128, h, 1, :], TRb[64:128, (2 + h) * 128:(3 + h) * 128], AF.Identity, [r_TRb], [dqT[b][1]])
            for h in range(2):
                p.act(KT[h][0][:, c * 128:(c + 1) * 128], TRb[:, (4 + h) * 128:(5 + h) * 128], AF.Identity, [r_TRb], [KT_r[h][c]])
        return [f1, f2, f3, f4]

    group_load(0, chunk_list[0][3])
    for f_ in front_fns(*chunk_list[0][:3]):
        f_()
    for ci_, (gi, ji, c, clist) in enumerate(chunk_list):
            b = c % 2
            steps = []
            def ret_step(b=b, c=c):
                gs, r_gs = gp()
                p.mm(gs[:, 0:128], rT[b][0][:, 2, :], rT[b][0][:, 0, :], True, True, [rT[b][1]], [r_gs])
                p.tt("dve", AT_r[b][0][:], gs[:, 0:128], tabs[:, 0, :], ALU.mult, [r_gs, r_tabs], [AT_r[b][1]])
                p.mm(OArh[:, 0:256], AT_r[b][0][:], Vr[b][0][:], True, False, [AT_r[b][1], Vr[b][1]], [r_OArh])
                p.mm(OArh[:, 0:256], rT[b][0][:, 1, :], Sb_ret[:], False, True, [rT[b][1], r_Sbret], [r_OArh])
                gsu, r_gsu = gp()
                p.mm(gsu[:, 0:256], kd[b][0][:], Vr[b][0][:], True, True, [kd[b][1], Vr[b][1]], [r_gsu])
                p.stt("dve", S_ret[:], S_ret[:], cols[:, 1:2], gsu[:, 0:256], ALU.mult, ALU.add, [r_Sret, r_cols, r_gsu], [r_Sret])
                p.cp("pool", Sb_ret[:], S_ret[:], [r_Sret], [r_Sbret])
                p.act(dtmp[b][0][:].rearrange("p a n -> p (a n)"), OArh[:, 0:256], AF.Square, [r_OArh], [dtmp[b][1], sm[b][1]],
                      accum_out=sm[b][0][:, 0:1])
                p.act(sm[b][0][:, 0:1], sm[b][0][:, 0:1], AF.Sqrt, [sm[b][1]], [sm[b][1]], bias=EPS, scale=1.0 / 256)
                p.recip(sm[b][0][:, 0:1], sm[b][0][:, 0:1], [sm[b][1]], [sm[b][1]])
                p.stt("dve", on[b][0][:, 0:256], OArh[:, 0:256], sm[b][0][:, 0:1], G[b][0][:, 0:256], ALU.mult, ALU.mult,
                      [r_OArh, sm[b][1], G[b][1]], [on[b][1]])
            steps.append(ret_step)

            def hg_prep(b=b, c=c):
                gc, r_gc = gp()
                p.mm(gc[:, 0:256], tabs[:, 2, :], lf[b][0][:], True, True, [r_tabs, lf[b][1]], [r_gc])
                p.mm(gc[:, 256:512], tabs[:, 3, :], lf[b][0][:], True, True, [r_tabs, lf[b][1]], [r_gc])
                p.act(eq[b][0][:], gc[:, 0:256], AF.Exp, [r_gc], [eq[b][1]])
                p.tt("dve", qt[b][0][:], hq[b][0][:], eq[b][0][:], ALU.mult, [hq[b][1], eq[b][1]], [qt[b][1]])
                p.act(eq[b][0][:], gc[:, 0:256], AF.Exp, [r_gc, qt[b][1]], [eq[b][1]], scale=-1.0)
                p.tt("pool", kt[b][0][:], kk[b][0][:], eq[b][0][:], ALU.mult, [kk[b][1], eq[b][1]], [kt[b][1]])
                p.act(kbar[b][0][:], gc[:, 256:512], AF.Exp, [r_gc], [kbar[b][1]])
                p.tt("pool", kbar[b][0][:], kbar[b][0][:], kk[b][0][:], ALU.mult, [kbar[b][1], kk[b][1]], [kbar[b][1]])
                for h in range(2):
                    p.tt("pool", kbZ[b][0][:, h, :, :], kbar[b][0][:, h * 128:(h + 1) * 128].unsqueeze(1).to_broadcast([128, 4, 128]),
                         cols[:, 2:6].unsqueeze(2).to_broadcast([128, 4, 128]), ALU.mult, [kbar[b][1], r_cols], [kbZ[b][1]])
                gd, r_gd = gp()
                for h in range(2):
                    p.mm(gd[:, h * 4:(h + 1) * 4], lf[b][0][:, h * 128:(h + 1) * 128], cols[:, 2:6], True, True, [lf[b][1], r_cols], [r_gd])
                p.act(dec[b][0][:].rearrange("p h j -> p (h j)"), gd[:, 0:8], AF.Exp, [r_gd], [dec[b][1]])
                for h in range(2):
                    p.tr(TRb[:, h * 128:(h + 1) * 128], qt[b][0][:, h * 128:(h + 1) * 128], ident[:], [qt[b][1], r_id], [r_TRb])
                    p.tr(TRb[:, (2 + h) * 128:(3 + h) * 128], kt[b][0][:, h * 128:(h + 1) * 128], ident[:], [kt[b][1], r_id], [r_TRb])
                p.cp("dve", hT[b][0][:].rearrange("p h t -> p (h t)"), TRb[:, 256:512], [r_TRb], [hT[b][1]])
                p.cp("dve", eqT[b][0][:].rearrange("p h t -> p (h t)"), TRb[:, 0:256], [r_TRb], [eqT[b][1]])
                for h in range(2):
                    qzf = QZ[h][0][:].rearrange("p j t -> p (j t)")
                    p.cp("pool", rawap(qzf, [[160, 4], [1, 32]]), eqT[b][0][:, h, :].rearrange("p (j i) -> p j i", i=32),
                         [eqT[b][1]], [QZ[h][1]])
            steps.append(hg_prep)

            def hg_head(h, b=b, c=c):
                oc = slice(256 + h * 128, 256 + (h + 1) * 128)
                st = {}

                def fa():
                    gs, r_gs = gp()
                    p.mm(gs[:, 0:128], hT[b][0][:, h, :], eqT[b][0][:, h, :], True, True, [hT[b][1], eqT[b][1]], [r_gs])
                    p.tt("dve", AT_h[b][0][:, h, :], gs[:, 0:128], tabs[:, 4, :], ALU.mult, [r_gs, r_tabs], [AT_h[b][1]])
                    gu, r_gu = gp()
                    for j in range(4):
                        p.mm(gu[:, j * 128:(j + 1) * 128], kbZ[b][0][:, h, j, :], hv[b][0][:, h * 128:(h + 1) * 128], True, True,
                             [kbZ[b][1], hv[b][1]], [r_gu])
                    S_, r_S = S_hg[h]
                    for j in range(4):
                        p.stt("dve", S_[:], S_[:], dec[b][0][:, h, j:j + 1], gu[:, j * 128:(j + 1) * 128], ALU.mult, ALU.add,
                              [r_S, dec[b][1], r_gu], [r_S])
                        sbn, r_sbn = Sb_hg[h][b][j + 1] if j < 3 else Sb_hg[h][1 - b][0]
                        p.cp("pool", sbn[:], S_[:], [r_S], [r_sbn])

                def fb():
                    p.mm(OArh[:, oc], AT_h[b][0][:, h, :], hv[b][0][:, h * 128:(h + 1) * 128], True, False, [AT_h[b][1], hv[b][1]], [r_OArh])
                    for j in range(4):
                        sbj, r_sbj = Sb_hg[h][b][j]
                        p.mm(OArh[:, oc], QZ[h][0][:, j, :], sbj[:], False, j == 3, [QZ[h][1], r_sbj], [r_OArh])
                    k0 = 1 + h
                    p.act(dtmp[b][0][:, 0, :], OArh[:, oc], AF.Square, [r_OArh], [dtmp[b][1], sm[b][1]], accum_out=sm[b][0][:, k0:k0 + 1])
                    p.act(sm[b][0][:, k0:k0 + 1], sm[b][0][:, k0:k0 + 1], AF.Sqrt, [sm[b][1]], [sm[b][1]], bias=EPS, scale=1.0 / 128)
                    p.recip(sm[b][0][:, k0:k0 + 1], sm[b][0][:, k0:k0 + 1], [sm[b][1]], [sm[b][1]])
                    p.stt("dve", on[b][0][:, oc], OArh[:, oc], sm[b][0][:, k0:k0 + 1], G[b][0][:, oc], ALU.mult, ALU.mult,
                          [r_OArh, sm[b][1], G[b][1]], [on[b][1]])
                return fa, fb
            hh0, hh1 = hg_head(0), hg_head(1)
            steps += [hh0[0], hh1[0], hh0[1], hh1[1]]

            def da_head(h, b=b, c=c):
                units = [(j, m) for j in range(c + 1) for m in range(2)]
                batches = [units[i:i + 4] for i in range(0, len(units), 4)]
                out = []

                def mk_batch(bi, bat):
                    st = {}

                    def fa():
                        gs, r_gs = gp()
                        for ui, (j, m) in enumerate(bat):
                            p.mm(gs[:, ui * 128:(ui + 1) * 128], KT[h][0][:, j * 128:(j + 1) * 128],
                                 dqT[b][0][:, h, m, :], True, True, [KT_r[h][j], dqT[b][1]], [r_gs])
                        pt, r_pt = PT[pt_i[0] % 3]
                        pt_i[0] += 1
                        st["pt"] = (pt, r_pt)
                        n = len(bat) * 128
                        p.act(pt[:, 0:n], gs[:, 0:n], AF.Exp, [r_gs], [r_pt], scale=0.125)
                        for ui, (j, m) in enumerate(bat):
                            if j == c:
                                p.tt("pool", pt[:, ui * 128:(ui + 1) * 128], pt[:, ui * 128:(ui + 1) * 128], tabs[:, 5, :], ALU.mult,
                                     [r_pt, r_tabs], [r_pt])

                    def fb():
                        pt, r_pt = st["pt"]
                        for ui, (j, m) in enumerate(bat):
                            p.mm(OAm[m][0][:, 0:130], pt[:, ui * 128:(ui + 1) * 128], VA[h][0][:, j, 0:130],
                                 j == 0, j == c, [r_pt, VA_r[h][j]], [OAm[m][1]])
                    return fa, fb
                pairs = [mk_batch(bi, bat) for bi, bat in enumerate(batches)]
                out.append(pairs[0][0])
                if len(pairs) > 1:
                    out.append(pairs[1][0])
                for bi in range(len(pairs)):
                    if bi + 2 < len(pairs):
                        out.append(pairs[bi + 2][0])
                    out.append(pairs[bi][1])

                def epi():
                    s0 = 4 + 4 * h
                    smt, r_sm = sm[b]
                    for m in range(2):
                        p.ts("dve", smt[:, s0 + m:s0 + m + 1], OAm[m][0][:, 128:129], 1e-30, None, ALU.max, ALU.bypass,
                             [OAm[m][1]], [r_sm])
                        p.recip(smt[:, s0 + m:s0 + m + 1], smt[:, s0 + m:s0 + m + 1], [r_sm], [r_sm])
                        p.act(dtmp[b][0][:, m, :], OAm[m][0][:, 0:128], AF.Identity, [OAm[m][1], r_sm], [dtmp[b][1]],
                              scale=smt[:, s0 + m:s0 + m + 1])
                    p.stt("dve", wda[b][0][:], dtmp[b][0][:, 1, :], nlam[:, 0:1], dtmp[b][0][:, 0, :], ALU.mult, ALU.add,
                          [dtmp[b][1], r_nlam], [wda[b][1]])
                    p.act(dtmp[b][0][:, 0, :], wda[b][0][:], AF.Square, [wda[b][1]], [dtmp[b][1], r_sm], accum_out=smt[:, s0 + 2:s0 + 3])
                    p.act(smt[:, s0 + 2:s0 + 3], smt[:, s0 + 2:s0 + 3], AF.Sqrt, [r_sm], [r_sm], bias=EPS, scale=1.0 / 128)
                    p.recip(smt[:, s0 + 2:s0 + 3], smt[:, s0 + 2:s0 + 3], [r_sm], [r_sm])
                    p.stt("dve", on[b][0][:, 512 + h * 128:512 + (h + 1) * 128], wda[b][0][:], smt[:, s0 + 2:s0 + 3], gsub[:],
                          ALU.mult, ALU.mult, [wda[b][1], r_sm, r_gsub], [on[b][1]])
                out.append(epi)
                return out

            def finish(b=b, c=c):
                def f():
                    for i in range(6):
                        p.tr(TRb[:, i * 128:(i + 1) * 128], on[b][0][:, i * 128:(i + 1) * 128], ident[:], [on[b][1], r_id], [r_TRb])
                    p.act(oT_sb[b][0][:].rearrange("p k t -> p (k t)"), TRb[:, 0:768], AF.Identity, [r_TRb], [oT_sb[b][1]])
                    if oidx is None:
                        p.dma("sp", oT_v[:, :, c * 128:(c + 1) * 128], oT_sb[b][0][:], [oT_sb[b][1]], [r_oT], "st_o%d" % b)
                    else:
                        for k6 in range(6):
                            p.op("pool", lambda e, k6=k6: e.indirect_dma_start(
                                out=io["oT_sh"],
                                out_offset=bass.IndirectOffsetOnAxis(ap=oidx[:, k6, c:c + 1], axis=0),
                                in_=oT_sb[b][0][:, k6, :], in_offset=None, bounds_check=4 * 768 * NCH - 1, oob_is_err=False),
                                [oT_sb[b][1], r_oidx], [r_oT], dma_key="st_o%d" % b)
                return f

            da_list = da_head(0) + da_head(1) + [finish()]
            fr = []
            if ci_ + 1 < len(chunk_list):
                ngi, nji, ncc, nclist = chunk_list[ci_ + 1]
                if nji == 0:
                    fr.append(lambda ngi=ngi, nclist=nclist: group_load(ngi, nclist))
                fr += front_fns(ngi, nji, ncc)
            allsteps = []
            for i_ in range(max(len(steps), len(fr))):
                if i_ < len(steps):
                    allsteps.append(steps[i_])
                if i_ < len(fr):
                    allsteps.append(fr[i_])
            ns, nd = len(allsteps), len(da_list)
            si = 0
            for di, dfn in enumerate(da_list[:-1]):
                while si < ns and si * (nd - 1) <= di * ns:
                    allsteps[si]()
                    si += 1
                dfn()
            while si < ns:
                allsteps[si]()
                si += 1
            da_list[-1]()
    return r_oT


def rms_tile(p, h_t, r_h, n, gcol, r_g, out_t, r_out, ones, r_ones, sq, r_sq, rstd, r_rstd, ps, r_ps):
    p.act(sq[:, :, :n], h_t[:, :, :n], AF.Square, [r_h], [r_sq])
    for k in range(8):
        p.mm(ps[:, :n], ones[:], sq[:, k, :n], k == 0, k == 7, [r_ones, r_sq], [r_ps])
    p.act(rstd[:, :n], ps[:, :n], AF.Sqrt, [r_ps], [r_rstd], bias=EPS, scale=1.0 / D)
    p.recip(rstd[:, :n], rstd[:, :n], [r_rstd], [r_rstd])
    for k in range(8):
        p.stt("dve" if k % 2 == 0 else "pool", out_t[:, k, :n], h_t[:, k, :n], gcol[:, k:k + 1], rstd[:, :n], ALU.mult, ALU.mult,
              [r_h, r_g, r_rstd], [r_out])


TILES = [(0, 128)] + [(128 + i * 342, 342) for i in range(6)]
NT = 342


def phase_A(p, io, tiles=None):
    tiles = tiles or TILES
    ones, r_ones = p.sb([128, 128], BF16, "onesA")
    p.memset("pool", ones[:], 1.0, [r_ones])
    gcol, r_g = p.sb([128, 8], F32, "gcolA")
    p.dma("sp", gcol[:], io["g"], [], [r_g], "c_g")
    xv = io["xT"].rearrange("(k p) t -> p k t", p=128)
    ov = io["hnT"].rearrange("(k p) t -> p k t", p=128)
    r_o = Res("hnT_d")
    bufs = []
    for i in range(2):
        bufs.append((p.sb([128, 8, NT], F32, "hA%d" % i), p.sb([128, 8, NT], BF16, "oA%d" % i), p.sb([128, 8, NT], BF16, "sqA%d" % i),
                     p.sb([128, NT], F32, "rsA%d" % i), p.ps([128, 512], F32, "psA%d" % i)))
    for ti, (t0, n) in enumerate(tiles):
        (h_t, r_h), (o_t, r_ot), (sq, r_sq), (rs, r_rs), (ps, r_ps) = bufs[ti % 2]
        p.dma("sp", h_t[:, :, :n], xv[:, :, t0:t0 + n], [], [r_h], "ldA%d" % (ti % 2))
        rms_tile(p, h_t, r_h, n, gcol, r_g, o_t, r_ot, ones, r_ones, sq, r_sq, rs, r_rs, ps, r_ps)
        p.dma("sp", ov[:, :, t0:t0 + n], o_t[:, :, :n], [r_ot], [r_o], "stA%d" % (ti % 2))
    return [r_o]


def phase_C(p, io, last, tiles=None, ybase=132, hist_reset=(0, 1)):
    nc = p.nc
    tiles = tiles or TILES
    WB, r_WB = p.sb([128, 67584], BF16, "WBUF")
    ones, r_ones = p.sb([128, 128], BF16, "onesC")
    p.memset("pool", ones[:], 1.0, [r_ones])
    gc2, r_gc2 = p.sb([128, 8], F32, "gffn")
    gc3, r_gc3 = p.sb([128, 8], F32, "gnext")
    p.dma("sp", gc2[:], io["g_ffn"], [], [r_gc2], "c_g2")
    p.dma("sp", gc3[:], io["g_next"], [], [r_gc3], "c_g3")
    cw, r_cw = p.sb([128, 44, 4], F32, "convw")
    p.dma("sp", cw[:], io["convp"], [], [r_cw], "c_cw")
    wg = WB[:, 0:24576].rearrange("p (k n) -> p k n", k=8)
    wb = WB[:, 24576:49152].rearrange("p (b k n) -> p b k n", b=3, k=8)
    wo = WB[:, 49152:57344].rearrange("p (k n) -> p k n", k=8)
    wgv = io["wg"].rearrange("(k p) n -> p k n", p=128)
    wbv = io["wb"].rearrange("b (k p) n -> p b k n", p=128)
    wov = io["wo"].rearrange("(k p) n -> p k n", p=128)
    for k in range(8):
        p.dma("pool", wg[:, k, :], wgv[:, k, :], [], [r_WB], "c_W")
    for b_ in range(3):
        for k in range(8):
            p.dma("pool", wb[:, b_, k, :], wbv[:, b_, k, :], [], [r_WB], "c_W")
    for k in range(8):
        p.dma("pool", wo[:, k, :], wov[:, k, :], [], [r_WB], "c_W")

    h_t, r_h = p.sb([128, 8, NT], F32, "hC")
    hn_t, r_hn = p.sb([128, 8, NT], BF16, "hnC")
    big, r_big = p.sb([128, 24 * NT], BF16, "bigC")
    y_t, r_y = p.sb([128, 8, NT], BF16, "yC")
    gt = [p.sb([128, NT], F32, "gt%d" % i) for i in range(2)]
    yacc, r_yacc = p.sb([128, NT], F32, "yacc")
    tmp = [p.sb([128, NT], F32, "tmpC%d" % i) for i in range(2)]
    sq, r_sq = p.sb([128, 8, NT], BF16, "sqC")
    rstd, r_rstd = p.sb([128, NT], F32, "rstdC")
    GPc = [p.ps([128, 512], F32, "GPc%d" % i) for i in range(7)]
    gi_ = [0]

    def gp():
        g = GPc[gi_[0] % 7]
        gi_[0] += 1
        return g
    hv_in = io["hT_in"].rearrange("(k p) t -> p k t", p=128)
    hnv_in = io["hnT_in"].rearrange("(k p) t -> p k t", p=128)
    brv = io["brT"].rearrange("g (k p) t -> p g k t", p=128)
    hmid_v = io["hmidT"].rearrange("(k p) t -> p k t", p=128)
    hn2_v = io["hn2T"].rearrange("(k p) t -> p k t", p=128)
    r_hmid, r_hn2d = Res("hmid_d"), Res("hn2_d")

    for ti, (t0, n) in enumerate(tiles):
        if int(os.environ.get("KC_STOP", "99")) == 0 and ti == 1:
            return [r_hmid, r_hn2d]
        br_t = big[:, 0:24 * n].rearrange("p (g k t) -> p g k t", g=4, k=6)
        p.dma("sp", h_t[:, :, :n], hv_in[:, :, t0:t0 + n], [], [r_h], "ldh")
        p.dma("sp", hn_t[:, :, :n], hnv_in[:, :, t0:t0 + n], [], [r_hn], "ldhn")
        if ti == 0:
            for g in range(4):
                p.dma("sp", br_t[:, g, :, :], brv[:, g, :, t0:t0 + n], [], [r_big], "ldbr")
        for m in range(8):
            ms = slice(m * 128, (m + 1) * 128)
            for nb in range(3):
                gps, r_gps = gp()
                for k in range(8):
                    p.mm(gps[:, :n], wg[:, k, nb * 1024 + m * 128:nb * 1024 + (m + 1) * 128], hn_t[:, k, :n], k == 0, k == 7,
                         [r_WB, r_hn], [r_gps])
                g_, r_g_ = gt[nb % 2]
                p.act(g_[:, :n], gps[:, :n], AF.Sigmoid, [r_gps], [r_g_])
                bps, r_bps = gp()
                for k in range(8):
                    p.mm(bps[:, :n], wb[:, nb, k, ms], br_t[:, k // 2, nb * 2 + k % 2, :], k == 0, k == 7, [r_WB, r_big], [r_bps])
                if nb == 0:
                    p.tt("dve", yacc[:, :n], g_[:, :n], bps[:, :n], ALU.mult, [r_g_, r_bps], [r_yacc])
                else:
                    t_, r_t = tmp[nb % 2]
                    p.tt("dve", t_[:, :n], g_[:, :n], bps[:, :n], ALU.mult, [r_g_, r_bps], [r_t])
                    if nb == 1:
                        p.tt("gps", yacc[:, :n], yacc[:, :n], t_[:, :n], ALU.add, [r_yacc, r_t], [r_yacc])
                    else:
                        p.tt("gps", y_t[:, m, :n], yacc[:, :n], t_[:, :n], ALU.add, [r_yacc, r_t], [r_y])
        if ti + 1 < len(tiles):
            t0n, nn = tiles[ti + 1]
            br_n = big[:, 0:24 * nn].rearrange("p (g k t) -> p g k t", g=4, k=6)
            for g in range(4):
                p.dma("sp", br_n[:, g, :, :], brv[:, g, :, t0n:t0n + nn], [], [r_big], "ldbr")
        for m in range(8):
            ops_, r_ops = gp()
            for k in range(8):
                p.mm(ops_[:, :n], wo[:, k, m * 128:(m + 1) * 128], y_t[:, k, :n], k == 0, k == 7, [r_WB, r_y], [r_ops])
            p.tt("dve", h_t[:, m, :n], h_t[:, m, :n], ops_[:, :n], ALU.add, [r_h, r_ops], [r_h])
        if ti == 0:
            p.memset("pool", h_t[:, :, 0:112], 0.0, [r_h])
        p.dma("sp", hmid_v[:, :, t0:t0 + n], h_t[:, :, :n], [r_h], [r_hmid], "sth")
        ps, r_ps = gp()
        rms_tile(p, h_t, r_h, n, gc2, r_gc2, hn_t, r_hn, ones, r_ones, sq, r_sq, rstd, r_rstd, ps, r_ps)
        p.dma("sp", hn2_v[:, :, t0:t0 + n], hn_t[:, :, :n], [r_hn], [r_hn2d], "sthn")

    KC = int(os.environ.get("KC_STOP", "99"))
    if KC == 1:
        return [r_hmid, r_hn2d]
    wfi = WB[:, 0:45056].rearrange("p (k n) -> p k n", k=8)
    wfo = WB[:, 45056:67584].rearrange("p (k n) -> p k n", k=22)
    wfiv = io["wfi"].rearrange("(k p) n -> p k n", p=128)
    wfov = io["wfo"].rearrange("(k p) n -> p k n", p=128)
    for k in range(8):
        p.dma("pool", wfi[:, k, :], wfiv[:, k, :], [], [r_WB], "c_W")
    for k in range(22):
        p.dma("pool", wfo[:, k, :], wfov[:, k, :], [], [r_WB], "c_W")
    U = [p.sb([128, NT + 2], F32, "U%d" % i) for i in range(2)]
    cb = [p.sb([128, NT], F32, "cb%d" % i) for i in range(2)]
    Hh, r_Hh = p.sb([128, 44, 2], F32, "Hh")
    sg, r_sg = p.sb([128, NT], F32, "sgC")
    r_hout, r_out2 = Res("hout_d"), Res("out2_d")
    hout_v = io["hT_out"].rearrange("(k p) t -> p k t", p=128)
    if last:
        yv = io["yT"].rearrange("(k p) t -> p k t", p=128)
        yo, r_yo = p.sb([128, 8, NT], F32, "yo")
    else:
        hnout_v = io["hnT_out"].rearrange("(k p) t -> p k t", p=128)
    for ti, (t0, n) in enumerate(tiles):
        a_t = big[:, 0:22 * n].rearrange("p (k t) -> p k t", k=22)
        if ti in hist_reset:
            p.memset("pool", Hh[:], 0.0, [r_Hh])
        p.dma("sp", h_t[:, :, :n], hmid_v[:, :, t0:t0 + n], [r_hmid], [r_h], "ldh")
        p.dma("sp", hn_t[:, :, :n], hn2_v[:, :, t0:t0 + n], [r_hn2d], [r_hn], "ldhn")
        for i in range(22):
            for wi, ci in enumerate((i, 22 + i)):
                ups, r_ups = gp()
                for k in range(8):
                    p.mm(ups[:, :n], wfi[:, k, ci * 128:(ci + 1) * 128], hn_t[:, k, :n], k == 0, k == 7, [r_WB, r_hn], [r_ups])
                u_, r_u = U[wi]
                c_, r_c = cb[wi]
                p.act(u_[:, 2:2 + n], ups[:, :n], AF.Identity, [r_ups], [r_u])
                p.act(c_[:, :n], ups[:, :n], AF.Identity, [r_ups, r_cw], [r_c], scale=cw[:, ci, 2:3], bias=cw[:, ci, 3:4])
                p.cp("gps", u_[:, 0:2], Hh[:, ci, :], [r_Hh], [r_u])
                p.stt("dve", c_[:, :n], u_[:, 1:1 + n], cw[:, ci, 1:2], c_[:, :n], ALU.mult, ALU.add, [r_u, r_cw, r_c], [r_c])
                p.stt("dve", c_[:, :n], u_[:, 0:n], cw[:, ci, 0:1], c_[:, :n], ALU.mult, ALU.add, [r_u, r_cw, r_c], [r_c])
                p.cp("gps", Hh[:, ci, :], u_[:, n:n + 2], [r_u], [r_Hh])
            p.act(sg[:, :n], cb[0][0][:, :n], AF.Silu, [cb[0][1]], [r_sg])
            p.tt("gps", a_t[:, i, :], sg[:, :n], cb[1][0][:, :n], ALU.mult, [r_sg, cb[1][1]], [r_big])
        for m in range(8):
            fps, r_fps = gp()
            for k in range(22):
                p.mm(fps[:, :n], wfo[:, k, m * 128:(m + 1) * 128], a_t[:, k, :], k == 0, k == 21, [r_WB, r_big], [r_fps])
            p.tt("dve", h_t[:, m, :n], h_t[:, m, :n], fps[:, :n], ALU.add, [r_h, r_fps], [r_h])
        if ti == 0:
            p.memset("pool", h_t[:, :, 0:112], 0.0, [r_h])
        p.dma("sp", hout_v[:, :, t0:t0 + n], h_t[:, :, :n], [r_h], [r_hout], "sth")
        ps, r_ps = gp()
        if last:
            lo = max(0, ybase - t0)
            if lo < n:
                rms_tile(p, h_t, r_h, n, gc3, r_gc3, yo, r_yo, ones, r_ones, sq, r_sq, rstd, r_rstd, ps, r_ps)
                g0 = t0 + lo - ybase
                p.dma("sp", yv[:, :, g0:g0 + n - lo], yo[:, :, lo:n], [r_yo], [r_out2], "sty")
        else:
            rms_tile(p, h_t, r_h, n, gc3, r_gc3, hn_t, r_hn, ones, r_ones, sq, r_sq, rstd, r_rstd, ps, r_ps)
            p.dma("sp", hnout_v[:, :, t0:t0 + n], hn_t[:, :, :n], [r_hn], [r_out2], "sthn")
    return [r_hout, r_out2, r_hmid, r_hn2d]


def _dram(nc, name, shape, dt, kind):
    return nc.dram_tensor(name, list(shape), dt, kind=kind).ap()


def build_A():
    nc = bass.Bass("TRN2", target_bir_lowering=False)
    io = {"xT": _dram(nc, "xT", [D, TLOC], F32, "ExternalInput"), "g": _dram(nc, "g", [128, 8], F32, "ExternalInput"),
          "hnT": _dram(nc, "hnT", [D, TLOC], BF16, "ExternalOutput")}
    p = Prog(nc)
    outs = phase_A(p, io)
    p.wait_only("sp", [r.lw for r in outs])
    p.emit()
    return nc


def build_B(li, nchunks=NCH):
    nc = bass.Bass("TRN2", target_bir_lowering=False)
    I = "ExternalInput"
    io = {"hn_meta": _dram(nc, "hn_meta", [D, 128], BF16, I), "hn_all": _dram(nc, "hn_all", [4, D, 2048], BF16, I),
          "w": _dram(nc, "w", [D, NBW], F32, I), "rope": _dram(nc, "rope", [LTOT, 288], F32, I),
          "tabs": _dram(nc, "tabs", [8, 128, 128], F32, I), "cols": _dram(nc, "cols", [128, 8], F32, I),
          "vcol": _dram(nc, "vcol", [128, NCH], F32, I), "hg_lb": _dram(nc, "hg_lb", [DEPTH, 256], F32, I),
          "da_lambda": _dram(nc, "da_lambda", [1, 256], F32, I), "subln": _dram(nc, "subln", [1, 128], F32, I),
          "ident": _dram(nc, "ident", [128, 128], F32, I),
          "oT": _dram(nc, "oT", [768, LTOT], BF16, "ExternalOutput")}
    p = Prog(nc)
    r_o = phase_B(p, li, io, nchunks)
    p.wait_only("sp", [r_o.lw])
    p.emit()
    return nc


def build_C(last):
    nc = bass.Bass("TRN2", target_bir_lowering=False)
    I = "ExternalInput"
    O = "ExternalOutput"
    io = {"hT_in": _dram(nc, "hT_in", [D, TLOC], F32, I), "hnT_in": _dram(nc, "hnT_in", [D, TLOC], BF16, I),
          "brT": _dram(nc, "brT", [4, 768, TLOC], BF16, I), "wg": _dram(nc, "wg", [D, 3072], F32, I),
          "wb": _dram(nc, "wb", [3, D, D], F32, I), "wo": _dram(nc, "wo", [D, D], F32, I),
          "g_ffn": _dram(nc, "g_ffn", [128, 8], F32, I), "g_next": _dram(nc, "g_next", [128, 8], F32, I),
          "convp": _dram(nc, "convp", [128, 44, 4], F32, I), "wfi": _dram(nc, "wfi", [D, 2 * DFF], F32, I),
          "wfo": _dram(nc, "wfo", [DFF, D], F32, I),
          "hmidT": _dram(nc, "hmidT", [D, TLOC], F32, "Internal"), "hn2T": _dram(nc, "hn2T", [D, TLOC], BF16, "Internal"),
          "hT_out": _dram(nc, "hT_out", [D, TLOC], F32, O)}
    if last:
        io["yT"] = _dram(nc, "yT", [D, NLOC * 128], F32, O)
    else:
        io["hnT_out"] = _dram(nc, "hnT_out", [D, TLOC], BF16, O)
    p = Prog(nc)
    outs = phase_C(p, io, last)
    p.wait_only("sp", [r.lw for r in outs])
    p.emit()
    return nc


def const_tables():
    f32 = np.float32
    pos = (np.arange(LTOT) - 112).astype(f32)
    rope = np.zeros((LTOT, 288), f32)
    inv = (10000.0 ** (-np.arange(0, 128, 2, dtype=f32) / 128)).astype(f32)
    ang = pos[:, None] * inv[None, :]
    rope[:, 0:64] = np.cos(ang)
    rope[:, 64:128] = np.sin(ang)
    rope[:, 128:192] = -np.sin(ang)
    inv = (10000.0 ** (-np.arange(0, 64, 2, dtype=f32) / 64)).astype(f32)
    ang = pos[:, None] * inv[None, :]
    rope[:, 192:224] = np.cos(ang)
    rope[:, 224:256] = np.sin(ang)
    rope[:, 256:288] = -np.sin(ang)
    idx = np.arange(128)
    tabs_h, cols_h = [], []
    same = (idx[:, None] // 32) == (idx[None, :] // 32)
    for hd in range(4):
        log_g = np.log1p(-np.exp2(-5.0 - hd))
        tabs = np.zeros((8, 128, 128), f32)
        gap = idx[None, :] - idx[:, None]
        tabs[0] = np.where(gap >= 0, np.exp(log_g * np.maximum(gap, 0)), 0.0) * 128 ** -0.5
        tabs[1] = np.exp(log_g * (idx[None, :] + 1.0)) * np.ones((128, 1))
        tabs[2] = (same & (idx[:, None] <= idx[None, :])).astype(f32)
        tabs[3] = (same & (idx[:, None] > idx[None, :])).astype(f32)
        tabs[4] = (same & (idx[None, :] >= idx[:, None])).astype(f32)
        tabs[5] = (idx[None, :] >= idx[:, None]).astype(f32)
        cols = np.zeros((128, 8), f32)
        cols[:, 0] = np.exp(log_g * (127.0 - idx)) * 128 ** -0.5
        cols[:, 1] = np.exp(log_g * 128.0)
        for j in range(4):
            cols[:, 2 + j] = (idx // 32 == j)
        tabs_h.append(tabs)
        cols_h.append(cols)
    vcol = np.ones((128, NCH), f32)
    vcol[:112, 0] = 0.0
    return rope, tabs_h, cols_h, vcol


def gcols(g):
    return np.ascontiguousarray(np.asarray(g, np.float32).reshape(8, 128).T)


def w_group(w_in_l, g):
    s = lambda off, width: w_in_l[:, off + g * width: off + (g + 1) * width]
    parts = [s(0, 128), s(512, 128), s(1024, 256), s(2048, 256), s(6144, 256), s(3072, 256), s(4096, 256), s(5120, 256),
             s(9216, 256), s(7168, 256), s(8192, 256)]
    return np.ascontiguousarray(np.concatenate(parts, axis=1))


_NC_CACHE = {}
_DBG = None


def _get(name, fn):
    if name not in _NC_CACHE:
        _NC_CACHE[name] = fn()
    return _NC_CACHE[name]


def local_tokens(q):
    base = 128 + q * 2048
    return np.concatenate([np.arange(128), np.arange(base - HALO, base), np.arange(base, base + 2048)])


def kernel(x, meta, norm_mix_g, w_in, w_branch, w_out, hg_lb, da_lambda, da_subln_g, norm_ffn_g, w_ffn_in,
           ffn_conv_w, ffn_conv_b, w_ffn_out, norm_final_g):
    f32 = np.float32
    bf = ml_dtypes.bfloat16
    A = lambda a: np.asarray(a, f32)
    x, meta, w_in, w_branch, w_out = A(x), A(meta), A(w_in), A(w_branch), A(w_out)
    w_ffn_in, w_ffn_out, ffn_conv_w, ffn_conv_b = A(w_ffn_in), A(w_ffn_out), A(ffn_conv_w), A(ffn_conv_b)
    hg_lb, da_lambda, da_subln_g = A(hg_lb), A(da_lambda), A(da_subln_g)
    rope, tabs_h, cols_h, vcol = const_tables()
    ident = np.eye(128, dtype=f32)
    cores = list(range(8))
    hfull = np.zeros((2, LTOT, D), f32)
    hfull[:, 112:128] = meta[None]
    hfull[:, 128:] = x
    loc = [local_tokens(q) for q in range(4)]
    in_maps = [{"xT": np.ascontiguousarray(hfull[c // 4][loc[c % 4]].T), "g": gcols(norm_mix_g[0])} for c in cores]
    hT = [m["xT"] for m in in_maps]
    res = run_bass_kernel_spmd(_get("A", build_A), in_maps, core_ids=cores)
    hnT = [np.asarray(r["hnT"]) for r in res.results]
    if _DBG is not None:
        _DBG["hnT_A"] = hnT
    out = np.zeros((2, SEQ, D), f32)
    for li in range(DEPTH):
        last = li == DEPTH - 1
        in_maps = []
        for c in cores:
            b, g = c // 4, c % 4
            in_maps.append({
                "hn_meta": np.ascontiguousarray(hnT[b * 4][:, 0:128]),
                "hn_all": np.ascontiguousarray(np.stack([hnT[b * 4 + q][:, 132:] for q in range(4)])),
                "w": w_group(w_in[li], g), "rope": rope, "tabs": tabs_h[g], "cols": cols_h[g], "vcol": vcol,
                "hg_lb": np.ascontiguousarray(hg_lb[:, g * 256:(g + 1) * 256]),
                "da_lambda": np.ascontiguousarray(da_lambda[li].reshape(1, 256)),
                "subln": np.ascontiguousarray(da_subln_g[li].reshape(1, 128)), "ident": ident})
        res = run_bass_kernel_spmd(_get("B%d" % li, lambda: build_B(li)), in_maps, core_ids=cores)
        oT = [np.asarray(r["oT"]) for r in res.results]
        if _DBG is not None:
            _DBG["oT%d" % li] = oT
        convp = np.concatenate([ffn_conv_w[li].T, ffn_conv_b[li][:, None]], axis=1)
        convp = np.ascontiguousarray(convp.reshape(44, 128, 4).transpose(1, 0, 2))
        g_next = norm_final_g if last else norm_mix_g[li + 1]
        in_maps = []
        for c in cores:
            b, q = c // 4, c % 4
            in_maps.append({
                "hT_in": hT[c], "hnT_in": hnT[c],
                "brT": np.ascontiguousarray(np.stack([oT[b * 4 + g][:, loc[q]] for g in range(4)])),
                "wg": np.ascontiguousarray(w_in[li][:, 10240:13312]), "wb": w_branch[li], "wo": w_out[li],
                "g_ffn": gcols(norm_ffn_g[li]), "g_next": gcols(g_next), "convp": convp,
                "wfi": w_ffn_in[li], "wfo": w_ffn_out[li]})
        res = run_bass_kernel_spmd(_get("C%d" % int(last), lambda: build_C(last)), in_maps, core_ids=cores)
        hT = [np.asarray(r["hT_out"]) for r in res.results]
        if _DBG is not None:
            _DBG["hT%d" % li] = hT
        if last:
            for c in cores:
                out[c // 4, (c % 4) * 2048:(c % 4 + 1) * 2048] = np.asarray(res.results[c]["yT"]).T
        else:
            hnT = [np.asarray(r["hnT_out"]) for r in res.results]
    return out


def phase_X(p, priv, sh2, oidx):
    bufs = [p.sb([128, LTOT], BF16, "xb%d" % i) for i in range(2)]
    r_sh = Res("sh")
    n = 0
    for j in range(NGL):
        ix, r_ix = p.sb([128, 6], mybir.dt.int32, "xi%d" % j)
        p.dma("sp", ix[:], oidx[j], [], [r_ix], "ldxi%d" % j)
        pv = priv[j].rearrange("(k p) t -> p k t", p=128)
        for k6 in range(6):
            buf, r_buf = bufs[n % 2]
            p.dma("sp", buf[:], pv[:, k6, :], [], [r_buf], "ldx%d" % (n % 2))
            p.op("pool", lambda e, buf=buf, ix=ix, k6=k6: e.indirect_dma_start(
                out=sh2, out_offset=bass.IndirectOffsetOnAxis(ap=ix[:, k6:k6 + 1], axis=0), in_=buf[:], in_offset=None,
                bounds_check=4 * 768 - 1, oob_is_err=False), [r_buf, r_ix], [r_sh], dma_key="scx%d" % (n % 2))
            n += 1


NTF = 320
TILES_F = [(i * NTF, NTF) for i in range(LTOT // NTF)]


PAIR = os.environ.get("K_PAIR", "1") == "1"
NGL = 2 if PAIR else 4


def build_fused():
    nc = bass.Bass("TRN2", target_bir_lowering=False, num_devices=4) if PAIR else bass.Bass("TRN2", target_bir_lowering=False)
    I = "ExternalInput"
    ext = {}

    def inp(name, shape, dt=F32):
        ext[name] = _dram(nc, name, shape, dt, I)
        return ext[name]
    xT = inp("xT", [D, LTOT])
    rope = inp("rope", [LTOT, 288])
    vcol = inp("vcol", [128, NCH])
    ident = inp("ident", [128, 128])
    tabs = [inp("tabs%d" % g, [8, 128, 128]) for g in range(NGL)]
    cols = [inp("cols%d" % g, [128, 8]) for g in range(NGL)]
    hglb = [inp("hg_lb%d" % g, [DEPTH, 256]) for g in range(NGL)]
    oidx = [inp("oidx%d" % g, [128, 6], mybir.dt.int32) for g in range(NGL)] if PAIR else None
    gmix = [inp("g_mix%d" % l, [128, 8]) for l in range(DEPTH)]
    gfin = inp("g_fin", [128, 8])
    L = []
    for l in range(DEPTH):
        L.append({"w": [inp("w%d_%d" % (l, g), [D, NBW]) for g in range(NGL)],
                  "da_lambda": inp("da_lambda%d" % l, [1, 256]), "subln": inp("subln%d" % l, [1, 128]),
                  "wg": inp("wg%d" % l, [D, 3072]), "wb": inp("wb%d" % l, [3, D, D]), "wo": inp("wo%d" % l, [D, D]),
                  "g_ffn": inp("g_ffn%d" % l, [128, 8]), "convp": inp("convp%d" % l, [128, 44, 4]),
                  "wfi": inp("wfi%d" % l, [D, 2 * DFF]), "wfo": inp("wfo%d" % l, [DFF, D])})
    yT = _dram(nc, "yT", [D, SEQ], F32, "ExternalOutput")
    hnT = _dram(nc, "hnT_i", [D, LTOT], BF16, "Internal")
    if PAIR:
        brT = nc.dram_tensor("brT_sh", [4, 768, LTOT], BF16, addr_space="Shared").ap()
        brT2 = brT.rearrange("g f t -> (g f) t")
        brP = _dram(nc, "brP_i", [NGL, 768, LTOT], BF16, "Internal")
    else:
        brT = _dram(nc, "brT_i", [4, 768, LTOT], BF16, "Internal")
    hT = _dram(nc, "hT_i", [D, LTOT], F32, "Internal")
    hmidT = _dram(nc, "hmidT_i", [D, LTOT], F32, "Internal")
    hn2T = _dram(nc, "hn2T_i", [D, LTOT], BF16, "Internal")
    counts = []

    fstop = int(os.environ.get("K_FSTOP", "99"))

    def close(p):
        if len(counts) >= fstop:
            p.stack.close()
            counts.append(None)
            return
        p.finish()
        counts.append({e: len(v) for e, v in p.ops.items()})
        p.emit()
        nc.all_engine_barrier()
    p = Prog(nc)
    phase_A(p, {"xT": xT, "g": gmix[0], "hnT": hnT}, TILES_F)
    close(p)
    for l in range(DEPTH):
        last = l == DEPTH - 1
        for g in range(NGL):
            p = Prog(nc)
            iob = {"hn_full": hnT, "w": L[l]["w"][g], "rope": rope, "tabs": tabs[g], "cols": cols[g], "vcol": vcol,
                   "hg_lb": hglb[g], "da_lambda": L[l]["da_lambda"], "subln": L[l]["subln"], "ident": ident}
            iob["oT"] = brP[g] if PAIR else brT[g]
            phase_B(p, l, iob)
            close(p)
        if PAIR:
            p = Prog(nc)
            phase_X(p, brP, brT2, oidx)
            close(p)
            nc.all_core_barrier()
        p = Prog(nc)
        io = {"hT_in": xT if l == 0 else hT, "hnT_in": hnT, "brT": brT, "wg": L[l]["wg"], "wb": L[l]["wb"], "wo": L[l]["wo"],
              "g_ffn": L[l]["g_ffn"], "g_next": gfin if last else gmix[l + 1], "convp": L[l]["convp"], "wfi": L[l]["wfi"],
              "wfo": L[l]["wfo"], "hmidT": hmidT, "hn2T": hn2T, "hT_out": hT}
        if last:
            io["yT"] = yT
        else:
            io["hnT_out"] = hnT
        phase_C(p, io, last, TILES_F, ybase=128, hist_reset=(0,))
        close(p)
        if PAIR and not last:
            nc.all_core_barrier()
    print("fused program op counts per phase:", counts)
    return nc


def kernel_fused(x, meta, norm_mix_g, w_in, w_branch, w_out, hg_lb, da_lambda, da_subln_g, norm_ffn_g, w_ffn_in,
                 ffn_conv_w, ffn_conv_b, w_ffn_out, norm_final_g):
    f32 = np.float32
    A = lambda a: np.asarray(a, f32)
    x, meta, w_in, w_branch, w_out = A(x), A(meta), A(w_in), A(w_branch), A(w_out)
    w_ffn_in, w_ffn_out, ffn_conv_w, ffn_conv_b = A(w_ffn_in), A(w_ffn_out), A(ffn_conv_w), A(ffn_conv_b)
    hg_lb, da_lambda, da_subln_g = A(hg_lb), A(da_lambda), A(da_subln_g)
    rope, tabs_h, cols_h, vcol = const_tables()
    shared = {"rope": rope, "vcol": vcol, "ident": np.eye(128, dtype=f32), "g_fin": gcols(norm_final_g)}
    percore = [dict() for _ in range(2)]
    for e in range(2 if PAIR else 1):
        for j in range(NGL):
            g = NGL * e + j
            percore[e]["tabs%d" % j] = tabs_h[g]
            percore[e]["cols%d" % j] = cols_h[g]
            percore[e]["hg_lb%d" % j] = np.ascontiguousarray(hg_lb[:, g * 256:(g + 1) * 256])
            if PAIR:
                percore[e]["oidx%d" % j] = np.ascontiguousarray(
                    (g * 768 + np.arange(6)[None, :] * 128 + np.arange(128)[:, None]).astype(np.int32))
            for l in range(DEPTH):
                percore[e]["w%d_%d" % (l, j)] = w_group(w_in[l], g)
    for l in range(DEPTH):
        shared["g_mix%d" % l] = gcols(norm_mix_g[l])
        shared["da_lambda%d" % l] = np.ascontiguousarray(da_lambda[l].reshape(1, 256))
        shared["subln%d" % l] = np.ascontiguousarray(da_subln_g[l].reshape(1, 128))
        shared["wg%d" % l] = np.ascontiguousarray(w_in[l][:, 10240:13312])
        shared["wb%d" % l] = w_branch[l]
        shared["wo%d" % l] = w_out[l]
        shared["g_ffn%d" % l] = gcols(norm_ffn_g[l])
        convp = np.concatenate([ffn_conv_w[l].T, ffn_conv_b[l][:, None]], axis=1)
        shared["convp%d" % l] = np.ascontiguousarray(convp.reshape(44, 128, 4).transpose(1, 0, 2))
        shared["wfi%d" % l] = w_ffn_in[l]
        shared["wfo%d" % l] = w_ffn_out[l]
    in_maps = []
    npc = 2 if PAIR else 1
    for b in range(2):
        hfull = np.zeros((LTOT, D), f32)
        hfull[112:128] = meta
        hfull[128:] = x[b]
        xT = np.ascontiguousarray(hfull.T)
        for e in range(npc):
            m = dict(shared)
            m.update(percore[e])
            m["xT"] = xT
            in_maps.append(m)
    res = run_bass_kernel_spmd(_get("F", build_fused), in_maps, core_ids=list(range(2 * npc)))
    return np.stack([np.ascontiguousarray(np.asarray(res.results[b * npc]["yT"]).T) for b in range(2)])


kernel_unfused = kernel
if os.environ.get("K_FUSED", "1") == "1":
    kernel = kernel_fused
```

```python
from contextlib import ExitStack
import math
import numpy as np
import ml_dtypes
import concourse.bass as bass
import concourse.mybir as mybir
from concourse.bass_utils import run_bass_kernel_spmd

F32 = mybir.dt.float32
BF16 = mybir.dt.bfloat16
ALU = mybir.AluOpType
AF = mybir.ActivationFunctionType

D = 1024
SEQ = 8192
NCH = 65
LTOT = NCH * 128
NLOC = 16
HALO = 4
TLOC = 128 + HALO + NLOC * 128
DFF = 2816
EPS = 1e-6
DEPTH = 2
NBW = 2560


class Res:
    __slots__ = ("name", "lw", "rd", "excl")

    def __init__(self, name="r", excl=False):
        self.name = name
        self.lw = None
        self.rd = {}
        self.excl = excl


import os
NO_POOL = os.environ.get("NO_POOL", "1") == "1"


class Prog:
    ENG = ("pe", "act", "dve", "pool", "sp")

    _uid = [0]

    def __init__(self, nc):
        Prog._uid[0] += 1
        self.pfx = "P%d_" % Prog._uid[0]
        self.nc = nc
        self.stack = ExitStack()
        self.ops = {e: [] for e in self.ENG}
        self.cnt = {e: 0 for e in self.ENG}
        self.seen = {e: {} for e in self.ENG}
        self.dma_cnt = {}
        self.n = 0

    def sb(self, shape, dt, name=None):
        self.n += 1
        name = self.pfx + (name or ("sb%d" % self.n))
        t = self.stack.enter_context(self.nc.sbuf_tensor(name, list(shape), dt))
        return t, Res(name)

    def ps(self, shape, dt, name=None):
        self.n += 1
        name = self.pfx + (name or ("ps%d" % self.n))
        t = self.stack.enter_context(self.nc.psum_tensor(name, list(shape), dt))
        return t, Res(name, excl=True)

    def op(self, eng, fn, reads=(), writes=(), dma_key=None):
        if eng == "pool" and dma_key is None and NO_POOL:
            eng = "dve"
        if eng == "gps":
            eng = "pool"
        deps = []
        for r in reads:
            if r.lw is not None:
                deps.append(r.lw)
            if r.excl:
                deps.extend((k, v) for k, v in r.rd.items() if k != eng)
        for w in writes:
            if w.lw is not None:
                deps.append(w.lw)
            deps.extend(w.rd.items())
        if dma_key is None:
            self.cnt[eng] += 1
            tok = (eng, self.cnt[eng])
        else:
            c = self.dma_cnt.get(dma_key, 0) + 16
            self.dma_cnt[dma_key] = c
            tok = (dma_key, c)
        waits = {}
        seen = self.seen[eng]
        for (k, v) in deps:
            if eng == "pe" and k == "pe":
                continue
            if seen.get(k, 0) >= v:
                continue
            if waits.get(k, 0) < v:
                waits[k] = v
        for k, v in waits.items():
            seen[k] = v
        self.ops[eng].append((fn, waits, tok, dma_key is not None))
        for r in reads:
            if r.rd.get(tok[0], 0) < tok[1]:
                r.rd[tok[0]] = tok[1]
        for w in writes:
            w.lw = tok
            w.rd = {}
        return tok

    def wait_only(self, eng, toks):
        waits = {}
        for (k, v) in toks:
            if self.seen[eng].get(k, 0) >= v:
                continue
            waits[k] = max(waits.get(k, 0), v)
        for k, v in waits.items():
            self.seen[eng][k] = v
        self.ops[eng].append((None, waits, None, False))

    def val(self, eng, name, fn, reads, store):
        waits = {}
        for r in reads:
            if r.lw is not None and self.seen[eng].get(r.lw[0], 0) < r.lw[1]:
                waits[r.lw[0]] = r.lw[1]
        for k, v in waits.items():
            self.seen[eng][k] = v

        def run(e):
            store[name] = fn(e)
            return None
        self.ops[eng].append((run, waits, None, False))

    def dma_dyn(self, eng, apfn, reads, writes, key, **kw):
        def run(e):
            o_, i_ = apfn()
            return e.dma_start(out=o_, in_=i_, **kw)
        return self.op(eng, run, reads, writes, dma_key=key)

    def finish(self):
        self.wait_only("sp", list(self.dma_cnt.items()))

    def dma(self, eng, out, in_, reads, writes, key, **kw):
        return self.op(eng, lambda e: e.dma_start(out=out, in_=in_, **kw), reads, writes, dma_key=key)

    def mm(self, out, lhsT, rhs, start, stop, reads, writes):
        return self.op("pe", lambda e: e.matmul(out, lhsT=lhsT, rhs=rhs, start=start, stop=stop), reads, writes)

    def tr(self, out, in_, ident, reads, writes):
        return self.op("pe", lambda e: e.transpose(out, in_, ident), reads, writes)

    def act(self, out, in_, func, reads, writes, **kw):
        return self.op("act", lambda e: e.activation(out=out, in_=in_, func=func, **kw), reads, writes)

    def tt(self, eng, out, in0, in1, op, reads, writes):
        return self.op(eng, lambda e: e.tensor_tensor(out=out, in0=in0, in1=in1, op=op), reads, writes)

    def ts(self, eng, out, in0, s1, s2, op0, op1, reads, writes):
        return self.op(eng, lambda e: e.tensor_scalar(out=out, in0=in0, scalar1=s1, scalar2=s2, op0=op0, op1=op1), reads, writes)

    def stt(self, eng, out, in0, scalar, in1, op0, op1, reads, writes):
        eng = "dve"
        return self.op(eng, lambda e: e.scalar_tensor_tensor(out=out, in0=in0, scalar=scalar, in1=in1, op0=op0, op1=op1), reads, writes)

    def cp(self, eng, out, in_, reads, writes):
        if eng == "act":
            return self.op("act", lambda e: e.copy(out=out, in_=in_), reads, writes)
        return self.op(eng, lambda e: e.tensor_copy(out=out, in_=in_), reads, writes)

    def memset(self, eng, ap, val, writes):
        return self.op(eng, lambda e: e.memset(ap, val), [], writes)

    def recip(self, out, in_, reads, writes):
        return self.op("dve", lambda e: e.reciprocal(out=out, in_=in_), reads, writes)

    def emit(self):
        nc = self.nc
        sems = {}
        for e in ("pe", "act", "dve", "pool"):
            sems[e] = nc.alloc_semaphore(name=self.pfx + "s_" + e)
        for k in self.dma_cnt:
            sems[k] = nc.alloc_semaphore(name=self.pfx + "d_" + str(k))
        ops = self.ops

        def mk(name):
            def body(eng):
                for fn, waits, tok, is_dma in ops[name]:
                    for k, v in waits.items():
                        eng.wait_ge(sems[k], v)
                    if fn is None:
                        continue
                    ins = fn(eng)
                    if ins is not None and tok is not None:
                        ins.then_inc(sems[tok[0]], 16 if is_dma else 1)
            return body

        with nc.Block() as block:
            block.tensor(mk("pe"))
            block.scalar(mk("act"))
            block.vector(mk("dve"))
            block.gpsimd(mk("pool"))
            block.sync(mk("sp"))
        self.stack.close()
        nc.all_engine_barrier()
        nc.clear_and_free_semaphores(list(sems.values()))
        nc.all_engine_barrier()


def rawap(t, extra):
    return bass.AP(t.tensor, t.offset, [list(t.ap[0])] + [list(x) for x in extra])


def phase_B(p, li, io, nchunks=NCH):
    nc = p.nc
    lam_init = 0.8 - 0.6 * math.exp(-0.3 * li)
    R = Res
    ident, r_id = p.sb([128, 128], BF16, "identB")
    p.dma("pool", ident[:], io["ident"], [], [r_id], "c_id")
    W, r_W = p.sb([128, 8, NBW], BF16, "Wg")
    wv = io["w"].rearrange("(k p) n -> p k n", p=128)
    for k in range(8):
        p.dma("pool", W[:, k, :], wv[:, k, :], [], [r_W], "c_w")
    tabs, r_tabs = p.sb([128, 6, 128], F32, "tabsB")
    p.dma("sp", tabs[:], io["tabs"][0:6].rearrange("k p n -> p k n"), [], [r_tabs], "c_tabs")
    cols, r_cols = p.sb([128, 8], F32, "colsB")
    p.dma("sp", cols[:], io["cols"], [], [r_cols], "c_cols")
    vcol, r_vcol = p.sb([128, NCH], F32, "vcolB")
    p.dma("sp", vcol[:], io["vcol"], [], [r_vcol], "c_vcol")
    lbr, r_lbr = p.sb([128, DEPTH, 256], F32, "lbr")
    for d_ in range(DEPTH):
        p.dma("sp", lbr[:, d_, :], io["hg_lb"][d_:d_ + 1, :].partition_broadcast(128), [], [r_lbr], "c_lb")
    p.act(lbr[:], lbr[:], AF.Exp, [r_lbr], [r_lbr])
    lsum, r_lsum = p.sb([128, 256], F32, "lsum")
    p.tt("dve", lsum[:], lbr[:, 0, :], lbr[:, 1, :], ALU.add, [r_lbr], [r_lsum])
    p.recip(lsum[:], lsum[:], [r_lsum], [r_lsum])
    lb, r_lb = p.sb([128, 256], F32, "lb")
    oml, r_oml = p.sb([128, 256], F32, "oml")
    p.memset("dve", lb[:], 0.0, [r_lb])
    for d_ in range(li + 1):
        p.stt("dve", lb[:], lbr[:, d_, :], 1.0, lb[:], ALU.mult, ALU.add, [r_lbr, r_lb], [r_lb])
    p.stt("dve", lb[:], lbr[:, 0, :], -1.0, lb[:], ALU.mult, ALU.add, [r_lbr, r_lb], [r_lb])
    p.tt("dve", lb[:], lb[:], lsum[:], ALU.mult, [r_lb, r_lsum], [r_lb])
    p.ts("dve", oml[:], lb[:], -1.0, 1.0, ALU.mult, ALU.add, [r_lb], [r_oml])
    lp, r_lp = p.sb([128, 4, 64], F32, "lp")
    p.dma("sp", lp[:].rearrange("p a d -> p (a d)"), io["da_lambda"].partition_broadcast(128), [], [r_lp], "c_lp")
    lpp, r_lpp = p.sb([128, 2, 64], F32, "lpp")
    p.tt("dve", lpp[:, 0, :], lp[:, 0, :], lp[:, 1, :], ALU.mult, [r_lp], [r_lpp])
    p.tt("dve", lpp[:, 1, :], lp[:, 2, :], lp[:, 3, :], ALU.mult, [r_lp], [r_lpp])
    lsm, r_lsm = p.sb([128, 2], F32, "lsm")
    p.op("dve", lambda e: e.reduce_sum(out=lsm[:], in_=lpp[:], axis=mybir.AxisListType.X), [r_lpp], [r_lsm])
    p.act(lsm[:], lsm[:], AF.Exp, [r_lsm], [r_lsm])
    nlam, r_nlam = p.sb([128, 1], F32, "nlam")
    p.tt("dve", nlam[:], lsm[:, 1:2], lsm[:, 0:1], ALU.subtract, [r_lsm], [r_nlam])
    p.ts("dve", nlam[:], nlam[:], -lam_init, None, ALU.add, ALU.bypass, [r_nlam], [r_nlam])
    gsub, r_gsub = p.sb([128, 128], F32, "gsub")
    p.dma("sp", gsub[:], io["subln"].partition_broadcast(128), [], [r_gsub], "c_gs")
    p.ts("dve", gsub[:], gsub[:], 1.0 - lam_init, None, ALU.mult, ALU.bypass, [r_gsub], [r_gsub])

    KT = []
    VA = []
    KT_r = [[Res("KT%d_%d" % (h, c)) for c in range(NCH)] for h in range(2)]
    VA_r = [[Res("VA%d_%d" % (h, c)) for c in range(NCH)] for h in range(2)]
    for h in range(2):
        KT.append(p.sb([128, LTOT], BF16, "KT%d" % h))
        VA.append(p.sb([128, NCH, 130], BF16, "VA%d" % h))
        p.memset("dve", VA[h][0][:], 0.0, VA_r[h])
    S_ret, r_Sret = p.sb([128, 256], F32, "S_ret")
    Sb_ret, r_Sbret = p.sb([128, 256], BF16, "Sb_ret")
    p.memset("pool", S_ret[:], 0.0, [r_Sret])
    p.memset("pool", Sb_ret[:], 0.0, [r_Sbret])
    S_hg = []
    Sb_hg = []
    for h in range(2):
        s_, r_ = p.sb([128, 128], F32, "S_hg%d" % h)
        p.memset("pool", s_[:], 0.0, [r_])
        S_hg.append((s_, r_))
        lst = []
        for par in range(2):
            ring = []
            for j in range(4):
                sb_, rb_ = p.sb([128, 128], BF16, "Sb_hg%d_%d_%d" % (h, par, j))
                p.memset("pool", sb_[:], 0.0, [rb_])
                ring.append((sb_, rb_))
            lst.append(ring)
        Sb_hg.append(lst)
    QZ = []
    for h in range(2):
        q_, r_ = p.sb([128, 4, 128], BF16, "QZ%d" % h)
        p.memset("pool", q_[:], 0.0, [r_])
        QZ.append((q_, r_))

    hnb = [p.sb([128, 8, 512], BF16, "hnb%d" % i) for i in range(2)]
    rtb = [p.sb([128, 4, 288], F32, "rtb%d" % i) for i in range(2)]
    GP = [p.ps([128, 512], F32, "GP%d" % i) for i in range(4)]
    gp_i = [0]

    def gp():
        g = GP[gp_i[0] % 4]
        gp_i[0] += 1
        return g
    TRb, r_TRb = p.ps([128, 1024], BF16, "TRb")
    OArh, r_OArh = p.ps([128, 512], F32, "OArh")
    SUr, r_SUr = p.ps([128, 512], F32, "SUr")
    OAda, r_OAda = p.ps([128, 512], F32, "OAda")
    OAm = [(OAda, r_OAda), (SUr, r_SUr)]

    def dbl(shape, dt, name):
        return [p.sb(shape, dt, "%s_%d" % (name, i)) for i in range(2)]
    qk_t1 = dbl([128, 256], F32, "qk_t1")
    qk_t2 = dbl([128, 256], F32, "qk_t2")
    qk_r = dbl([128, 256], BF16, "qk_r")
    kd = dbl([128, 128], BF16, "kd")
    Vr = dbl([128, 256], BF16, "Vr")
    G = dbl([128, 512], F32, "G")
    sig = dbl([128, 256], F32, "sig")
    lf = dbl([128, 256], F32, "lf")
    kk = dbl([128, 256], F32, "kk")
    hq = dbl([128, 256], F32, "hq")
    hv = dbl([128, 256], BF16, "hv")
    _t1 = p.sb([128, 512], F32, "dq_t1")
    _t2 = p.sb([128, 512], F32, "dq_t2")
    dq_t1 = [_t1, _t1]
    dq_t2 = [_t2, _t2]
    dqk = dbl([128, 512], BF16, "dqk")
    rT = dbl([128, 3, 128], BF16, "rT")
    dqT = dbl([128, 2, 2, 128], BF16, "dqT")
    for i_ in range(2):
        p.memset("dve", dqT[i_][0][:], 0.0, [dqT[i_][1]])
    AT_r = dbl([128, 128], BF16, "AT_r")
    eq = dbl([128, 256], F32, "eq")
    qt = dbl([128, 256], BF16, "qt")
    kt = dbl([128, 256], BF16, "kt")
    kbar = dbl([128, 256], F32, "kbar")
    kbZ = dbl([128, 2, 4, 128], BF16, "kbZ")
    hT = dbl([128, 2, 128], BF16, "hT")
    eqT = dbl([128, 2, 128], BF16, "eqT")
    AT_h = dbl([128, 2, 128], BF16, "AT_h")
    dec = dbl([128, 2, 4], F32, "dec")
    on = dbl([128, 768], BF16, "on")
    oT_sb = dbl([128, 6, 128], BF16, "oT_sb")
    sm = dbl([128, 16], F32, "sm")
    dtmp = dbl([128, 2, 128], F32, "dtmp")
    wda = dbl([128, 128], F32, "wda")
    PT = [p.sb([128, 512], BF16, "PT%d" % i) for i in range(4)]
    pt_i = [0]
    r_oT = R("oT_dram")

    oidx = r_oidx = oT_v = None
    if "oT_sh" in io:
        oidx, r_oidx = p.sb([128, 6, NCH], mybir.dt.int32, "oidx")
        p.dma("sp", oidx[:], io["oidx"], [], [r_oidx], "c_oidx")
    else:
        oT_v = io["oT"].rearrange("(k p) t -> p k t", p=128)
    hn_full_v = hn_all_v = hn_meta_v = None
    if "hn_full" in io:
        hn_full_v = io["hn_full"].rearrange("(k p) t -> p k t", p=128)
    else:
        hn_all_v = io["hn_all"].rearrange("r (k p) t -> r p k t", p=128)
        hn_meta_v = io["hn_meta"].rearrange("(k p) t -> p k t", p=128)
    rope_v = io["rope"].rearrange("(c p) n -> p c n", p=128)

    def rope_ops(b, src_ps, r_src, t1, t2, dst, ngrp, half, cos_ap, sin_ap, nsin_ap, r_tab):
        (t1a, r_t1), (t2a, r_t2), (da_, r_d) = t1, t2, dst
        w = ngrp * 2 * half
        p.tt("dve", t1a[:, :w].rearrange("p (g h) -> p g h", h=half), src_ps.rearrange("p (g h) -> p g h", h=half),
             cos_ap.unsqueeze(1).to_broadcast([128, ngrp * 2, half]), ALU.mult, [r_src, r_tab], [r_t1])
        sv = src_ps.rearrange("p (g two h) -> p g two h", two=2, h=half)
        t2v = t2a[:, :w].rearrange("p (g two h) -> p g two h", two=2, h=half)
        p.tt("dve", t2v[:, :, 0, :], sv[:, :, 1, :], nsin_ap.unsqueeze(1).to_broadcast([128, ngrp, half]), ALU.mult,
             [r_src, r_tab], [r_t2])
        p.tt("dve", t2v[:, :, 1, :], sv[:, :, 0, :], sin_ap.unsqueeze(1).to_broadcast([128, ngrp, half]), ALU.mult,
             [r_src, r_tab], [r_t2])
        p.tt("pool", da_[:, :w], t1a[:, :w], t2a[:, :w], ALU.add, [r_t1, r_t2], [r_d])

    import os
    KSTOP = int(os.environ.get("KSTOP", "99"))

    def bail():
        if oT_v is not None:
            p.dma("sp", oT_v[:, 0:1, 0:128], ident[:].unsqueeze(1), [r_id], [r_oT], "st_o0")
        return r_oT
    if KSTOP == 0:
        return bail()
    ngroups = 1 + (nchunks - 1 + 3) // 4
    chunk_list = []
    for gi in range(ngroups):
        if gi == 0:
            clist = [0]
        else:
            c0 = 1 + (gi - 1) * 4
            clist = [c for c in range(c0, min(c0 + 4, nchunks))]
        for ji, c in enumerate(clist):
            chunk_list.append((gi, ji, c, clist))

    def group_load(gi, clist):
        hb, r_hb = hnb[gi % 2]
        rt, r_rt = rtb[gi % 2]
        if gi == 0:
            p.dma("sp", hb[:, :, 0:128], hn_full_v[:, :, 0:128] if hn_full_v is not None else hn_meta_v, [], [r_hb], "hn%d" % (gi % 2))
            p.dma("sp", rt[:, 0:1, :], rope_v[:, 0:1, :], [], [r_rt], "rt%d" % (gi % 2))
        else:
            c0 = clist[0]
            rk = (gi - 1) // 4
            lo = ((gi - 1) % 4) * 512
            nt = len(clist) * 128
            p.dma("sp", hb[:, :, 0:nt], hn_full_v[:, :, c0 * 128:c0 * 128 + nt] if hn_full_v is not None else hn_all_v[rk, :, :, lo:lo + nt],
                  [], [r_hb], "hn%d" % (gi % 2))
            p.dma("sp", rt[:, 0:len(clist), :], rope_v[:, c0:c0 + len(clist), :], [], [r_rt], "rt%d" % (gi % 2))

    def front_fns(gi, ji, c):
        b = c % 2
        hb, r_hb = hnb[gi % 2]
        rt, r_rt = rtb[gi % 2]
        hn_c = hb[:, :, ji * 128:(ji + 1) * 128]
        cosR, sinR, nsinR = rt[:, ji, 0:64], rt[:, ji, 64:128], rt[:, ji, 128:192]

        def proj(ti):
            g, r_g = gp()
            for k in range(8):
                p.mm(g[:, :], hn_c[:, k, :], W[:, k, ti * 512:(ti + 1) * 512], k == 0, k == 7, [r_hb, r_W], [r_g])
            return g, r_g

        def f1():
            p.memset("pool", sm[b][0][:], 0.0, [sm[b][1]])
            g0, r_g0 = proj(0)
            rope_ops(b, g0[:, 0:256], r_g0, qk_t1[b], qk_t2[b], qk_r[b], 2, 64, cosR, sinR, nsinR, r_rt)
            p.act(Vr[b][0][:], g0[:, 256:512], AF.Identity, [r_g0], [Vr[b][1]])
            p.ts("pool", kd[b][0][:], qk_r[b][0][:, 128:256], cols[:, 0:1], 0.0, ALU.mult, ALU.add, [qk_r[b][1], r_cols], [kd[b][1]])
            g1, r_g1 = proj(1)
            p.act(G[b][0][:], g1[:, :], AF.Silu, [r_g1], [G[b][1]])

        def f2():
            g2, r_g2 = proj(2)
            p.act(sig[b][0][:], g2[:, 256:512], AF.Sigmoid, [r_g2], [sig[b][1]])
            p.act(hq[b][0][:], g2[:, 0:256], AF.Identity, [r_g2], [hq[b][1]])
            p.tt("dve", sig[b][0][:], sig[b][0][:], oml[:], ALU.mult, [sig[b][1], r_oml], [sig[b][1]])
            p.tt("dve", sig[b][0][:], sig[b][0][:], lb[:], ALU.add, [sig[b][1], r_lb], [sig[b][1]])
            p.act(lf[b][0][:], sig[b][0][:], AF.Ln, [sig[b][1]], [lf[b][1]])
            p.ts("pool", kk[b][0][:], sig[b][0][:], -1.0, 1.0, ALU.mult, ALU.add, [sig[b][1]], [kk[b][1]])

        def f3():
            g3, r_g3 = proj(3)
            p.act(hv[b][0][:], g3[:, 0:256], AF.Identity, [r_g3], [hv[b][1]])
            for h in range(2):
                p.act(VA[h][0][:, c, 0:128], g3[:, 256 + h * 128:256 + (h + 1) * 128], AF.Identity, [r_g3], [VA_r[h][c]])
                p.cp("pool", VA[h][0][:, c, 128:129], vcol[:, c:c + 1], [r_vcol], [VA_r[h][c]])
            g4, r_g4 = proj(4)
            rope_ops(b, g4[:, 0:512], r_g4, dq_t1[b], dq_t2[b], dqk[b], 8, 32, rt[:, ji, 192:224], rt[:, ji, 224:256],
                     rt[:, ji, 256:288], r_rt)

        def f4():
            for i in range(2):
                p.tr(TRb[:, i * 128:(i + 1) * 128], qk_r[b][0][:, i * 128:(i + 1) * 128], ident[:], [qk_r[b][1], r_id], [r_TRb])
            for i in range(4):
                p.tr(TRb[:, (2 + i) * 128:(3 + i) * 128], dqk[b][0][:, i * 128:(i + 1) * 128], ident[:], [dqk[b][1], r_id], [r_TRb])
            p.cp("dve", rT[b][0][:, 0, :], TRb[:, 0:128], [r_TRb], [rT[b][1]])
            p.tt("dve", rT[b][0][:, 1, :], TRb[:, 0:128], tabs[:, 1, :], ALU.mult, [r_TRb, r_tabs], [rT[b][1]])
            p.cp("dve", rT[b][0][:, 2, :], TRb[:, 128:256], [r_TRb], [rT[b][1]])
            for h in range(2):
                p.act(dqT[b][0][0:64, h, 0, :], TRb[0:64, (2 + h) * 128:(3 + h) * 128], AF.Identity, [r_TRb], [dqT[b][1]])
                p.act(dqT[b][0][64:128, h, 1, :], TRb[64:128, (2 + h) * 128:(3 + h) * 128], AF.Identity, [r_TRb], [dqT[b][1]])
            for h in range(2):
                p.act(KT[h][0][:, c * 128:(c + 1) * 128], TRb[:, (4 + h) * 128:(5 + h) * 128], AF.Identity, [r_TRb], [KT_r[h][c]])
        return [f1, f2, f3, f4]

    group_load(0, chunk_list[0][3])
    for f_ in front_fns(*chunk_list[0][:3]):
        f_()
    for ci_, (gi, ji, c, clist) in enumerate(chunk_list):
            b = c % 2
            steps = []
            def ret_step(b=b, c=c):
                gs, r_gs = gp()
                p.mm(gs[:, 0:128], rT[b][0][:, 2, :], rT[b][0][:, 0, :], True, True, [rT[b][1]], [r_gs])
                p.tt("dve", AT_r[b][0][:], gs[:, 0:128], tabs[:, 0, :], ALU.mult, [r_gs, r_tabs], [AT_r[b][1]])
                p.mm(OArh[:, 0:256], AT_r[b][0][:], Vr[b][0][:], True, False, [AT_r[b][1], Vr[b][1]], [r_OArh])
                p.mm(OArh[:, 0:256], rT[b][0][:, 1, :], Sb_ret[:], False, True, [rT[b][1], r_Sbret], [r_OArh])
                gsu, r_gsu = gp()
                p.mm(gsu[:, 0:256], kd[b][0][:], Vr[b][0][:], True, True, [kd[b][1], Vr[b][1]], [r_gsu])
                p.stt("dve", S_ret[:], S_ret[:], cols[:, 1:2], gsu[:, 0:256], ALU.mult, ALU.add, [r_Sret, r_cols, r_gsu], [r_Sret])
                p.cp("pool", Sb_ret[:], S_ret[:], [r_Sret], [r_Sbret])
                p.act(dtmp[b][0][:].rearrange("p a n -> p (a n)"), OArh[:, 0:256], AF.Square, [r_OArh], [dtmp[b][1], sm[b][1]],
                      accum_out=sm[b][0][:, 0:1])
                p.act(sm[b][0][:, 0:1], sm[b][0][:, 0:1], AF.Sqrt, [sm[b][1]], [sm[b][1]], bias=EPS, scale=1.0 / 256)
                p.recip(sm[b][0][:, 0:1], sm[b][0][:, 0:1], [sm[b][1]], [sm[b][1]])
                p.stt("dve", on[b][0][:, 0:256], OArh[:, 0:256], sm[b][0][:, 0:1], G[b][0][:, 0:256], ALU.mult, ALU.mult,
                      [r_OArh, sm[b][1], G[b][1]], [on[b][1]])
            steps.append(ret_step)

            def hg_prep(b=b, c=c):
                gc, r_gc = gp()
                p.mm(gc[:, 0:256], tabs[:, 2, :], lf[b][0][:], True, True, [r_tabs, lf[b][1]], [r_gc])
                p.mm(gc[:, 256:512], tabs[:, 3, :], lf[b][0][:], True, True, [r_tabs, lf[b][1]], [r_gc])
                p.act(eq[b][0][:], gc[:, 0:256], AF.Exp, [r_gc], [eq[b][1]])
                p.tt("dve", qt[b][0][:], hq[b][0][:], eq[b][0][:], ALU.mult, [hq[b][1], eq[b][1]], [qt[b][1]])
                p.act(eq[b][0][:], gc[:, 0:256], AF.Exp, [r_gc, qt[b][1]], [eq[b][1]], scale=-1.0)
                p.tt("pool", kt[b][0][:], kk[b][0][:], eq[b][0][:], ALU.mult, [kk[b][1], eq[b][1]], [kt[b][1]])
                p.act(kbar[b][0][:], gc[:, 256:512], AF.Exp, [r_gc], [kbar[b][1]])
                p.tt("pool", kbar[b][0][:], kbar[b][0][:], kk[b][0][:], ALU.mult, [kbar[b][1], kk[b][1]], [kbar[b][1]])
                for h in range(2):
                    p.tt("pool", kbZ[b][0][:, h, :, :], kbar[b][0][:, h * 128:(h + 1) * 128].unsqueeze(1).to_broadcast([128, 4, 128]),
                         cols[:, 2:6].unsqueeze(2).to_broadcast([128, 4, 128]), ALU.mult, [kbar[b][1], r_cols], [kbZ[b][1]])
                gd, r_gd = gp()
                for h in range(2):
                    p.mm(gd[:, h * 4:(h + 1) * 4], lf[b][0][:, h * 128:(h + 1) * 128], cols[:, 2:6], True, True, [lf[b][1], r_cols], [r_gd])
                p.act(dec[b][0][:].rearrange("p h j -> p (h j)"), gd[:, 0:8], AF.Exp, [r_gd], [dec[b][1]])
                for h in range(2):
                    p.tr(TRb[:, h * 128:(h + 1) * 128], qt[b][0][:, h * 128:(h + 1) * 128], ident[:], [qt[b][1], r_id], [r_TRb])
                    p.tr(TRb[:, (2 + h) * 128:(3 + h) * 128], kt[b][0][:, h * 128:(h + 1) * 128], ident[:], [kt[b][1], r_id], [r_TRb])
                p.cp("dve", hT[b][0][:].rearrange("p h t -> p (h t)"), TRb[:, 256:512], [r_TRb], [hT[b][1]])
                p.cp("dve", eqT[b][0][:].rearrange("p h t -> p (h t)"), TRb[:, 0:256], [r_TRb], [eqT[b][1]])
                for h in range(2):
                    qzf = QZ[h][0][:].rearrange("p j t -> p (j t)")
                    p.cp("pool", rawap(qzf, [[160, 4], [1, 32]]), eqT[b][0][:, h, :].rearrange("p (j i) -> p j i", i=32),
                         [eqT[b][1]], [QZ[h][1]])
            steps.append(hg_prep)

            def hg_head(h, b=b, c=c):
                oc = slice(256 + h * 128, 256 + (h + 1) * 128)
                st = {}

                def fa():
                    gs, r_gs = gp()
                    p.mm(gs[:, 0:128], hT[b][0][:, h, :], eqT[b][0][:, h, :], True, True, [hT[b][1], eqT[b][1]], [r_gs])
                    p.tt("dve", AT_h[b][0][:, h, :], gs[:, 0:128], tabs[:, 4, :], ALU.mult, [r_gs, r_tabs], [AT_h[b][1]])
                    gu, r_gu = gp()
                    for j in range(4):
                        p.mm(gu[:, j * 128:(j + 1) * 128], kbZ[b][0][:, h, j, :], hv[b][0][:, h * 128:(h + 1) * 128], True, True,
                             [kbZ[b][1], hv[b][1]], [r_gu])
                    S_, r_S = S_hg[h]
                    for j in range(4):
                        p.stt("dve", S_[:], S_[:], dec[b][0][:, h, j:j + 1], gu[:, j * 128:(j + 1) * 128], ALU.mult, ALU.add,
                              [r_S, dec[b][1], r_gu], [r_S])
                        sbn, r_sbn = Sb_hg[h][b][j + 1] if j < 3 else Sb_hg[h][1 - b][0]
                        p.cp("pool", sbn[:], S_[:], [r_S], [r_sbn])

                def fb():
                    p.mm(OArh[:, oc], AT_h[b][0][:, h, :], hv[b][0][:, h * 128:(h + 1) * 128], True, False, [AT_h[b][1], hv[b][1]], [r_OArh])
                    for j in range(4):
                        sbj, r_sbj = Sb_hg[h][b][j]
                        p.mm(OArh[:, oc], QZ[h][0][:, j, :], sbj[:], False, j == 3, [QZ[h][1], r_sbj], [r_OArh])
                    k0 = 1 + h
                    p.act(dtmp[b][0][:, 0, :], OArh[:, oc], AF.Square, [r_OArh], [dtmp[b][1], sm[b][1]], accum_out=sm[b][0][:, k0:k0 + 1])
                    p.act(sm[b][0][:, k0:k0 + 1], sm[b][0][:, k0:k0 + 1], AF.Sqrt, [sm[b][1]], [sm[b][1]], bias=EPS, scale=1.0 / 128)
                    p.recip(sm[b][0][:, k0:k0 + 1], sm[b][0][:, k0:k0 + 1], [sm[b][1]], [sm[b][1]])
                    p.stt("dve", on[b][0][:, oc], OArh[:, oc], sm[b][0][:, k0:k0 + 1], G[b][0][:, oc], ALU.mult, ALU.mult,
                          [r_OArh, sm[b][1], G[b][1]], [on[b][1]])
                return fa, fb
            hh0, hh1 = hg_head(0), hg_head(1)
            steps += [hh0[0], hh1[0], hh0[1], hh1[1]]

            def da_head(h, b=b, c=c):
                units = [(j, m) for j in range(c + 1) for m in range(2)]
                batches = [units[i:i + 4] for i in range(0, len(units), 4)]
                out = []

                def mk_batch(bi, bat):
                    st = {}

                    def fa():
                        gs, r_gs = gp()
                        for ui, (j, m) in enumerate(bat):
                            p.mm(gs[:, ui * 128:(ui + 1) * 128], KT[h][0][:, j * 128:(j + 1) * 128],
                                 dqT[b][0][:, h, m, :], True, True, [KT_r[h][j], dqT[b][1]], [r_gs])
                        pt, r_pt = PT[pt_i[0] % 4]
                        pt_i[0] += 1
                        st["pt"] = (pt, r_pt)
                        n = len(bat) * 128
                        p.act(pt[:, 0:n], gs[:, 0:n], AF.Exp, [r_gs], [r_pt], scale=0.125)
                        for ui, (j, m) in enumerate(bat):
                            if j == c:
                                p.tt("pool", pt[:, ui * 128:(ui + 1) * 128], pt[:, ui * 128:(ui + 1) * 128], tabs[:, 5, :], ALU.mult,
                                     [r_pt, r_tabs], [r_pt])

                    def fb():
                        pt, r_pt = st["pt"]
                        for ui, (j, m) in enumerate(bat):
                            p.mm(OAm[m][0][:, 0:130], pt[:, ui * 128:(ui + 1) * 128], VA[h][0][:, j, 0:130],
                                 j == 0, j == c, [r_pt, VA_r[h][j]], [OAm[m][1]])
                    return fa, fb
                pairs = [mk_batch(bi, bat) for bi, bat in enumerate(batches)]
                for bi in range(min(3, len(pairs))):
                    out.append(pairs[bi][0])
                for bi in range(len(pairs)):
                    if bi + 3 < len(pairs):
                        out.append(pairs[bi + 3][0])
                    out.append(pairs[bi][1])

                def epi():
                    s0 = 4 + 4 * h
                    smt, r_sm = sm[b]
                    for m in range(2):
                        p.ts("dve", smt[:, s0 + m:s0 + m + 1], OAm[m][0][:, 128:129], 1e-30, None, ALU.max, ALU.bypass,
                             [OAm[m][1]], [r_sm])
                        p.recip(smt[:, s0 + m:s0 + m + 1], smt[:, s0 + m:s0 + m + 1], [r_sm], [r_sm])
                        p.act(dtmp[b][0][:, m, :], OAm[m][0][:, 0:128], AF.Identity, [OAm[m][1], r_sm], [dtmp[b][1]],
                              scale=smt[:, s0 + m:s0 + m + 1])
                    p.stt("dve", wda[b][0][:], dtmp[b][0][:, 1, :], nlam[:, 0:1], dtmp[b][0][:, 0, :], ALU.mult, ALU.add,
                          [dtmp[b][1], r_nlam], [wda[b][1]])
                    p.act(dtmp[b][0][:, 0, :], wda[b][0][:], AF.Square, [wda[b][1]], [dtmp[b][1], r_sm], accum_out=smt[:, s0 + 2:s0 + 3])
                    p.act(smt[:, s0 + 2:s0 + 3], smt[:, s0 + 2:s0 + 3], AF.Sqrt, [r_sm], [r_sm], bias=EPS, scale=1.0 / 128)
                    p.recip(smt[:, s0 + 2:s0 + 3], smt[:, s0 + 2:s0 + 3], [r_sm], [r_sm])
                    p.stt("dve", on[b][0][:, 512 + h * 128:512 + (h + 1) * 128], wda[b][0][:], smt[:, s0 + 2:s0 + 3], gsub[:],
                          ALU.mult, ALU.mult, [wda[b][1], r_sm, r_gsub], [on[b][1]])
                out.append(epi)
                return out

            def finish(b=b, c=c):
                def f():
                    for i in range(6):
                        p.tr(TRb[:, i * 128:(i + 1) * 128], on[b][0][:, i * 128:(i + 1) * 128], ident[:], [on[b][1], r_id], [r_TRb])
                    p.act(oT_sb[b][0][:].rearrange("p k t -> p (k t)"), TRb[:, 0:768], AF.Identity, [r_TRb], [oT_sb[b][1]])
                    if oidx is None:
                        p.dma("sp", oT_v[:, :, c * 128:(c + 1) * 128], oT_sb[b][0][:], [oT_sb[b][1]], [r_oT], "st_o%d" % b)
                    else:
                        for k6 in range(6):
                            p.op("pool", lambda e, k6=k6: e.indirect_dma_start(
                                out=io["oT_sh"],
                                out_offset=bass.IndirectOffsetOnAxis(ap=oidx[:, k6, c:c + 1], axis=0),
                                in_=oT_sb[b][0][:, k6, :], in_offset=None, bounds_check=4 * 768 * NCH - 1, oob_is_err=False),
                                [oT_sb[b][1], r_oidx], [r_oT], dma_key="st_o%d" % b)
                return f

            da_list = da_head(0) + da_head(1) + [finish()]
            fr = []
            if ci_ + 1 < len(chunk_list):
                ngi, nji, ncc, nclist = chunk_list[ci_ + 1]
                if nji == 0:
                    fr.append(lambda ngi=ngi, nclist=nclist: group_load(ngi, nclist))
                fr += front_fns(ngi, nji, ncc)
            allsteps = []
            for i_ in range(max(len(steps), len(fr))):
                if i_ < len(steps):
                    allsteps.append(steps[i_])
                if i_ < len(fr):
                    allsteps.append(fr[i_])
            ns, nd = len(allsteps), len(da_list)
            si = 0
            for di, dfn in enumerate(da_list[:-1]):
                while si < ns and si * (nd - 1) <= di * ns:
                    allsteps[si]()
                    si += 1
                dfn()
            while si < ns:
                allsteps[si]()
                si += 1
            da_list[-1]()
    return r_oT


def rms_tile(p, h_t, r_h, n, gcol, r_g, out_t, r_out, ones, r_ones, sq, r_sq, rstd, r_rstd, ps, r_ps):
    p.act(sq[:, :, :n], h_t[:, :, :n], AF.Square, [r_h], [r_sq])
    for k in range(8):
        p.mm(ps[:, :n], ones[:], sq[:, k, :n], k == 0, k == 7, [r_ones, r_sq], [r_ps])
    p.act(rstd[:, :n], ps[:, :n], AF.Sqrt, [r_ps], [r_rstd], bias=EPS, scale=1.0 / D)
    p.recip(rstd[:, :n], rstd[:, :n], [r_rstd], [r_rstd])
    for k in range(8):
        p.stt("dve" if k % 2 == 0 else "pool", out_t[:, k, :n], h_t[:, k, :n], gcol[:, k:k + 1], rstd[:, :n], ALU.mult, ALU.mult,
              [r_h, r_g, r_rstd], [r_out])


TILES = [(0, 128)] + [(128 + i * 342, 342) for i in range(6)]
NT = 342


def phase_A(p, io, tiles=None):
    tiles = tiles or TILES
    ones, r_ones = p.sb([128, 128], BF16, "onesA")
    p.memset("pool", ones[:], 1.0, [r_ones])
    gcol, r_g = p.sb([128, 8], F32, "gcolA")
    p.dma("sp", gcol[:], io["g"], [], [r_g], "c_g")
    xv = io["xT"].rearrange("(k p) t -> p k t", p=128)
    ov = io["hnT"].rearrange("(k p) t -> p k t", p=128)
    r_o = Res("hnT_d")
    bufs = []
    for i in range(2):
        bufs.append((p.sb([128, 8, NT], F32, "hA%d" % i), p.sb([128, 8, NT], BF16, "oA%d" % i), p.sb([128, 8, NT], BF16, "sqA%d" % i),
                     p.sb([128, NT], F32, "rsA%d" % i), p.ps([128, 512], F32, "psA%d" % i)))
    for ti, (t0, n) in enumerate(tiles):
        (h_t, r_h), (o_t, r_ot), (sq, r_sq), (rs, r_rs), (ps, r_ps) = bufs[ti % 2]
        p.dma("sp", h_t[:, :, :n], xv[:, :, t0:t0 + n], [], [r_h], "ldA%d" % (ti % 2))
        rms_tile(p, h_t, r_h, n, gcol, r_g, o_t, r_ot, ones, r_ones, sq, r_sq, rs, r_rs, ps, r_ps)
        p.dma("sp", ov[:, :, t0:t0 + n], o_t[:, :, :n], [r_ot], [r_o], "stA%d" % (ti % 2))
    return [r_o]


def phase_C(p, io, last, tiles=None, ybase=132, hist_reset=(0, 1)):
    nc = p.nc
    tiles = tiles or TILES
    WB, r_WB = p.sb([128, 67584], BF16, "WBUF")
    ones, r_ones = p.sb([128, 128], BF16, "onesC")
    p.memset("pool", ones[:], 1.0, [r_ones])
    gc2, r_gc2 = p.sb([128, 8], F32, "gffn")
    gc3, r_gc3 = p.sb([128, 8], F32, "gnext")
    p.dma("sp", gc2[:], io["g_ffn"], [], [r_gc2], "c_g2")
    p.dma("sp", gc3[:], io["g_next"], [], [r_gc3], "c_g3")
    cw, r_cw = p.sb([128, 44, 4], F32, "convw")
    p.dma("sp", cw[:], io["convp"], [], [r_cw], "c_cw")
    wg = WB[:, 0:24576].rearrange("p (k n) -> p k n", k=8)
    wb = WB[:, 24576:49152].rearrange("p (b k n) -> p b k n", b=3, k=8)
    wo = WB[:, 49152:57344].rearrange("p (k n) -> p k n", k=8)
    wgv = io["wg"].rearrange("(k p) n -> p k n", p=128)
    wbv = io["wb"].rearrange("b (k p) n -> p b k n", p=128)
    wov = io["wo"].rearrange("(k p) n -> p k n", p=128)
    for k in range(8):
        p.dma("pool", wg[:, k, :], wgv[:, k, :], [], [r_WB], "c_W")
    for b_ in range(3):
        for k in range(8):
            p.dma("pool", wb[:, b_, k, :], wbv[:, b_, k, :], [], [r_WB], "c_W")
    for k in range(8):
        p.dma("pool", wo[:, k, :], wov[:, k, :], [], [r_WB], "c_W")

    h_t, r_h = p.sb([128, 8, NT], F32, "hC")
    hn_t, r_hn = p.sb([128, 8, NT], BF16, "hnC")
    big, r_big = p.sb([128, 24 * NT], BF16, "bigC")
    y_t, r_y = p.sb([128, 8, NT], BF16, "yC")
    gt = [p.sb([128, NT], F32, "gt%d" % i) for i in range(2)]
    yacc, r_yacc = p.sb([128, NT], F32, "yacc")
    tmp = [p.sb([128, NT], F32, "tmpC%d" % i) for i in range(2)]
    sq, r_sq = p.sb([128, 8, NT], BF16, "sqC")
    rstd, r_rstd = p.sb([128, NT], F32, "rstdC")
    GPc = [p.ps([128, 512], F32, "GPc%d" % i) for i in range(7)]
    gi_ = [0]

    def gp():
        g = GPc[gi_[0] % 7]
        gi_[0] += 1
        return g
    hv_in = io["hT_in"].rearrange("(k p) t -> p k t", p=128)
    hnv_in = io["hnT_in"].rearrange("(k p) t -> p k t", p=128)
    brv = io["brT"].rearrange("g (k p) t -> p g k t", p=128)
    hmid_v = io["hmidT"].rearrange("(k p) t -> p k t", p=128)
    hn2_v = io["hn2T"].rearrange("(k p) t -> p k t", p=128)
    r_hmid, r_hn2d = Res("hmid_d"), Res("hn2_d")

    for ti, (t0, n) in enumerate(tiles):
        if int(os.environ.get("KC_STOP", "99")) == 0 and ti == 1:
            return [r_hmid, r_hn2d]
        br_t = big[:, 0:24 * n].rearrange("p (g k t) -> p g k t", g=4, k=6)
        p.dma("sp", h_t[:, :, :n], hv_in[:, :, t0:t0 + n], [], [r_h], "ldh")
        p.dma("sp", hn_t[:, :, :n], hnv_in[:, :, t0:t0 + n], [], [r_hn], "ldhn")
        if ti == 0:
            for g in range(4):
                p.dma("sp", br_t[:, g, :, :], brv[:, g, :, t0:t0 + n], [], [r_big], "ldbr")
        for m in range(8):
            ms = slice(m * 128, (m + 1) * 128)
            for nb in range(3):
                gps, r_gps = gp()
                for k in range(8):
                    p.mm(gps[:, :n], wg[:, k, nb * 1024 + m * 128:nb * 1024 + (m + 1) * 128], hn_t[:, k, :n], k == 0, k == 7,
                         [r_WB, r_hn], [r_gps])
                g_, r_g_ = gt[nb % 2]
                p.act(g_[:, :n], gps[:, :n], AF.Sigmoid, [r_gps], [r_g_])
                bps, r_bps = gp()
                for k in range(8):
                    p.mm(bps[:, :n], wb[:, nb, k, ms], br_t[:, k // 2, nb * 2 + k % 2, :], k == 0, k == 7, [r_WB, r_big], [r_bps])
                if nb == 0:
                    p.tt("dve", yacc[:, :n], g_[:, :n], bps[:, :n], ALU.mult, [r_g_, r_bps], [r_yacc])
                else:
                    t_, r_t = tmp[nb % 2]
                    p.tt("dve", t_[:, :n], g_[:, :n], bps[:, :n], ALU.mult, [r_g_, r_bps], [r_t])
                    if nb == 1:
                        p.tt("gps", yacc[:, :n], yacc[:, :n], t_[:, :n], ALU.add, [r_yacc, r_t], [r_yacc])
                    else:
                        p.tt("gps", y_t[:, m, :n], yacc[:, :n], t_[:, :n], ALU.add, [r_yacc, r_t], [r_y])
        if ti + 1 < len(tiles):
            t0n, nn = tiles[ti + 1]
            br_n = big[:, 0:24 * nn].rearrange("p (g k t) -> p g k t", g=4, k=6)
            for g in range(4):
                p.dma("sp", br_n[:, g, :, :], brv[:, g, :, t0n:t0n + nn], [], [r_big], "ldbr")
        for m in range(8):
            ops_, r_ops = gp()
            for k in range(8):
                p.mm(ops_[:, :n], wo[:, k, m * 128:(m + 1) * 128], y_t[:, k, :n], k == 0, k == 7, [r_WB, r_y], [r_ops])
            p.tt("dve", h_t[:, m, :n], h_t[:, m, :n], ops_[:, :n], ALU.add, [r_h, r_ops], [r_h])
        if ti == 0:
            p.memset("pool", h_t[:, :, 0:112], 0.0, [r_h])
        p.dma("sp", hmid_v[:, :, t0:t0 + n], h_t[:, :, :n], [r_h], [r_hmid], "sth")
        ps, r_ps = gp()
        rms_tile(p, h_t, r_h, n, gc2, r_gc2, hn_t, r_hn, ones, r_ones, sq, r_sq, rstd, r_rstd, ps, r_ps)
        p.dma("sp", hn2_v[:, :, t0:t0 + n], hn_t[:, :, :n], [r_hn], [r_hn2d], "sthn")

    KC = int(os.environ.get("KC_STOP", "99"))
    if KC == 1:
        return [r_hmid, r_hn2d]
    wfi = WB[:, 0:45056].rearrange("p (k n) -> p k n", k=8)
    wfo = WB[:, 45056:67584].rearrange("p (k n) -> p k n", k=22)
    wfiv = io["wfi"].rearrange("(k p) n -> p k n", p=128)
    wfov = io["wfo"].rearrange("(k p) n -> p k n", p=128)
    for k in range(8):
        p.dma("pool", wfi[:, k, :], wfiv[:, k, :], [], [r_WB], "c_W")
    for k in range(22):
        p.dma("pool", wfo[:, k, :], wfov[:, k, :], [], [r_WB], "c_W")
    U = [p.sb([128, NT + 2], F32, "U%d" % i) for i in range(2)]
    cb = [p.sb([128, NT], F32, "cb%d" % i) for i in range(2)]
    Hh, r_Hh = p.sb([128, 44, 2], F32, "Hh")
    sg, r_sg = p.sb([128, NT], F32, "sgC")
    r_hout, r_out2 = Res("hout_d"), Res("out2_d")
    hout_v = io["hT_out"].rearrange("(k p) t -> p k t", p=128)
    if last:
        yv = io["yT"].rearrange("(k p) t -> p k t", p=128)
        yo, r_yo = p.sb([128, 8, NT], F32, "yo")
    else:
        hnout_v = io["hnT_out"].rearrange("(k p) t -> p k t", p=128)
    for ti, (t0, n) in enumerate(tiles):
        a_t = big[:, 0:22 * n].rearrange("p (k t) -> p k t", k=22)
        if ti in hist_reset:
            p.memset("pool", Hh[:], 0.0, [r_Hh])
        p.dma("sp", h_t[:, :, :n], hmid_v[:, :, t0:t0 + n], [r_hmid], [r_h], "ldh")
        p.dma("sp", hn_t[:, :, :n], hn2_v[:, :, t0:t0 + n], [r_hn2d], [r_hn], "ldhn")
        for i in range(22):
            for wi, ci in enumerate((i, 22 + i)):
                ups, r_ups = gp()
                for k in range(8):
                    p.mm(ups[:, :n], wfi[:, k, ci * 128:(ci + 1) * 128], hn_t[:, k, :n], k == 0, k == 7, [r_WB, r_hn], [r_ups])
                u_, r_u = U[wi]
                c_, r_c = cb[wi]
                p.act(u_[:, 2:2 + n], ups[:, :n], AF.Identity, [r_ups], [r_u])
                p.act(c_[:, :n], ups[:, :n], AF.Identity, [r_ups, r_cw], [r_c], scale=cw[:, ci, 2:3], bias=cw[:, ci, 3:4])
                p.cp("gps", u_[:, 0:2], Hh[:, ci, :], [r_Hh], [r_u])
                p.stt("dve", c_[:, :n], u_[:, 1:1 + n], cw[:, ci, 1:2], c_[:, :n], ALU.mult, ALU.add, [r_u, r_cw, r_c], [r_c])
                p.stt("dve", c_[:, :n], u_[:, 0:n], cw[:, ci, 0:1], c_[:, :n], ALU.mult, ALU.add, [r_u, r_cw, r_c], [r_c])
                p.cp("gps", Hh[:, ci, :], u_[:, n:n + 2], [r_u], [r_Hh])
            p.act(sg[:, :n], cb[0][0][:, :n], AF.Silu, [cb[0][1]], [r_sg])
            p.tt("gps", a_t[:, i, :], sg[:, :n], cb[1][0][:, :n], ALU.mult, [r_sg, cb[1][1]], [r_big])
        for m in range(8):
            fps, r_fps = gp()
            for k in range(22):
                p.mm(fps[:, :n], wfo[:, k, m * 128:(m + 1) * 128], a_t[:, k, :], k == 0, k == 21, [r_WB, r_big], [r_fps])
            p.tt("dve", h_t[:, m, :n], h_t[:, m, :n], fps[:, :n], ALU.add, [r_h, r_fps], [r_h])
        if ti == 0:
            p.memset("pool", h_t[:, :, 0:112], 0.0, [r_h])
        p.dma("sp", hout_v[:, :, t0:t0 + n], h_t[:, :, :n], [r_h], [r_hout], "sth")
        ps, r_ps = gp()
        if last:
            lo = max(0, ybase - t0)
            if lo < n:
                rms_tile(p, h_t, r_h, n, gc3, r_gc3, yo, r_yo, ones, r_ones, sq, r_sq, rstd, r_rstd, ps, r_ps)
                g0 = t0 + lo - ybase
                p.dma("sp", yv[:, :, g0:g0 + n - lo], yo[:, :, lo:n], [r_yo], [r_out2], "sty")
        else:
            rms_tile(p, h_t, r_h, n, gc3, r_gc3, hn_t, r_hn, ones, r_ones, sq, r_sq, rstd, r_rstd, ps, r_ps)
            p.dma("sp", hnout_v[:, :, t0:t0 + n], hn_t[:, :, :n], [r_hn], [r_out2], "sthn")
    return [r_hout, r_out2, r_hmid, r_hn2d]


def _dram(nc, name, shape, dt, kind):
    return nc.dram_tensor(name, list(shape), dt, kind=kind).ap()


def build_A():
    nc = bass.Bass("TRN2", target_bir_lowering=False)
    io = {"xT": _dram(nc, "xT", [D, TLOC], F32, "ExternalInput"), "g": _dram(nc, "g", [128, 8], F32, "ExternalInput"),
          "hnT": _dram(nc, "hnT", [D, TLOC], BF16, "ExternalOutput")}
    p = Prog(nc)
    outs = phase_A(p, io)
    p.wait_only("sp", [r.lw for r in outs])
    p.emit()
    return nc


def build_B(li, nchunks=NCH):
    nc = bass.Bass("TRN2", target_bir_lowering=False)
    I = "ExternalInput"
    io = {"hn_meta": _dram(nc, "hn_meta", [D, 128], BF16, I), "hn_all": _dram(nc, "hn_all", [4, D, 2048], BF16, I),
          "w": _dram(nc, "w", [D, NBW], F32, I), "rope": _dram(nc, "rope", [LTOT, 288], F32, I),
          "tabs": _dram(nc, "tabs", [8, 128, 128], F32, I), "cols": _dram(nc, "cols", [128, 8], F32, I),
          "vcol": _dram(nc, "vcol", [128, NCH], F32, I), "hg_lb": _dram(nc, "hg_lb", [DEPTH, 256], F32, I),
          "da_lambda": _dram(nc, "da_lambda", [1, 256], F32, I), "subln": _dram(nc, "subln", [1, 128], F32, I),
          "ident": _dram(nc, "ident", [128, 128], F32, I),
          "oT": _dram(nc, "oT", [768, LTOT], BF16, "ExternalOutput")}
    p = Prog(nc)
    r_o = phase_B(p, li, io, nchunks)
    p.wait_only("sp", [r_o.lw])
    p.emit()
    return nc


def build_C(last):
    nc = bass.Bass("TRN2", target_bir_lowering=False)
    I = "ExternalInput"
    O = "ExternalOutput"
    io = {"hT_in": _dram(nc, "hT_in", [D, TLOC], F32, I), "hnT_in": _dram(nc, "hnT_in", [D, TLOC], BF16, I),
          "brT": _dram(nc, "brT", [4, 768, TLOC], BF16, I), "wg": _dram(nc, "wg", [D, 3072], F32, I),
          "wb": _dram(nc, "wb", [3, D, D], F32, I), "wo": _dram(nc, "wo", [D, D], F32, I),
          "g_ffn": _dram(nc, "g_ffn", [128, 8], F32, I), "g_next": _dram(nc, "g_next", [128, 8], F32, I),
          "convp": _dram(nc, "convp", [128, 44, 4], F32, I), "wfi": _dram(nc, "wfi", [D, 2 * DFF], F32, I),
          "wfo": _dram(nc, "wfo", [DFF, D], F32, I),
          "hmidT": _dram(nc, "hmidT", [D, TLOC], F32, "Internal"), "hn2T": _dram(nc, "hn2T", [D, TLOC], BF16, "Internal"),
          "hT_out": _dram(nc, "hT_out", [D, TLOC], F32, O)}
    if last:
        io["yT"] = _dram(nc, "yT", [D, NLOC * 128], F32, O)
    else:
        io["hnT_out"] = _dram(nc, "hnT_out", [D, TLOC], BF16, O)
    p = Prog(nc)
    outs = phase_C(p, io, last)
    p.wait_only("sp", [r.lw for r in outs])
    p.emit()
    return nc


def const_tables():
    f32 = np.float32
    pos = (np.arange(LTOT) - 112).astype(f32)
    rope = np.zeros((LTOT, 288), f32)
    inv = (10000.0 ** (-np.arange(0, 128, 2, dtype=f32) / 128)).astype(f32)
    ang = pos[:, None] * inv[None, :]
    rope[:, 0:64] = np.cos(ang)
    rope[:, 64:128] = np.sin(ang)
    rope[:, 128:192] = -np.sin(ang)
    inv = (10000.0 ** (-np.arange(0, 64, 2, dtype=f32) / 64)).astype(f32)
    ang = pos[:, None] * inv[None, :]
    rope[:, 192:224] = np.cos(ang)
    rope[:, 224:256] = np.sin(ang)
    rope[:, 256:288] = -np.sin(ang)
    idx = np.arange(128)
    tabs_h, cols_h = [], []
    same = (idx[:, None] // 32) == (idx[None, :] // 32)
    for hd in range(4):
        log_g = np.log1p(-np.exp2(-5.0 - hd))
        tabs = np.zeros((8, 128, 128), f32)
        gap = idx[None, :] - idx[:, None]
        tabs[0] = np.where(gap >= 0, np.exp(log_g * np.maximum(gap, 0)), 0.0) * 128 ** -0.5
        tabs[1] = np.exp(log_g * (idx[None, :] + 1.0)) * np.ones((128, 1))
        tabs[2] = (same & (idx[:, None] <= idx[None, :])).astype(f32)
        tabs[3] = (same & (idx[:, None] > idx[None, :])).astype(f32)
        tabs[4] = (same & (idx[None, :] >= idx[:, None])).astype(f32)
        tabs[5] = (idx[None, :] >= idx[:, None]).astype(f32)
        cols = np.zeros((128, 8), f32)
        cols[:, 0] = np.exp(log_g * (127.0 - idx)) * 128 ** -0.5
        cols[:, 1] = np.exp(log_g * 128.0)
        for j in range(4):
            cols[:, 2 + j] = (idx // 32 == j)
        tabs_h.append(tabs)
        cols_h.append(cols)
    vcol = np.ones((128, NCH), f32)
    vcol[:112, 0] = 0.0
    return rope, tabs_h, cols_h, vcol


def gcols(g):
    return np.ascontiguousarray(np.asarray(g, np.float32).reshape(8, 128).T)


def w_group(w_in_l, g):
    s = lambda off, width: w_in_l[:, off + g * width: off + (g + 1) * width]
    parts = [s(0, 128), s(512, 128), s(1024, 256), s(2048, 256), s(6144, 256), s(3072, 256), s(4096, 256), s(5120, 256),
             s(9216, 256), s(7168, 256), s(8192, 256)]
    return np.ascontiguousarray(np.concatenate(parts, axis=1))


_NC_CACHE = {}
_DBG = None


def _get(name, fn):
    if name not in _NC_CACHE:
        _NC_CACHE[name] = fn()
    return _NC_CACHE[name]


def local_tokens(q):
    base = 128 + q * 2048
    return np.concatenate([np.arange(128), np.arange(base - HALO, base), np.arange(base, base + 2048)])


def kernel(x, meta, norm_mix_g, w_in, w_branch, w_out, hg_lb, da_lambda, da_subln_g, norm_ffn_g, w_ffn_in,
           ffn_conv_w, ffn_conv_b, w_ffn_out, norm_final_g):
    f32 = np.float32
    bf = ml_dtypes.bfloat16
    A = lambda a: np.asarray(a, f32)
    x, meta, w_in, w_branch, w_out = A(x), A(meta), A(w_in), A(w_branch), A(w_out)
    w_ffn_in, w_ffn_out, ffn_conv_w, ffn_conv_b = A(w_ffn_in), A(w_ffn_out), A(ffn_conv_w), A(ffn_conv_b)
    hg_lb, da_lambda, da_subln_g = A(hg_lb), A(da_lambda), A(da_subln_g)
    rope, tabs_h, cols_h, vcol = const_tables()
    ident = np.eye(128, dtype=f32)
    cores = list(range(8))
    hfull = np.zeros((2, LTOT, D), f32)
    hfull[:, 112:128] = meta[None]
    hfull[:, 128:] = x
    loc = [local_tokens(q) for q in range(4)]
    in_maps = [{"xT": np.ascontiguousarray(hfull[c // 4][loc[c % 4]].T), "g": gcols(norm_mix_g[0])} for c in cores]
    hT = [m["xT"] for m in in_maps]
    res = run_bass_kernel_spmd(_get("A", build_A), in_maps, core_ids=cores)
    hnT = [np.asarray(r["hnT"]) for r in res.results]
    if _DBG is not None:
        _DBG["hnT_A"] = hnT
    out = np.zeros((2, SEQ, D), f32)
    for li in range(DEPTH):
        last = li == DEPTH - 1
        in_maps = []
        for c in cores:
            b, g = c // 4, c % 4
            in_maps.append({
                "hn_meta": np.ascontiguousarray(hnT[b * 4][:, 0:128]),
                "hn_all": np.ascontiguousarray(np.stack([hnT[b * 4 + q][:, 132:] for q in range(4)])),
                "w": w_group(w_in[li], g), "rope": rope, "tabs": tabs_h[g], "cols": cols_h[g], "vcol": vcol,
                "hg_lb": np.ascontiguousarray(hg_lb[:, g * 256:(g + 1) * 256]),
                "da_lambda": np.ascontiguousarray(da_lambda[li].reshape(1, 256)),
                "subln": np.ascontiguousarray(da_subln_g[li].reshape(1, 128)), "ident": ident})
        res = run_bass_kernel_spmd(_get("B%d" % li, lambda: build_B(li)), in_maps, core_ids=cores)
        oT = [np.asarray(r["oT"]) for r in res.results]
        if _DBG is not None:
            _DBG["oT%d" % li] = oT
        convp = np.concatenate([ffn_conv_w[li].T, ffn_conv_b[li][:, None]], axis=1)
        convp = np.ascontiguousarray(convp.reshape(44, 128, 4).transpose(1, 0, 2))
        g_next = norm_final_g if last else norm_mix_g[li + 1]
        in_maps = []
        for c in cores:
            b, q = c // 4, c % 4
            in_maps.append({
                "hT_in": hT[c], "hnT_in": hnT[c],
                "brT": np.ascontiguousarray(np.stack([oT[b * 4 + g][:, loc[q]] for g in range(4)])),
                "wg": np.ascontiguousarray(w_in[li][:, 10240:13312]), "wb": w_branch[li], "wo": w_out[li],
                "g_ffn": gcols(norm_ffn_g[li]), "g_next": gcols(g_next), "convp": convp,
                "wfi": w_ffn_in[li], "wfo": w_ffn_out[li]})
        res = run_bass_kernel_spmd(_get("C%d" % int(last), lambda: build_C(last)), in_maps, core_ids=cores)
        hT = [np.asarray(r["hT_out"]) for r in res.results]
        if _DBG is not None:
            _DBG["hT%d" % li] = hT
        if last:
            for c in cores:
                out[c // 4, (c % 4) * 2048:(c % 4 + 1) * 2048] = np.asarray(res.results[c]["yT"]).T
        else:
            hnT = [np.asarray(r["hnT_out"]) for r in res.results]
    return out


def phase_X(p, priv, sh2, oidx):
    bufs = [p.sb([128, LTOT], BF16, "xb%d" % i) for i in range(2)]
    r_sh = Res("sh")
    n = 0
    for j in range(NGL):
        ix, r_ix = p.sb([128, 6], mybir.dt.int32, "xi%d" % j)
        p.dma("sp", ix[:], oidx[j], [], [r_ix], "ldxi%d" % j)
        pv = priv[j].rearrange("(k p) t -> p k t", p=128)
        for k6 in range(6):
            buf, r_buf = bufs[n % 2]
            p.dma("sp", buf[:], pv[:, k6, :], [], [r_buf], "ldx%d" % (n % 2))
            p.op("pool", lambda e, buf=buf, ix=ix, k6=k6: e.indirect_dma_start(
                out=sh2, out_offset=bass.IndirectOffsetOnAxis(ap=ix[:, k6:k6 + 1], axis=0), in_=buf[:], in_offset=None,
                bounds_check=4 * 768 - 1, oob_is_err=False), [r_buf, r_ix], [r_sh], dma_key="scx%d" % (n % 2))
            n += 1


NTF = 320
TILES_F = [(i * NTF, NTF) for i in range(LTOT // NTF)]


PAIR = os.environ.get("K_PAIR", "1") == "1"
NGL = 2 if PAIR else 4


def build_fused():
    nc = bass.Bass("TRN2", target_bir_lowering=False, num_devices=4) if PAIR else bass.Bass("TRN2", target_bir_lowering=False)
    I = "ExternalInput"
    ext = {}

    def inp(name, shape, dt=F32):
        ext[name] = _dram(nc, name, shape, dt, I)
        return ext[name]
    xT = inp("xT", [D, LTOT])
    rope = inp("rope", [LTOT, 288])
    vcol = inp("vcol", [128, NCH])
    ident = inp("ident", [128, 128])
    tabs = [inp("tabs%d" % g, [8, 128, 128]) for g in range(NGL)]
    cols = [inp("cols%d" % g, [128, 8]) for g in range(NGL)]
    hglb = [inp("hg_lb%d" % g, [DEPTH, 256]) for g in range(NGL)]
    oidx = [inp("oidx%d" % g, [128, 6], mybir.dt.int32) for g in range(NGL)] if PAIR else None
    gmix = [inp("g_mix%d" % l, [128, 8]) for l in range(DEPTH)]
    gfin = inp("g_fin", [128, 8])
    L = []
    for l in range(DEPTH):
        L.append({"w": [inp("w%d_%d" % (l, g), [D, NBW]) for g in range(NGL)],
                  "da_lambda": inp("da_lambda%d" % l, [1, 256]), "subln": inp("subln%d" % l, [1, 128]),
                  "wg": inp("wg%d" % l, [D, 3072]), "wb": inp("wb%d" % l, [3, D, D]), "wo": inp("wo%d" % l, [D, D]),
                  "g_ffn": inp("g_ffn%d" % l, [128, 8]), "convp": inp("convp%d" % l, [128, 44, 4]),
                  "wfi": inp("wfi%d" % l, [D, 2 * DFF]), "wfo": inp("wfo%d" % l, [DFF, D])})
    yT = _dram(nc, "yT", [D, SEQ], F32, "ExternalOutput")
    hnT = _dram(nc, "hnT_i", [D, LTOT], BF16, "Internal")
    if PAIR:
        brT = nc.dram_tensor("brT_sh", [4, 768, LTOT], BF16, addr_space="Shared").ap()
        brT2 = brT.rearrange("g f t -> (g f) t")
        brP = _dram(nc, "brP_i", [NGL, 768, LTOT], BF16, "Internal")
    else:
        brT = _dram(nc, "brT_i", [4, 768, LTOT], BF16, "Internal")
    hT = _dram(nc, "hT_i", [D, LTOT], F32, "Internal")
    hmidT = _dram(nc, "hmidT_i", [D, LTOT], F32, "Internal")
    hn2T = _dram(nc, "hn2T_i", [D, LTOT], BF16, "Internal")
    counts = []

    fstop = int(os.environ.get("K_FSTOP", "99"))

    def close(p):
        if len(counts) >= fstop:
            p.stack.close()
            counts.append(None)
            return
        p.finish()
        counts.append({e: len(v) for e, v in p.ops.items()})
        p.emit()
        nc.all_engine_barrier()
    p = Prog(nc)
    phase_A(p, {"xT": xT, "g": gmix[0], "hnT": hnT}, TILES_F)
    close(p)
    for l in range(DEPTH):
        last = l == DEPTH - 1
        for g in range(NGL):
            p = Prog(nc)
            iob = {"hn_full": hnT, "w": L[l]["w"][g], "rope": rope, "tabs": tabs[g], "cols": cols[g], "vcol": vcol,
                   "hg_lb": hglb[g], "da_lambda": L[l]["da_lambda"], "subln": L[l]["subln"], "ident": ident}
            iob["oT"] = brP[g] if PAIR else brT[g]
            phase_B(p, l, iob)
            close(p)
        if PAIR:
            p = Prog(nc)
            phase_X(p, brP, brT2, oidx)
            close(p)
            nc.all_core_barrier()
        p = Prog(nc)
        io = {"hT_in": xT if l == 0 else hT, "hnT_in": hnT, "brT": brT, "wg": L[l]["wg"], "wb": L[l]["wb"], "wo": L[l]["wo"],
              "g_ffn": L[l]["g_ffn"], "g_next": gfin if last else gmix[l + 1], "convp": L[l]["convp"], "wfi": L[l]["wfi"],
              "wfo": L[l]["wfo"], "hmidT": hmidT, "hn2T": hn2T, "hT_out": hT}
        if last:
            io["yT"] = yT
        else:
            io["hnT_out"] = hnT
        phase_C(p, io, last, TILES_F, ybase=128, hist_reset=(0,))
        close(p)
        if PAIR and not last:
            nc.all_core_barrier()
    print("fused program op counts per phase:", counts)
    return nc


def kernel_fused(x, meta, norm_mix_g, w_in, w_branch, w_out, hg_lb, da_lambda, da_subln_g, norm_ffn_g, w_ffn_in,
                 ffn_conv_w, ffn_conv_b, w_ffn_out, norm_final_g):
    f32 = np.float32
    A = lambda a: np.asarray(a, f32)
    x, meta, w_in, w_branch, w_out = A(x), A(meta), A(w_in), A(w_branch), A(w_out)
    w_ffn_in, w_ffn_out, ffn_conv_w, ffn_conv_b = A(w_ffn_in), A(w_ffn_out), A(ffn_conv_w), A(ffn_conv_b)
    hg_lb, da_lambda, da_subln_g = A(hg_lb), A(da_lambda), A(da_subln_g)
    rope, tabs_h, cols_h, vcol = const_tables()
    shared = {"rope": rope, "vcol": vcol, "ident": np.eye(128, dtype=f32), "g_fin": gcols(norm_final_g)}
    percore = [dict() for _ in range(2)]
    for e in range(2 if PAIR else 1):
        for j in range(NGL):
            g = NGL * e + j
            percore[e]["tabs%d" % j] = tabs_h[g]
            percore[e]["cols%d" % j] = cols_h[g]
            percore[e]["hg_lb%d" % j] = np.ascontiguousarray(hg_lb[:, g * 256:(g + 1) * 256])
            if PAIR:
                percore[e]["oidx%d" % j] = np.ascontiguousarray(
                    (g * 768 + np.arange(6)[None, :] * 128 + np.arange(128)[:, None]).astype(np.int32))
            for l in range(DEPTH):
                percore[e]["w%d_%d" % (l, j)] = w_group(w_in[l], g)
    for l in range(DEPTH):
        shared["g_mix%d" % l] = gcols(norm_mix_g[l])
        shared["da_lambda%d" % l] = np.ascontiguousarray(da_lambda[l].reshape(1, 256))
        shared["subln%d" % l] = np.ascontiguousarray(da_subln_g[l].reshape(1, 128))
        shared["wg%d" % l] = np.ascontiguousarray(w_in[l][:, 10240:13312])
        shared["wb%d" % l] = w_branch[l]
        shared["wo%d" % l] = w_out[l]
        shared["g_ffn%d" % l] = gcols(norm_ffn_g[l])
        convp = np.concatenate([ffn_conv_w[l].T, ffn_conv_b[l][:, None]], axis=1)
        shared["convp%d" % l] = np.ascontiguousarray(convp.reshape(44, 128, 4).transpose(1, 0, 2))
        shared["wfi%d" % l] = w_ffn_in[l]
        shared["wfo%d" % l] = w_ffn_out[l]
    in_maps = []
    npc = 2 if PAIR else 1
    for b in range(2):
        hfull = np.zeros((LTOT, D), f32)
        hfull[112:128] = meta
        hfull[128:] = x[b]
        xT = np.ascontiguousarray(hfull.T)
        for e in range(npc):
            m = dict(shared)
            m.update(percore[e])
            m["xT"] = xT
            in_maps.append(m)
    res = run_bass_kernel_spmd(_get("F", build_fused), in_maps, core_ids=list(range(2 * npc)))
    return np.stack([np.ascontiguousarray(np.asarray(res.results[b * npc]["yT"]).T) for b in range(2)])


kernel_unfused = kernel
if os.environ.get("K_FUSED", "1") == "1":
    kernel = kernel_fused
```
